# Optimizing a Trainium2 kernel written in Bass

```python
import math
import jax, jax.numpy as jnp
from jax import lax
import numpy as np


D_MODEL = 1024
BATCH = 4
SEQ = 4096
DEPTH = 2
DEC_BATCH = 16
DEC_SEQ = 64
PAST_LEN = 2048

CHUNK = 64
PLE_DIM = 256
SSM_WIDTH = 256
SSM_GROUP = 16
SSM_GROUPS = SSM_WIDTH // SSM_GROUP
SSM_STATE = 64
ATT_HEADS = 8
ATT_HEAD_DIM = 64
ATT_WIDTH = ATT_HEADS * ATT_HEAD_DIM
ATT_PAST_CHUNKS = 8
REL_CLIP = 256
RET_HEADS = 4
RET_KEY_DIM = 64
RET_VAL_DIM = 64
RET_QK = RET_HEADS * RET_KEY_DIM
RET_V = RET_HEADS * RET_VAL_DIM
ROPE_BASE = 10000.0
PEER_HEADS = 8
PEER_KEY_DIM = 256
N_KEYS = 128
N_EXPERTS = N_KEYS * N_KEYS
PEER_TOPK = 16
PEER_BLOCK = 128
DN_ALPHA = (2 * DEPTH) ** 0.25
DN_BETA = (8 * DEPTH) ** -0.25
LN_EPS = 1e-5
IN_SPLITS = (SSM_WIDTH, ATT_WIDTH, ATT_WIDTH, ATT_WIDTH, RET_QK, RET_QK, RET_V, RET_V, D_MODEL, D_MODEL, D_MODEL)
IN_COLS = sum(IN_SPLITS)
IN_OFFSETS = tuple(np.cumsum(IN_SPLITS)[:-1].tolist())

kernel_name = 'hybrid_s5_bandattn_retention_peer_stream_step'

F32 = jnp.float32


def layer_norm(x, g, b):
    xf = x.astype(F32)
    mu = xf.mean(-1, keepdims=True)
    var = jnp.square(xf - mu).mean(-1, keepdims=True)
    return ((xf - mu) * lax.rsqrt(var + LN_EPS) * g.astype(F32) + b.astype(F32)).astype(x.dtype)


def s5_mixer(u, h0_re, h0_im, lam_re, lam_im, b_re, b_im, c_re, c_im, log_dt, d_skip, w_glu, b_glu):
    bsz, t, _ = u.shape
    uf = u.astype(F32)
    ug = uf.reshape(bsz, t, SSM_GROUPS, SSM_GROUP)
    lam = lax.complex(lam_re.astype(F32), lam_im.astype(F32))
    dt = jnp.exp(log_dt.astype(F32))[:, None]
    lam_dt = lam * dt
    a_bar = jnp.exp(lam_dt)
    b = lax.complex(b_re.astype(F32), b_im.astype(F32))
    b_bar = ((a_bar - 1.0) / lam)[..., None] * b
    bu = jnp.einsum('gnp,btgp->btgn', b_bar, ug.astype(jnp.complex64))
    a_full = jnp.broadcast_to(a_bar, bu.shape)

    def combine(left, right):
        return (left[0] * right[0], right[0] * left[1] + right[1])

    _, h = lax.associative_scan(combine, (a_full, bu), axis=1)
    steps = jnp.arange(1, t + 1, dtype=F32)
    carry = jnp.exp(lam_dt[None] * steps[:, None, None])
    h0 = lax.complex(h0_re.astype(F32), h0_im.astype(F32))
    h = h + carry[None] * h0[:, None]
    c = lax.complex(c_re.astype(F32), c_im.astype(F32))
    y = jnp.real(jnp.einsum('gpn,btgn->btgp', c, h)).reshape(bsz, t, SSM_WIDTH) + d_skip.astype(F32) * uf
    z = jax.nn.gelu(y)
    out = z * jax.nn.sigmoid(z @ w_glu.astype(F32) + b_glu.astype(F32))
    h_last = h[:, -1]
    return out.astype(u.dtype), jnp.real(h_last), jnp.imag(h_last)


def rel_bias_lookup(dist, table):
    idx = jnp.clip(dist, -REL_CLIP, REL_CLIP) + REL_CLIP
    return jnp.moveaxis(table[idx], -1, 0).astype(F32)


def band_attention_prompt(q, k, v, table, n_rows):
    bsz, t, h, hd = q.shape
    n_chunks = t // CHUNK
    pad = ATT_PAST_CHUNKS * CHUNK
    band = pad + CHUNK
    kp = jnp.pad(k, ((0, 0), (pad, 0), (0, 0), (0, 0)))
    vp = jnp.pad(v, ((0, 0), (pad, 0), (0, 0), (0, 0)))
    i = jnp.arange(CHUNK)
    j = jnp.arange(band)
    bias = rel_bias_lookup(i[:, None] - j[None, :] + pad, table)
    qc = jnp.moveaxis(q.reshape(bsz, n_chunks, CHUNK, h, hd), 1, 0)
    scale = hd ** -0.5

    def one_chunk(args):
        c, qb = args
        start = c * CHUNK
        kb = lax.dynamic_slice_in_dim(kp, start, band, axis=1)
        vb = lax.dynamic_slice_in_dim(vp, start, band, axis=1)
        s = jnp.einsum('bihd,bjhd->bhij', qb, kb).astype(F32) * scale + bias
        valid = (j + start) >= pad
        s = jnp.where(valid, s, jnp.finfo(F32).min)
        p = jax.nn.softmax(s, axis=-1).astype(vb.dtype)
        return jnp.einsum('bhij,bjhd->bihd', p, vb)

    o = lax.map(one_chunk, (jnp.arange(n_chunks), qc))
    o = jnp.moveaxis(o, 0, 1).reshape(bsz, t, h * hd)
    return o, kp[:, -n_rows:], vp[:, -n_rows:]


def band_attention_sample(q, k, v, cache_k, cache_v, table):
    bsz, l, h, hd = q.shape
    r = cache_k.shape[1]
    kk = jnp.concatenate([cache_k, k], axis=1)
    vv = jnp.concatenate([cache_v, v], axis=1)
    q_pos = PAST_LEN + jnp.arange(l)
    k_pos = PAST_LEN - r + jnp.arange(r + l)
    bias = rel_bias_lookup(q_pos[:, None] - k_pos[None, :], table)
    s = jnp.einsum('bihd,bjhd->bhij', q, kk).astype(F32) * (hd ** -0.5) + bias
    p = jax.nn.softmax(s, axis=-1).astype(vv.dtype)
    o = jnp.einsum('bhij,bjhd->bihd', p, vv).reshape(bsz, l, h * hd)
    return o, k, v


def rope(x, pos):
    half = x.shape[-1] // 2
    freqs = ROPE_BASE ** (-jnp.arange(half, dtype=F32) / half)
    ang = pos.astype(F32)[:, None] * freqs[None]
    cos = jnp.cos(ang)[None, :, None, :]
    sin = jnp.sin(ang)[None, :, None, :]
    x1 = x[..., :half].astype(F32)
    x2 = x[..., half:].astype(F32)
    return jnp.concatenate([x1 * cos - x2 * sin, x1 * sin + x2 * cos], axis=-1).astype(x.dtype)


def retention(q, k, v, s0, chunk):
    bsz, t, h, dk = q.shape
    dv = v.shape[-1]
    nc = t // chunk
    lg = jnp.log1p(-(2.0 ** (-5.0 - jnp.arange(h, dtype=F32))))
    qc = q.reshape(bsz, nc, chunk, h, dk).astype(F32) * (dk ** -0.5)
    kc = k.reshape(bsz, nc, chunk, h, dk).astype(F32)
    vc = v.reshape(bsz, nc, chunk, h, dv).astype(F32)
    i = jnp.arange(chunk, dtype=F32)
    intra = jnp.exp(lg[:, None, None] * jnp.abs(i[:, None] - i[None, :]))
    scores = jnp.einsum('bnihd,bnjhd->bnhij', qc, kc) * intra
    o = jnp.einsum('bnhij,bnjhe->bnihe', scores, vc)
    k_w = jnp.exp(lg[None, :] * (chunk - 1.0 - i)[:, None])
    contrib = jnp.einsum('bnjhd,jh,bnjhe->bnhde', kc, k_w, vc)
    n = jnp.arange(nc, dtype=F32)
    lag = n[:, None] - 1.0 - n[None, :]
    carry_w = jnp.where(lag >= 0, jnp.exp(lg[:, None, None] * chunk * jnp.maximum(lag, 0.0)), 0.0)
    s0f = s0.astype(F32)
    s_prev = (jnp.einsum('hnm,bmhde->bnhde', carry_w, contrib)
              + jnp.exp(lg[None, :] * chunk * n[:, None])[None, :, :, None, None] * s0f[:, None])
    q_w = jnp.exp(lg[None, :] * (i[:, None] + 1.0))
    o = o + jnp.einsum('bnihd,ih,bnhde->bnihe', qc, q_w, s_prev)
    fin_w = jnp.exp(lg[:, None] * chunk * (nc - 1.0 - n)[None, :])
    s_fin = (jnp.einsum('hm,bmhde->bhde', fin_w, contrib)
             + jnp.exp(lg * chunk * nc)[None, :, None, None] * s0f)
    return o.reshape(bsz, t, h, dv), s_fin


def peer_ffn(x, w_q, sub_keys, u_tab, v_tab):
    bsz, t, d = x.shape
    xf = x.reshape(-1, d)
    n_tok = xf.shape[0]
    q = (xf @ w_q).reshape(n_tok, PEER_HEADS, 2, PEER_KEY_DIM // 2).astype(F32)
    s = jnp.einsum('thsd,hskd->thsk', q, sub_keys.astype(F32))
    sc, ix = lax.top_k(s, PEER_TOPK)
    cand = (sc[:, :, 0, :, None] + sc[:, :, 1, None, :]).reshape(n_tok, PEER_HEADS, -1)
    cand_ix = (ix[:, :, 0, :, None] * N_KEYS + ix[:, :, 1, None, :]).reshape(n_tok, PEER_HEADS, -1)
    top_s, sel = lax.top_k(cand, PEER_TOPK)
    expert = jnp.take_along_axis(cand_ix, sel, axis=-1)
    gate = jax.nn.softmax(top_s, axis=-1)
    n_blk = -(-n_tok // PEER_BLOCK)
    pad = n_blk * PEER_BLOCK - n_tok
    xb = jnp.pad(xf, ((0, pad), (0, 0))).reshape(n_blk, PEER_BLOCK, d)
    eb = jnp.pad(expert, ((0, pad), (0, 0), (0, 0))).reshape(n_blk, PEER_BLOCK, PEER_HEADS, PEER_TOPK)
    gb = jnp.pad(gate, ((0, pad), (0, 0), (0, 0))).reshape(n_blk, PEER_BLOCK, PEER_HEADS, PEER_TOPK)

    def block(args):
        xk, ek, gk = args
        u = u_tab[ek]
        act = jax.nn.gelu(jnp.einsum('bhkd,bd->bhk', u, xk).astype(F32))
        w = (gk * act).astype(xk.dtype)
        return jnp.einsum('bhk,bhkd->bd', w, v_tab[ek])

    out = lax.map(block, (xb, eb, gb)).reshape(-1, d)[:n_tok]
    return out.reshape(bsz, t, d)


def trunk_layer(x, pe, h0_re, h0_im, ck, cv, s0, pos, prompt, n_rows, lp):
    bsz, t, _ = x.shape
    h = x @ lp['w_in']
    (u_ssm, q_att, k_att, v_att, q_ret, k_ret, v_ret, g_ret,
     gate_ssm, gate_att, gate_ret) = jnp.split(h, IN_OFFSETS, axis=-1)
    y_ssm, h_re, h_im = s5_mixer(u_ssm, h0_re, h0_im, lp['ssm_lam_re'], lp['ssm_lam_im'],
                                 lp['ssm_b_re'], lp['ssm_b_im'], lp['ssm_c_re'], lp['ssm_c_im'],
                                 lp['ssm_log_dt'], lp['ssm_d'], lp['ssm_w_glu'], lp['ssm_b_glu'])
    q_att = q_att.reshape(bsz, t, ATT_HEADS, ATT_HEAD_DIM)
    k_att = k_att.reshape(bsz, t, ATT_HEADS, ATT_HEAD_DIM)
    v_att = v_att.reshape(bsz, t, ATT_HEADS, ATT_HEAD_DIM)
    if prompt:
        y_att, new_k, new_v = band_attention_prompt(q_att, k_att, v_att, lp['att_rel_bias'], n_rows)
    else:
        y_att, new_k, new_v = band_attention_sample(q_att, k_att, v_att, ck, cv, lp['att_rel_bias'])
    q_ret = rope(q_ret.reshape(bsz, t, RET_HEADS, RET_KEY_DIM), pos)
    k_ret = rope(k_ret.reshape(bsz, t, RET_HEADS, RET_KEY_DIM), pos)
    o_ret, s_fin = retention(q_ret, k_ret, v_ret.reshape(bsz, t, RET_HEADS, RET_VAL_DIM), s0,
                             CHUNK if prompt else t)
    mu = o_ret.mean(-1, keepdims=True)
    var = jnp.square(o_ret - mu).mean(-1, keepdims=True)
    o_ret = ((o_ret - mu) * lax.rsqrt(var + LN_EPS)).reshape(bsz, t, RET_V)
    y_ret = (o_ret * lp['ret_gn_g'].astype(F32) * jax.nn.silu(g_ret.astype(F32))).astype(x.dtype)
    merged = (jax.nn.sigmoid(gate_ssm) * (y_ssm @ lp['w_br_ssm'])
              + jax.nn.sigmoid(gate_att) * (y_att @ lp['w_br_att'])
              + jax.nn.sigmoid(gate_ret) * (y_ret @ lp['w_br_ret']))
    x = layer_norm(DN_ALPHA * x + merged @ lp['w_o'], lp['ln1_g'], lp['ln1_b'])
    x = layer_norm(DN_ALPHA * x + peer_ffn(x, lp['peer_w_q'], lp['peer_sub_keys'], lp['peer_u'], lp['peer_v']),
                   lp['ln2_g'], lp['ln2_b'])
    ple = jax.nn.sigmoid(x @ lp['ple_w_g']) * (pe @ lp['ple_w_p'])
    x = layer_norm(DN_ALPHA * x + ple, lp['ln3_g'], lp['ln3_b'])
    return x, (h_re, h_im, new_k, new_v, s_fin)


def setup_inputs(seed: int = 0) -> dict:
    key = jax.random.key(seed)
    ks = iter(jax.random.split(key, 48))

    def nrm(shape, scale):
        return jax.random.normal(next(ks), shape, F32) * scale

    att_rows = min(ATT_PAST_CHUNKS * CHUNK, PAST_LEN)
    lam_im = jnp.tile(math.pi * jnp.arange(SSM_STATE, dtype=F32), (DEPTH, SSM_GROUPS, 1))
    return {
        'x_prompt': nrm((BATCH, SEQ, D_MODEL), 1.0),
        'x_sample': nrm((DEC_BATCH, DEC_SEQ, D_MODEL), 1.0),
        'p_prompt': nrm((DEPTH, BATCH, SEQ, PLE_DIM), 1.0),
        'p_sample': nrm((DEPTH, DEC_BATCH, DEC_SEQ, PLE_DIM), 1.0),
        'state_ssm_re': nrm((DEPTH, DEC_BATCH, SSM_GROUPS, SSM_STATE), 0.1),
        'state_ssm_im': nrm((DEPTH, DEC_BATCH, SSM_GROUPS, SSM_STATE), 0.1),
        'cache_attn_k': nrm((DEPTH, DEC_BATCH, att_rows, ATT_HEADS, ATT_HEAD_DIM), 1.0),
        'cache_attn_v': nrm((DEPTH, DEC_BATCH, att_rows, ATT_HEADS, ATT_HEAD_DIM), 1.0),
        'state_ret': nrm((DEPTH, DEC_BATCH, RET_HEADS, RET_KEY_DIM, RET_VAL_DIM), 4.0),
        'w_in': nrm((DEPTH, D_MODEL, IN_COLS), D_MODEL ** -0.5),
        'ssm_lam_re': -0.5 + nrm((DEPTH, SSM_GROUPS, SSM_STATE), 0.01),
        'ssm_lam_im': lam_im,
        'ssm_b_re': nrm((DEPTH, SSM_GROUPS, SSM_STATE, SSM_GROUP), (2 * SSM_GROUP) ** -0.5),
        'ssm_b_im': nrm((DEPTH, SSM_GROUPS, SSM_STATE, SSM_GROUP), (2 * SSM_GROUP) ** -0.5),
        'ssm_c_re': nrm((DEPTH, SSM_GROUPS, SSM_GROUP, SSM_STATE), SSM_STATE ** -0.5),
        'ssm_c_im': nrm((DEPTH, SSM_GROUPS, SSM_GROUP, SSM_STATE), SSM_STATE ** -0.5),
        'ssm_log_dt': jax.random.uniform(next(ks), (DEPTH, SSM_GROUPS), F32, math.log(1e-3), math.log(1e-1)),
        'ssm_d': nrm((DEPTH, SSM_WIDTH), 1.0),
        'ssm_w_glu': nrm((DEPTH, SSM_WIDTH, SSM_WIDTH), SSM_WIDTH ** -0.5),
        'ssm_b_glu': nrm((DEPTH, SSM_WIDTH), 0.01),
        'att_rel_bias': nrm((DEPTH, 2 * REL_CLIP + 1, ATT_HEADS), 0.1),
        'ret_gn_g': 1.0 + nrm((DEPTH, RET_V), 0.02),
        'w_br_ssm': nrm((DEPTH, SSM_WIDTH, D_MODEL), SSM_WIDTH ** -0.5),
        'w_br_att': nrm((DEPTH, ATT_WIDTH, D_MODEL), ATT_WIDTH ** -0.5),
        'w_br_ret': nrm((DEPTH, RET_V, D_MODEL), RET_V ** -0.5),
        'w_o': nrm((DEPTH, D_MODEL, D_MODEL), DN_BETA * D_MODEL ** -0.5),
        'ln1_g': 1.0 + nrm((DEPTH, D_MODEL), 0.02),
        'ln1_b': nrm((DEPTH, D_MODEL), 0.01),
        'peer_w_q': nrm((DEPTH, D_MODEL, PEER_HEADS * PEER_KEY_DIM), D_MODEL ** -0.5),
        'peer_sub_keys': nrm((DEPTH, PEER_HEADS, 2, N_KEYS, PEER_KEY_DIM // 2), (PEER_KEY_DIM // 2) ** -0.5),
        'peer_u': nrm((DEPTH, N_EXPERTS, D_MODEL), D_MODEL ** -0.5),
        'peer_v': nrm((DEPTH, N_EXPERTS, D_MODEL), DN_BETA * PEER_HEADS ** -0.5),
        'ln2_g': 1.0 + nrm((DEPTH, D_MODEL), 0.02),
        'ln2_b': nrm((DEPTH, D_MODEL), 0.01),
        'ple_w_g': nrm((DEPTH, D_MODEL, D_MODEL), D_MODEL ** -0.5),
        'ple_w_p': nrm((DEPTH, PLE_DIM, D_MODEL), DN_BETA * PLE_DIM ** -0.5),
        'ln3_g': 1.0 + nrm((DEPTH, D_MODEL), 0.02),
        'ln3_b': nrm((DEPTH, D_MODEL), 0.01),
    }


def reference(x_prompt, x_sample, p_prompt, p_sample, state_ssm_re, state_ssm_im, cache_attn_k, cache_attn_v,
              state_ret, w_in, ssm_lam_re, ssm_lam_im, ssm_b_re, ssm_b_im, ssm_c_re, ssm_c_im, ssm_log_dt,
              ssm_d, ssm_w_glu, ssm_b_glu, att_rel_bias, ret_gn_g, w_br_ssm, w_br_att, w_br_ret, w_o,
              ln1_g, ln1_b, peer_w_q, peer_sub_keys, peer_u, peer_v, ln2_g, ln2_b, ple_w_g, ple_w_p,
              ln3_g, ln3_b):
    n_rows = cache_attn_k.shape[2]
    bsz_p = x_prompt.shape[0]
    pos_p = jnp.arange(x_prompt.shape[1])
    pos_s = PAST_LEN + jnp.arange(x_sample.shape[1])
    y_p, y_s = x_prompt, x_sample
    new_p = ([], [], [], [], [])
    new_s = ([], [], [], [], [])
    for li in range(DEPTH):
        lp = {
            'w_in': w_in[li], 'ssm_lam_re': ssm_lam_re[li], 'ssm_lam_im': ssm_lam_im[li],
            'ssm_b_re': ssm_b_re[li], 'ssm_b_im': ssm_b_im[li], 'ssm_c_re': ssm_c_re[li],
            'ssm_c_im': ssm_c_im[li], 'ssm_log_dt': ssm_log_dt[li], 'ssm_d': ssm_d[li],
            'ssm_w_glu': ssm_w_glu[li], 'ssm_b_glu': ssm_b_glu[li], 'att_rel_bias': att_rel_bias[li],
            'ret_gn_g': ret_gn_g[li], 'w_br_ssm': w_br_ssm[li], 'w_br_att': w_br_att[li],
            'w_br_ret': w_br_ret[li], 'w_o': w_o[li], 'ln1_g': ln1_g[li], 'ln1_b': ln1_b[li],
            'peer_w_q': peer_w_q[li], 'peer_sub_keys': peer_sub_keys[li], 'peer_u': peer_u[li],
            'peer_v': peer_v[li], 'ln2_g': ln2_g[li], 'ln2_b': ln2_b[li], 'ple_w_g': ple_w_g[li],
            'ple_w_p': ple_w_p[li], 'ln3_g': ln3_g[li], 'ln3_b': ln3_b[li],
        }
        zero_ssm = jnp.zeros((bsz_p, SSM_GROUPS, SSM_STATE), F32)
        zero_ret = jnp.zeros((bsz_p, RET_HEADS, RET_KEY_DIM, RET_VAL_DIM), F32)
        y_p, st = trunk_layer(y_p, p_prompt[li], zero_ssm, zero_ssm, None, None, zero_ret, pos_p,
                              True, n_rows, lp)
        for lst, a in zip(new_p, st):
            lst.append(a)
        y_s, st = trunk_layer(y_s, p_sample[li], state_ssm_re[li], state_ssm_im[li], cache_attn_k[li],
                              cache_attn_v[li], state_ret[li], pos_s, False, n_rows, lp)
        for lst, a in zip(new_s, st):
            lst.append(a)
    ssm_re_p = jnp.stack(new_p[0])
    ssm_im_p = jnp.stack(new_p[1])
    att_k_p = jnp.stack(new_p[2])
    att_v_p = jnp.stack(new_p[3])
    ret_p = jnp.stack(new_p[4])
    ssm_re_s = jnp.stack(new_s[0])
    ssm_im_s = jnp.stack(new_s[1])
    att_k_s = jnp.stack(new_s[2])
    att_v_s = jnp.stack(new_s[3])
    ret_s = jnp.stack(new_s[4])
    return (y_p, y_s, ssm_re_p, ssm_im_p, att_k_p, att_v_p, ret_p, ssm_re_s, ssm_im_s, att_k_s, att_v_s, ret_s)
```

```python
import numpy as np
import concourse.bass as bass
import concourse.mybir as mybir
from concourse.bass_utils import run_bass_kernel_spmd
from contextlib import ExitStack

F32 = mybir.dt.float32
BF16 = mybir.dt.bfloat16
U32 = mybir.dt.uint32
ALU = mybir.AluOpType
AF = mybir.ActivationFunctionType
AX = mybir.AxisListType

D = 1024
DEPTH = 2
NTP_FULL = 32
ALPHA = float((2 * DEPTH) ** 0.25)
LN_EPS = 1e-5
NEG = -30000.0
IN_BLOCKS = [(0, 256), (256, 512), (768, 512), (1280, 512), (1792, 512), (2304, 512),
             (2816, 512), (3328, 512), (3840, 512), (4352, 512), (4864, 512), (5376, 512)]
R_DSK, R_BGLU, R_GNG = 0, 256, 512
NROW = 6912


STRICT = True


class Dep:
    def __init__(self, parent=None):
        self.w = None
        self.r = {}
        self.parent = parent
        self.kids = {}

    def lane(self, key):
        if key not in self.kids:
            self.kids[key] = Dep(self)
        return self.kids[key]


class Tn(Dep):
    def __init__(self, t, name=""):
        super().__init__(None)
        self.t = t
        self.name = name

    def __getitem__(self, k):
        return self.t[k]


class KB:
    NS = 8

    def __init__(self, nc, es):
        self.nc = nc
        self.es = es
        self.engs = {"pe": nc.tensor, "dve": nc.vector, "act": nc.scalar, "pool": nc.gpsimd, "sp": nc.sync}
        self.sem = {e: es.enter_context(nc.semaphore("s_" + e)) for e in self.engs}
        self.cnt = {e: 0 for e in self.engs}
        self.seen = {e: {} for e in self.engs}
        self.dsem = {q: [es.enter_context(nc.semaphore("d_%s%d" % (q, i))) for i in range(self.NS)] for q in ("sp", "pool")}
        self.dcnt = {q: [0] * self.NS for q in ("sp", "pool")}
        self.dnext = {q: 0 for q in ("sp", "pool")}
        self.out_tokens = []
        self.rec = None
        self.iter = 0
        self.tok_iter = {}

    def sb(self, name, shape, dtype):
        return Tn(self.es.enter_context(self.nc.sbuf_tensor(name, shape, dtype)), name)

    def _wait(self, eng, tok):
        sem, val, key = tok
        if self.seen[eng].get(key, 0) >= val:
            return
        self.engs[eng].wait_ge(sem, val)
        self.seen[eng][key] = val

    def _deps1(self, eng, b, writing):
        strict = STRICT and eng != "pe"
        if b.w is not None and (b.w[2] != eng or (strict if writing else eng != "pe")):
            self._wait(eng, b.w)
        if writing:
            for k, t in b.r.items():
                if strict or k != eng:
                    self._wait(eng, t)

    def _deps(self, eng, r, w):
        for lst, writing in ((r, False), (w, True)):
            for b in lst:
                self._deps1(eng, b, writing)
                if b.parent is not None:
                    self._deps1(eng, b.parent, writing)
                for k_ in b.kids.values():
                    self._deps1(eng, k_, writing)

    def _mark(self, tok, r, w):
        for b in r:
            b.r[tok[2]] = tok
        for b in w:
            b.w = tok
            b.r = {}
            for k_ in b.kids.values():
                k_.w = tok
                k_.r = {}

    def ready(self, eng, r, w, age):
        ok = [True]

        def chk(b, writing):
            toks = []
            if b.w is not None:
                toks.append(b.w)
            if writing:
                toks.extend(b.r.values())
            for t in toks:
                sem, val, key = t
                if key == eng or self.seen[eng].get(key, 0) >= val:
                    continue
                if self.iter - self.tok_iter.get((key, val), -10 ** 9) < age:
                    ok[0] = False
        for lst, writing in ((r, False), (w, True)):
            for b in lst:
                chk(b, writing)
                if b.parent is not None:
                    chk(b.parent, writing)
                for k_ in b.kids.values():
                    chk(k_, writing)
        return ok[0]

    def op(self, eng, fn, r=(), w=()):
        r = _flat(r)
        w = _flat(w)
        if self.rec is not None:
            self.rec.append(("op", eng, fn, r, w, False))
            return None
        self._deps(eng, r, w)
        ins = fn(self.engs[eng])
        self.cnt[eng] += 1
        ins.then_inc(self.sem[eng], 1)
        tok = (self.sem[eng], self.cnt[eng], eng)
        self.tok_iter[(eng, self.cnt[eng])] = self.iter
        self._mark(tok, r, w)
        return tok

    def dma(self, q, fn, r=(), w=(), is_out=False):
        r = _flat(r)
        w = _flat(w)
        if self.rec is not None:
            self.rec.append(("dma", q, fn, r, w, is_out))
            return None
        self._deps(q, r, w)
        slot = self.dnext[q]
        self.dnext[q] = (slot + 1) % self.NS
        sem = self.dsem[q][slot]
        key = (q, slot)
        if self.dcnt[q][slot] > 0:
            self._wait(q, (sem, 16 * self.dcnt[q][slot], key))
        ins = fn(self.engs[q])
        self.dcnt[q][slot] += 1
        ins.then_inc(sem, 16)
        tok = (sem, 16 * self.dcnt[q][slot], key)
        self.tok_iter[(key, 16 * self.dcnt[q][slot])] = self.iter
        self._mark(tok, r, w)
        if is_out:
            self.out_tokens.append(tok)
        return tok

    def finish(self):
        for q in ("sp", "pool"):
            for slot in range(self.NS):
                if self.dcnt[q][slot] > 0:
                    self._wait("sp", (self.dsem[q][slot], 16 * self.dcnt[q][slot], (q, slot)))


def _flat(lst):
    out = []
    for x in lst:
        if x is None:
            continue
        if isinstance(x, (list, tuple)):
            out.extend(_flat(x))
        else:
            out.append(x)
    return out


STAGES = {"prep", "inproj", "s5", "attn", "ret", "merge", "peer", "ple"}
PIPELINE = True
PIPE_EVERY = 4
PIPE_VERBOSE = False
PIPE_AGE = 1
DBG_TILE = (0, 0)


def build_program(NTP=NTP_FULL, dbg=None):
    NT = NTP + 2
    NTOK = NTP * 128 + 128
    nc = bass.Bass("TRN2", target_bir_lowering=False)
    es = ExitStack()

    def din(name, shape, dt=F32):
        return nc.dram_tensor(name, list(shape), dt, kind="ExternalInput").ap()

    def dout(name, shape, dt=F32):
        return nc.dram_tensor(name, list(shape), dt, kind="ExternalOutput").ap()

    def dint(name, shape, dt=F32):
        return nc.dram_tensor(name, list(shape), dt, kind="Internal").ap()

    xin = din("xin", [NTOK, D])
    pin = din("pin", [2, NTOK, 256])
    st_ssm = din("st_ssm", [2, 2, 128, 16])
    cache_k = din("cache_k", [2, 2, 512, 512])
    cache_v = din("cache_v", [2, 2, 512, 512])
    st_ret = din("st_ret", [2, 2, 64, 256])
    W = {
        "w_in": din("w_in", [2, 1024, 5888]), "w_glu": din("w_glu", [2, 256, 256]),
        "w_br_ssm": din("w_br_ssm", [2, 256, 1024]), "w_br_att": din("w_br_att", [2, 512, 1024]),
        "w_br_ret": din("w_br_ret", [2, 256, 1024]), "w_o": din("w_o", [2, 1024, 1024]),
        "w_q": din("w_q", [2, 1024, 2048]), "w_g": din("w_g", [2, 1024, 1024]),
        "w_p": din("w_p", [2, 256, 1024]), "bbd": din("bbd", [2, 256, 2048]),
    }
    peer_u = [din("peer_u%d" % i, [16384, 1024]) for i in range(2)]
    peer_v = [din("peer_v%d" % i, [16384, 1024]) for i in range(2)]
    cmat = din("cmat", [2, 128, 512])
    keysT = din("keysT", [2, 128, 2048])
    s5par = din("s5par", [2, 128, 24])
    rows = din("rows", [2, 128, NROW])
    bias2 = din("bias2", [2, 128, 8 * 640])
    c_ident = din("c_ident", [128, 128])
    c_l2t = din("c_l2t", [128, 128])
    c_kk = din("c_kk", [128, 128])
    c_iota = din("c_iota", [128, 16])
    c_mask = din("c_mask", [128, 640])
    c_cos = din("c_cos", [128, NT * 32])
    c_sin = din("c_sin", [128, NT * 32])
    c_ret = din("c_ret", [128, 12])
    c_dt = din("c_dt", [128, 512])
    c_dec = din("c_dec", [64, 512])

    y_out = dout("y_out", [NTOK, D])
    ssm_out = dout("ssm_out", [2, 3, 128, 16])
    k_out = dout("k_out", [2, 640, 512])
    v_out = dout("v_out", [2, 640, 512])
    ret_out = dout("ret_out", [2, 3, 64, 256])
    dbg_aps = {}
    if dbg:
        for name, (shape, dt_) in dbg.items():
            dbg_aps[name] = dout("dbg_" + name, shape, dt_)

    X1 = dint("X1", [NTOK, D])
    WB = {k: dint("wb_" + k, list(v.shape), BF16) for k, v in W.items()}
    peer_uv = [dint("peer_uv%d" % i, [16384, 2048], BF16) for i in range(2)]

    with es:
        kb = KB(nc, es)
        pe, dve, act, pool = "pe", "dve", "act", "pool"

        Fd = [es.enter_context(nc.psum_tensor("F%d" % i, [128, 1024], F32)) for i in range(3)]
        B0 = es.enter_context(nc.psum_tensor("B0", [128, 1024], BF16))
        FA = es.enter_context(nc.psum_tensor("FA", [128, 512], F32))
        Fs = [Tn(Fd[i // 2][:, (i % 2) * 512:(i % 2) * 512 + 512], "F%d_%d" % (i // 2, i % 2)) for i in range(6)]
        Bs = [Tn(B0, "B0")]
        FAs = Tn(FA, "FA")
        psst = {"f1": 0, "f2": 0}

        class PS:
            def __init__(self, ap, deps):
                self.ap = ap
                self.deps = deps

            def __getitem__(self, k):
                return self.ap[k]

        def ps1():
            i = psst["f1"] % 4
            psst["f1"] += 1
            return PS(Fs[i].t, [Fs[i]])

        def ps2():
            i = psst["f2"] % 2
            psst["f2"] += 1
            return PS(Fd[i], [Fs[2 * i], Fs[2 * i + 1]])

        def ps_acc():
            return PS(FA, [FAs])

        def ps_vacc():
            return PS(Fd[2], [Fs[4], Fs[5]])

        def psb():
            return PS(B0, [Bs[0]])

        def D_(x):
            return x.deps if isinstance(x, PS) else x

        def op(eng, fn, r=(), w=()):
            return kb.op(eng, fn, [D_(x) for x in r], [D_(x) for x in w])

        identF = kb.sb("identF", [128, 128], F32)
        ident = kb.sb("ident", [128, 128], BF16)
        l2t = kb.sb("l2t", [128, 128], BF16)
        iota16 = kb.sb("iota16", [128, 16], F32)
        cs_t = kb.sb("cs_t", [128, 2, 32], F32)
        cret = kb.sb("cret", [128, 12], F32)
        dtt = kb.sb("dtt", [128, 4, 128], F32)
        dect = kb.sb("dect", [64, 2, 256], F32)
        ctmp = kb.sb("ctmp", [128, 128], F32)

        def ld(dst, dst_ap, src, q="sp"):
            kb.dma(q, lambda e: e.dma_start(out=dst_ap, in_=src), r=[], w=[dst])

        ld(identF, identF[:], c_ident)
        ld(ctmp, ctmp[:], c_l2t)
        ld(iota16, iota16[:], c_iota)
        ld(cret, cret[:], c_ret)
        ld(dtt, dtt[:], c_dt.rearrange("p (h i) -> p h i", h=4))
        ld(dect, dect[:], c_dec.rearrange("p (a c) -> p a c", a=2))
        op(dve, lambda e: e.tensor_copy(out=ident[:], in_=identF[:]), r=[identF], w=[ident])
        op(dve, lambda e: e.tensor_copy(out=l2t[:], in_=ctmp[:]), r=[ctmp], w=[l2t])

        wb_dep = Dep()
        for k, src in W.items():
            tot = 1
            for s_ in src.shape:
                tot *= s_
            sf = src.rearrange("l a b -> (l a b)").rearrange("(r c) -> r c", c=2048)
            df = WB[k].rearrange("l a b -> (l a b)").rearrange("(r c) -> r c", c=2048)
            nrow = tot // 2048
            for r0 in range(0, nrow, 256):
                r1 = min(nrow, r0 + 256)
                kb.dma(pool, lambda e, a=df[r0:r1, :], b=sf[r0:r1, :]: e.dma_start(out=a, in_=b), r=[], w=[])
        for li_ in range(2):
            for hh_, src in enumerate((peer_u[li_], peer_v[li_])):
                for r0 in range(0, 16384, 512):
                    kb.dma(pool, lambda e, a=peer_uv[li_][r0:r0 + 512, hh_ * 1024:(hh_ + 1) * 1024], b=src[r0:r0 + 512, :]: e.dma_start(out=a, in_=b), r=[], w=[])
        cast_tokens = [(kb.dsem[pool][s], 16 * kb.dcnt[pool][s], (pool, s)) for s in range(kb.NS) if kb.dcnt[pool][s] > 0]
        for t in cast_tokens:
            kb._wait("sp", t)

        NWS = 2
        wst = [kb.sb("wst%d" % i, [128, 8, 512], BF16) for i in range(NWS)]
        wq = []
        wstate = {"issued": 0, "used": 0}

        def w_issue():
            i = wstate["issued"]
            name, li, k0, K, c0, N = wq[i]
            slot = wst[i % NWS]
            src = WB[name][li, k0 * 128:(k0 + K) * 128, c0:c0 + N].rearrange("(k p) n -> p k n", p=128)
            kb.dma("sp", lambda e: e.dma_start(out=slot[:, 0:K, 0:N], in_=src), r=[], w=[slot])
            wstate["issued"] += 1

        def w_issue_upto(i):
            while wstate["issued"] < min(len(wq), i + NWS):
                w_issue()

        def wget(spec):
            i = wstate["used"]
            assert wq[i] == spec, (wq[i], spec)
            wstate["used"] += 1
            if kb.rec is not None:
                kb.rec.append(("wissue", i))
            else:
                w_issue_upto(i)
            return wst[i % NWS]

        def tile_specs(li):
            sp = []
            if "inproj" in STAGES:
                sp += [("w_in", li, 0, 8, c0, n) for (c0, n) in IN_BLOCKS]
            if "merge" in STAGES:
                for name, K in (("w_br_ssm", 2), ("w_br_att", 4), ("w_br_ret", 2)):
                    sp += [(name, li, 0, K, 0, 512), (name, li, 0, K, 512, 512)]
                sp += [("w_o", li, 0, 8, 0, 512), ("w_o", li, 0, 8, 512, 512)]
            if "peer" in STAGES:
                sp += [("w_q", li, 0, 8, c * 512, 512) for c in range(4)]
            if "ple" in STAGES:
                sp += [("w_g", li, 0, 8, 0, 512), ("w_g", li, 0, 8, 512, 512)]
                sp += [("w_p", li, 0, 2, 0, 512), ("w_p", li, 0, 2, 512, 512)]
            return sp

        def specs_p1a(li):
            sp = []
            if "inproj" in STAGES:
                sp += [("w_in", li, 0, 8, c0, n) for (c0, n) in IN_BLOCKS]
            if "merge" in STAGES:
                for name, K in (("w_br_ssm", 2), ("w_br_att", 4), ("w_br_ret", 2)):
                    sp += [(name, li, 0, K, 0, 512), (name, li, 0, K, 512, 512)]
                sp += [("w_o", li, 0, 8, 0, 512), ("w_o", li, 0, 8, 512, 512)]
            return sp

        for li in range(DEPTH):
            wq.extend(specs_p1a(li))
            wq.extend([("w_q", li, 0, 8, c * 512, 512) for c in range(4)])
            for n in range(NT):
                if n + 1 < NT:
                    wq.extend(specs_p1a(li))
                    wq.extend([("w_q", li, 0, 8, c * 512, 512) for c in range(4)])
                if "ple" in STAGES:
                    wq.extend([("w_g", li, 0, 8, 0, 512), ("w_g", li, 0, 8, 512, 512), ("w_p", li, 0, 2, 0, 512), ("w_p", li, 0, 2, 512, 512)])

        rows_g = kb.sb("rows_g", [128, 3, 1024], BF16)
        rows_b = kb.sb("rows_b", [128, 3, 1024], BF16)
        rows_s = kb.sb("rows_s", [128, 768], BF16)
        bbd_b = kb.sb("bbd_b", [128, 2, 2, 512], BF16)
        cm_f = kb.sb("cm_f", [128, 512], F32)
        cm_b = kb.sb("cm_b", [128, 8, 2, 32], BF16)
        wglu_b = kb.sb("wglu_b", [128, 2, 256], BF16)
        keys_b = kb.sb("keys_b", [128, 16, 128], BF16)
        bias_b = kb.sb("bias_b", [128, 8, 640], BF16)
        ainvk = kb.sb("ainvk", [128, 2, 1024], BF16)
        a1t = kb.sb("a1t", [128, 2, 8, 128], F32)
        par = kb.sb("par", [128, 24], F32)
        sp_ = {n_: kb.sb("sp_" + n_, [128, 8], F32) for n_ in
               ("dt", "ldr", "th", "c1", "s1", "t0", "t1", "t2", "are", "aim", "kre", "kim", "l2", "nldr")}
        hst = kb.sb("hst", [128, 16], F32)
        sret_f = kb.sb("sret_f", [64, 256], F32)
        sret_b = kb.sb("sret_b", [64, 4, 64], BF16)
        kt2 = kb.sb("kt2", [128, 4, 9 * 128], BF16)
        vr = kb.sb("vr", [128, 9, 512], BF16)
        scA = kb.sb("scA", [128, 2048], F32)
        scB = kb.sb("scB", [128, 2048], F32)
        scC = kb.sb("scC", [128, 1024], F32)
        x_f = kb.sb("x_f", [128, D], F32)
        x1_f = kb.sb("x1_f", [128, D], F32)
        xpre = kb.sb("xpre", [128, D], F32)
        xb = kb.sb("xb", [128, D], BF16)
        xT = kb.sb("xT", [128, 8, 128], BF16)
        gates_b = kb.sb("gates_b", [128, 3, 1024], BF16)
        u_f = kb.sb("u_f", [128, 256], F32)
        u_b = kb.sb("u_b", [128, 256], BF16)
        qa_b = kb.sb("qa_b", [128, 512], BF16)
        ka_b = kb.sb("ka_b", [128, 512], BF16)
        vr_b = kb.sb("vr_b", [128, 256], BF16)
        sg_f = kb.sb("sg_f", [128, 256], F32)
        sm = kb.sb("sm", [128, 64], F32)
        bigb = kb.sb("bigb", [128, 2048], BF16)
        uT = kb.sb("uT", [128, 2, 128], BF16)
        yssm_b = kb.sb("yssm_b", [128, 256], BF16)
        yatt_b = kb.sb("yatt_b", [128, 512], BF16)
        yret_b = kb.sb("yret_b", [128, 256], BF16)
        qT2 = kb.sb("qT2", [128, 4, 128], BF16)
        pb_att = kb.sb("pb_att", [128, 640], BF16)
        pT = kb.sb("pT", [128, 5, 128], BF16)
        qkT = kb.sb("qkT", [64, 8, 128], BF16)
        qsT = kb.sb("qsT", [64, 4, 128], BF16)
        qk_b = kb.sb("qk_b", [128, 512], BF16)
        qs_b = kb.sb("qs_b", [128, 256], BF16)
        kk_b = kb.sb("kk_b", [128, 256], BF16)
        scT_b = kb.sb("scT_b", [128, 4, 128], BF16)
        yT = kb.sb("yT", [128, 8, 128], BF16)
        topa = kb.sb("topa", [128, 16, 16], F32)
        topi = kb.sb("topi", [128, 16, 16], U32)
        top2 = kb.sb("top2", [128, 8, 16], F32)
        pos2 = kb.sb("pos2", [128, 8, 16], U32)
        eidx = kb.sb("eidx", [128, 128], U32)
        eidx_b = kb.sb("eidx_b", [128, 128], U32)
        gate_a = kb.sb("gate_a", [128, 128], F32)
        gate_b = kb.sb("gate_b", [128, 128], F32)
        pact = kb.sb("pact", [128, 128], F32)
        pwv = kb.sb("pwv", [128, 128], F32)
        ptmp = kb.sb("ptmp", [128, 128], F32)
        pag = kb.sb("pag", [128, 128], F32)
        NGB = 8
        ubuf = [kb.sb("ubuf%d" % i, [128, 2 * D], BF16) for i in range(NGB)]
        NVR = 6
        vring = [kb.sb("vring%d" % i, [128, D], BF16) for i in range(NVR)]
        vbuf = ubuf
        dgb = [kb.sb("dgb%d" % i, [128, 128], BF16) for i in range(4)]
        pe_f = kb.sb("pe_f", [128, 256], F32)
        print("SBUF bytes remaining per partition:", nc.sbuf_bytes_remaining)

        def dump(name, src_ap, deps):
            if name in dbg_aps:
                kb.dma("sp", lambda e: e.dma_start(out=dbg_aps[name], in_=src_ap), r=deps, w=[], is_out=True)

        op(pool, lambda e: e.memset(eidx[:], 0), w=[eidx])
        op(pool, lambda e: e.memset(eidx_b[:], 0), w=[eidx_b])
        X1P = [x1_f, xpre]
        EIP = [eidx, eidx_b]
        GTP = [gate_a, gate_b]

        def transposes(src, nblk, M, dst, dst_slices, cw=128, src_cols=None):
            pb = psb()

            def f(e):
                ins = None
                for i in range(nblk):
                    c0 = i * cw if src_cols is None else src_cols[i]
                    ins = e.transpose(out=pb[0:cw, i * 128:i * 128 + M], in_=src[0:M, c0:c0 + cw], identity=ident[0:M, 0:M])
                return ins
            op(pe, f, r=[src, ident], w=[pb])
            op(dve, lambda e: e.tensor_copy(out=dst_slices, in_=pb[0:cw, 0:nblk * 128].rearrange("p (b m) -> p b m", m=128)[:, :, 0:M]),
               r=[pb], w=[dst])

        def layer_norm(src, idx, dst, M, src_ap=None):
            junk = scA
            sap = src[0:M, :] if src_ap is None else src_ap
            op(act, lambda e: e.activation(out=scA[0:M, 0:1024], in_=sap, func=AF.Identity, accum_out=sm[0:M, 0:1]), r=[src], w=[junk, sm])
            op(act, lambda e: e.activation(out=scA[0:M, 0:1024], in_=sap, func=AF.Square, accum_out=sm[0:M, 1:2]), r=[src], w=[junk, sm])
            op(dve, lambda e: e.tensor_scalar(out=sm[0:M, 2:3], in0=sm[0:M, 0:1], scalar1=1.0 / D, scalar2=None, op0=ALU.mult), r=[sm], w=[sm])
            op(dve, lambda e: e.tensor_tensor(out=sm[0:M, 3:4], in0=sm[0:M, 2:3], in1=sm[0:M, 2:3], op=ALU.mult), r=[sm], w=[sm])
            op(dve, lambda e: e.scalar_tensor_tensor(out=sm[0:M, 4:5], in0=sm[0:M, 1:2], scalar=1.0 / D, in1=sm[0:M, 3:4], op0=ALU.mult, op1=ALU.subtract), r=[sm], w=[sm])
            op(dve, lambda e: e.tensor_scalar(out=sm[0:M, 4:5], in0=sm[0:M, 4:5], scalar1=LN_EPS, scalar2=None, op0=ALU.add), r=[sm], w=[sm])
            op(act, lambda e: e.activation(out=sm[0:M, 5:6], in_=sm[0:M, 4:5], func=AF.Sqrt), r=[sm], w=[sm])
            op(dve, lambda e: e.reciprocal(out=sm[0:M, 6:7], in_=sm[0:M, 5:6]), r=[sm], w=[sm])
            op(dve, lambda e: e.scalar_tensor_tensor(out=sm[0:M, 7:8], in0=sm[0:M, 2:3], scalar=-1.0, in1=sm[0:M, 6:7], op0=ALU.mult, op1=ALU.mult), r=[sm], w=[sm])
            op(act, lambda e: e.activation(out=scA[0:M, 0:1024], in_=sap, func=AF.Identity, scale=sm[0:M, 6:7], bias=sm[0:M, 7:8]), r=[src, sm], w=[junk])
            op(dve, lambda e: e.tensor_tensor(out=scA[0:M, 0:1024], in0=scA[0:M, 0:1024], in1=rows_g[0:M, idx, :], op=ALU.mult), r=[junk, rows_g], w=[junk])
            op(dve, lambda e: e.tensor_tensor(out=dst[0:M, :], in0=scA[0:M, 0:1024], in1=rows_b[0:M, idx, :], op=ALU.add), r=[junk, rows_b], w=[dst])

        def gelu_tanh(src_ap, dst_ap, tmp_ap, deps_r, deps_w, tmp_dep):
            op(dve, lambda e: e.tensor_tensor(out=tmp_ap, in0=src_ap, in1=src_ap, op=ALU.mult), r=deps_r, w=[tmp_dep])
            op(dve, lambda e: e.tensor_scalar(out=tmp_ap, in0=tmp_ap, scalar1=0.044715, scalar2=1.0, op0=ALU.mult, op1=ALU.add), r=[tmp_dep], w=[tmp_dep])
            op(dve, lambda e: e.tensor_tensor(out=tmp_ap, in0=tmp_ap, in1=src_ap, op=ALU.mult), r=[tmp_dep] + deps_r, w=[tmp_dep])
            op(act, lambda e: e.activation(out=tmp_ap, in_=tmp_ap, func=AF.Sigmoid, scale=1.5957691216), r=[tmp_dep], w=[tmp_dep])
            op(dve, lambda e: e.tensor_tensor(out=dst_ap, in0=tmp_ap, in1=src_ap, op=ALU.mult), r=[tmp_dep] + deps_r, w=deps_w)

        def layer_prep(li):
            kb.dma(pool, lambda e: e.dma_start(out=rows_g[:], in_=rows[li][:, 0:3072].rearrange("p (a d) -> p a d", a=3)), w=[rows_g])
            kb.dma(pool, lambda e: e.dma_start(out=rows_b[:], in_=rows[li][:, 3072:6144].rearrange("p (a d) -> p a d", a=3)), w=[rows_b])
            kb.dma(pool, lambda e: e.dma_start(out=rows_s[:], in_=rows[li][:, 6144:6912]), w=[rows_s])
            for k_ in range(2):
                for part_ in range(2):
                    c0_ = part_ * 1024 + k_ * 512
                    kb.dma("sp", lambda e, k_=k_, part_=part_, c0_=c0_: e.dma_start(out=bbd_b[:, k_, part_, :], in_=WB["bbd"][li, k_ * 128:(k_ + 1) * 128, c0_:c0_ + 512]), w=[bbd_b])
            kb.dma("sp", lambda e: e.dma_start(out=wglu_b[:], in_=WB["w_glu"][li].rearrange("(k p) n -> p k n", p=128)), w=[wglu_b])
            kb.dma("sp", lambda e: e.dma_start(out=cm_f[:], in_=cmat[li]), w=[cm_f])
            op(dve, lambda e: e.tensor_copy(out=cm_b[:].rearrange("p a b c -> p (a b c)"), in_=cm_f[:]), r=[cm_f], w=[cm_b])
            kb.dma("sp", lambda e: e.dma_start(out=scA[:, :], in_=keysT[li]), w=[scA])
            op(dve, lambda e: e.tensor_copy(out=keys_b[:].rearrange("p a b -> p (a b)"), in_=scA[:, :]), r=[scA], w=[keys_b])
            kb.dma("sp", lambda e: e.dma_start(out=scB[:, 0:640], in_=c_mask), w=[scB])
            for h in range(8):
                kb.dma("sp", lambda e, h=h: e.dma_start(out=scC[:, 0:640], in_=bias2[li][:, h * 640:(h + 1) * 640]), w=[scC])
                op(dve, lambda e, h=h: e.tensor_tensor(out=bias_b[:, h, :], in0=scC[:, 0:640], in1=scB[:, 0:640], op=ALU.add), r=[scC, scB], w=[bias_b])
            kb.dma("sp", lambda e: e.dma_start(out=par[:], in_=s5par[li]), w=[par])
            P = sp_
            lre, lim, ldt = par[:, 0:8], par[:, 8:16], par[:, 16:24]
            op(act, lambda e: e.activation(out=P["dt"][:], in_=ldt, func=AF.Exp), r=[par], w=[P["dt"]])
            op(dve, lambda e: e.tensor_tensor(out=P["ldr"][:], in0=lre, in1=P["dt"][:], op=ALU.mult), r=[par, P["dt"]], w=[P["ldr"]])
            op(dve, lambda e: e.tensor_tensor(out=P["th"][:], in0=lim, in1=P["dt"][:], op=ALU.mult), r=[par, P["dt"]], w=[P["th"]])
            op(dve, lambda e: e.tensor_scalar(out=P["nldr"][:], in0=P["ldr"][:], scalar1=-1.0, scalar2=None, op0=ALU.mult), r=[P["ldr"]], w=[P["nldr"]])

            def sin_reduced(dst, shift):
                TWO_PI = 2.0 * np.pi
                op(dve, lambda e: e.tensor_scalar(out=P["t0"][:], in0=P["th"][:], scalar1=float(shift), scalar2=1.0 / TWO_PI, op0=ALU.add, op1=ALU.mult), r=[P["th"]], w=[P["t0"]])
                tI = topi
                op(dve, lambda e: e.tensor_copy(out=tI[:, 0, 0:8], in_=P["t0"][:]), r=[P["t0"]], w=[tI])
                op(dve, lambda e: e.tensor_copy(out=P["t1"][:], in_=tI[:, 0, 0:8]), r=[tI], w=[P["t1"]])
                op(dve, lambda e: e.tensor_tensor(out=P["t2"][:], in0=P["t0"][:], in1=P["t1"][:], op=ALU.subtract), r=[P["t0"], P["t1"]], w=[P["t2"]])
                op(dve, lambda e: e.tensor_scalar(out=P["t1"][:], in0=P["t2"][:], scalar1=0.5, scalar2=None, op0=ALU.is_gt), r=[P["t2"]], w=[P["t1"]])
                op(dve, lambda e: e.tensor_tensor(out=P["t2"][:], in0=P["t2"][:], in1=P["t1"][:], op=ALU.subtract), r=[P["t2"], P["t1"]], w=[P["t2"]])
                op(dve, lambda e: e.tensor_scalar(out=P["t2"][:], in0=P["t2"][:], scalar1=TWO_PI, scalar2=3.1415925, op0=ALU.mult, op1=ALU.min), r=[P["t2"]], w=[P["t2"]])
                op(dve, lambda e: e.tensor_scalar(out=P["t2"][:], in0=P["t2"][:], scalar1=-3.1415925, scalar2=None, op0=ALU.max), r=[P["t2"]], w=[P["t2"]])
                op(act, lambda e: e.activation(out=dst[:], in_=P["t2"][:], func=AF.Sin), r=[P["t2"]], w=[dst])
            sin_reduced(P["s1"], 0.0)
            sin_reduced(P["c1"], np.pi / 2)
            Ere = scA[:, 0:1024].rearrange("p (c k) -> p c k", c=8)
            Eim = scA[:, 1024:2048].rearrange("p (c k) -> p c k", c=8)
            Tm = scB[:, 0:1024].rearrange("p (c k) -> p c k", c=8)
            Tm2 = scB[:, 1024:2048].rearrange("p (c k) -> p c k", c=8)
            op(dve, lambda e: e.tensor_copy(out=Ere[:, :, 0:1], in_=P["c1"][:].unsqueeze(2)), r=[P["c1"]], w=[scA])
            op(dve, lambda e: e.tensor_copy(out=Eim[:, :, 0:1], in_=P["s1"][:].unsqueeze(2)), r=[P["s1"]], w=[scA])
            n_have = 1
            while n_have < 128:
                m = n_have
                br = Ere[:, :, m - 1:m].to_broadcast([128, 8, m])
                bi = Eim[:, :, m - 1:m].to_broadcast([128, 8, m])
                op(dve, lambda e: e.tensor_tensor(out=Tm[:, :, 0:m], in0=Ere[:, :, 0:m], in1=br, op=ALU.mult), r=[scA], w=[scB])
                op(dve, lambda e: e.tensor_tensor(out=Tm2[:, :, 0:m], in0=Eim[:, :, 0:m], in1=bi, op=ALU.mult), r=[scA], w=[scB])
                op(dve, lambda e: e.tensor_tensor(out=Tm[:, :, 0:m], in0=Tm[:, :, 0:m], in1=Tm2[:, :, 0:m], op=ALU.subtract), r=[scB], w=[scB])
                op(dve, lambda e: e.tensor_tensor(out=Tm2[:, :, 0:m], in0=Ere[:, :, 0:m], in1=bi, op=ALU.mult), r=[scA, scB], w=[scB])
                op(dve, lambda e: e.tensor_tensor(out=Eim[:, :, m:2 * m], in0=Eim[:, :, 0:m], in1=br, op=ALU.mult), r=[scA], w=[scA])
                op(dve, lambda e: e.tensor_tensor(out=Eim[:, :, m:2 * m], in0=Eim[:, :, m:2 * m], in1=Tm2[:, :, 0:m], op=ALU.add), r=[scA, scB], w=[scA])
                op(dve, lambda e: e.tensor_copy(out=Ere[:, :, m:2 * m], in_=Tm[:, :, 0:m]), r=[scB], w=[scA])
                n_have *= 2
            MP = scB[:, 0:1024].rearrange("p (c k) -> p c k", c=8)
            MN = scB[:, 1024:2048].rearrange("p (c k) -> p c k", c=8)
            kb.dma("sp", lambda e: e.dma_start(out=scC[:, 0:128], in_=c_kk), w=[scC])
            kkb = scC[:, 0:128].unsqueeze(1).to_broadcast([128, 8, 128])
            op(dve, lambda e: e.tensor_tensor(out=MP, in0=kkb, in1=P["ldr"][:].unsqueeze(2).to_broadcast([128, 8, 128]), op=ALU.mult), r=[scC, P["ldr"]], w=[scB])
            op(act, lambda e: e.activation(out=MN, in_=MP, func=AF.Exp, scale=-1.0), r=[scB], w=[scB])
            op(act, lambda e: e.activation(out=MP, in_=MP, func=AF.Exp), r=[scB], w=[scB])
            op(dve, lambda e: e.tensor_tensor(out=a1t[:, 0, :, :], in0=MP, in1=Ere, op=ALU.mult), r=[scA, scB], w=[a1t])
            op(dve, lambda e: e.tensor_tensor(out=a1t[:, 1, :, :], in0=MP, in1=Eim, op=ALU.mult), r=[scA, scB], w=[a1t])
            op(dve, lambda e: e.tensor_scalar(out=P["are"][:], in0=a1t[:, 0, :, 0], scalar1=-1.0, scalar2=None, op0=ALU.add), r=[a1t], w=[P["are"]])
            op(dve, lambda e: e.tensor_copy(out=P["aim"][:], in_=a1t[:, 1, :, 0]), r=[a1t], w=[P["aim"]])
            op(dve, lambda e: e.tensor_tensor(out=P["l2"][:], in0=lre, in1=lre, op=ALU.mult), r=[par], w=[P["l2"]])
            op(dve, lambda e: e.tensor_tensor(out=P["t0"][:], in0=lim, in1=lim, op=ALU.mult), r=[par], w=[P["t0"]])
            op(dve, lambda e: e.tensor_tensor(out=P["l2"][:], in0=P["l2"][:], in1=P["t0"][:], op=ALU.add), r=[P["l2"], P["t0"]], w=[P["l2"]])
            op(dve, lambda e: e.reciprocal(out=P["l2"][:], in_=P["l2"][:]), r=[P["l2"]], w=[P["l2"]])
            op(dve, lambda e: e.tensor_tensor(out=P["t0"][:], in0=P["are"][:], in1=lre, op=ALU.mult), r=[P["are"], par], w=[P["t0"]])
            op(dve, lambda e: e.tensor_tensor(out=P["t1"][:], in0=P["aim"][:], in1=lim, op=ALU.mult), r=[P["aim"], par], w=[P["t1"]])
            op(dve, lambda e: e.tensor_tensor(out=P["t0"][:], in0=P["t0"][:], in1=P["t1"][:], op=ALU.add), r=[P["t0"], P["t1"]], w=[P["t0"]])
            op(dve, lambda e: e.tensor_tensor(out=P["kre"][:], in0=P["t0"][:], in1=P["l2"][:], op=ALU.mult), r=[P["t0"], P["l2"]], w=[P["kre"]])
            op(dve, lambda e: e.tensor_tensor(out=P["t0"][:], in0=P["aim"][:], in1=lre, op=ALU.mult), r=[P["aim"], par], w=[P["t0"]])
            op(dve, lambda e: e.tensor_tensor(out=P["t1"][:], in0=P["are"][:], in1=lim, op=ALU.mult), r=[P["are"], par], w=[P["t1"]])
            op(dve, lambda e: e.tensor_tensor(out=P["t0"][:], in0=P["t0"][:], in1=P["t1"][:], op=ALU.subtract), r=[P["t0"], P["t1"]], w=[P["t0"]])
            op(dve, lambda e: e.tensor_tensor(out=P["kim"][:], in0=P["t0"][:], in1=P["l2"][:], op=ALU.mult), r=[P["t0"], P["l2"]], w=[P["kim"]])
            GR = scC[:, :].rearrange("p (c k) -> p c k", c=8)
            kreb = P["kre"][:].unsqueeze(2).to_broadcast([128, 8, 128])
            kimb = P["kim"][:].unsqueeze(2).to_broadcast([128, 8, 128])
            for part in range(2):
                if part == 0:
                    op(dve, lambda e: e.tensor_tensor(out=GR, in0=Ere, in1=kreb, op=ALU.mult), r=[scA, P["kre"]], w=[scC])
                    op(dve, lambda e: e.tensor_tensor(out=MP, in0=Eim, in1=kimb, op=ALU.mult), r=[scA, P["kim"]], w=[scB])
                    op(dve, lambda e: e.tensor_tensor(out=GR, in0=GR, in1=MP, op=ALU.add), r=[scC, scB], w=[scC])
                else:
                    op(dve, lambda e: e.tensor_tensor(out=GR, in0=Ere, in1=kimb, op=ALU.mult), r=[scA, P["kim"]], w=[scC])
                    op(dve, lambda e: e.tensor_tensor(out=MP, in0=Eim, in1=kreb, op=ALU.mult), r=[scA, P["kre"]], w=[scB])
                    op(dve, lambda e: e.tensor_tensor(out=GR, in0=GR, in1=MP, op=ALU.subtract), r=[scC, scB], w=[scC])
                op(dve, lambda e: e.tensor_tensor(out=GR, in0=GR, in1=MN, op=ALU.mult), r=[scC, scB], w=[scC])
                for half in range(2):
                    pp = ps1()

                    def f(e, half=half, pp=pp):
                        ins = None
                        for cc in range(4):
                            c = half * 4 + cc
                            ins = e.transpose(out=pp[:, cc * 128:(cc + 1) * 128], in_=scC[:, c * 128:(c + 1) * 128], identity=identF[:])
                        return ins
                    op(pe, f, r=[scC, identF], w=[pp])
                    op(act, lambda e, half=half, pp=pp, part=part: e.activation(out=ainvk[:, part, half * 512:(half + 1) * 512], in_=pp[:, 0:512], func=AF.Copy), r=[pp], w=[ainvk])

        def tile_geom(n):
            sample = n >= NTP
            M = 64 if sample else 128
            row0 = NTP * 128 + (n - NTP) * 64 if sample else n * 128
            return sample, M, row0

        def p1a(li, n):
            sample = n >= NTP
            M = 64 if sample else 128
            row0 = NTP * 128 + (n - NTP) * 64 if sample else n * 128
            seq = (n - NTP + 1) if sample else 0
            last_of_seq = sample or (n == NTP - 1)
            src_x = xin if li == 0 else X1
            dst_x = X1 if li == 0 else y_out
            kb.dma("sp", lambda e: e.dma_start(out=x_f[0:M, :], in_=src_x[row0:row0 + M, :]), w=[x_f])
            if n == 0:
                op(pool, lambda e: e.memset(hst[:], 0.0), w=[hst])
                op(pool, lambda e: e.memset(sret_f[:], 0.0), w=[sret_f])
                op(pool, lambda e: e.memset(sret_b[:], 0.0), w=[sret_b])
            if sample:
                s = n - NTP
                kb.dma("sp", lambda e: e.dma_start(out=hst[:], in_=st_ssm[li, s]), w=[hst])
                kb.dma("sp", lambda e: e.dma_start(out=sret_f[:], in_=st_ret[li, s]), w=[sret_f])
                op(act, lambda e: e.activation(out=sret_b[:].rearrange("p h e -> p (h e)"), in_=sret_f[:], func=AF.Copy), r=[sret_f], w=[sret_b])
                kc_b = bigb[:, :].rearrange("p (t c) -> p t c", t=4)
                kb.dma(pool, lambda e: e.dma_start(out=kc_b, in_=cache_k[li, s].rearrange("(t p) c -> p t c", p=128)), w=[bigb])
                kb.dma(pool, lambda e: e.dma_start(out=vr[:, 0:4, :], in_=cache_v[li, s].rearrange("(t p) c -> p t c", p=128)), w=[vr])
                for t_ in range(4):
                    pb = psb()

                    def f(e, t_=t_, pb=pb):
                        ins = None
                        for hp in range(4):
                            ins = e.transpose(out=pb[:, hp * 128:(hp + 1) * 128], in_=kc_b[:, t_, hp * 128:(hp + 1) * 128], identity=ident[:])
                        return ins
                    op(pe, f, r=[bigb, ident], w=[pb])
                    op(dve, lambda e, t_=t_, pb=pb: e.tensor_copy(out=kt2[:, :, t_ * 128:(t_ + 1) * 128], in_=pb[:, 0:512].rearrange("p (h m) -> p h m", h=4)), r=[pb], w=[kt2])
            if sample:
                wslot, ws, boff = 4, 0, 0
                Wn = 576
            else:
                wslot = 4 + (n % 5)
                if n > 0 and n % 5 == 0:
                    op(pool, lambda e: e.tensor_copy(out=kt2[:, :, 0:512], in_=kt2[:, :, 640:1152]), r=[kt2], w=[kt2])
                    op(pool, lambda e: e.tensor_copy(out=vr[:, 0:4, :], in_=vr[:, 5:9, :]), r=[vr], w=[vr])
                nvalid = min(n, 4)
                ws = wslot - nvalid
                boff = (4 - nvalid) * 128
                Wn = (nvalid + 1) * 128

            op(act, lambda e: e.activation(out=xb[0:M, :], in_=x_f[0:M, :], func=AF.Copy), r=[x_f], w=[xb])
            transposes(xb, 8, M, xT, xT[:, :, 0:M])
            yield

            want_kv = sample or (n >= NTP - 4)
            kv_row = (512 + (n - NTP) * 64) if sample else (n - (NTP - 4)) * 128
            b4 = None
            for bi, (c0, N) in enumerate(IN_BLOCKS if "inproj" in STAGES else []):
                wt = wget(("w_in", li, 0, 8, c0, N))
                pp = ps1()

                def f(e, wt=wt, pp=pp, N=N):
                    ins = None
                    for k in range(8):
                        ins = e.matmul(pp[0:M, 0:N], lhsT=xT[:, k, 0:M], rhs=wt[:, k, 0:N], start=(k == 0), stop=(k == 7))
                    return ins
                op(pe, f, r=[xT, wt], w=[pp])
                if bi == 0:
                    op(act, lambda e, pp=pp: e.activation(out=u_f[0:M, :], in_=pp[0:M, 0:256], func=AF.Copy), r=[pp], w=[u_f])
                    op(act, lambda e, pp=pp: e.activation(out=u_b[0:M, :], in_=pp[0:M, 0:256], func=AF.Copy), r=[pp], w=[u_b])
                elif bi == 1:
                    op(act, lambda e, pp=pp: e.activation(out=qa_b[0:M, :], in_=pp[0:M, 0:512], func=AF.Copy), r=[pp], w=[qa_b])
                elif bi == 2:
                    op(act, lambda e, pp=pp: e.activation(out=ka_b[0:M, :], in_=pp[0:M, 0:512], func=AF.Copy), r=[pp], w=[ka_b])
                    if want_kv:
                        op(act, lambda e, pp=pp: e.activation(out=scC[0:M, 0:512], in_=pp[0:M, 0:512], func=AF.Copy), r=[pp], w=[scC])
                elif bi == 3:
                    op(act, lambda e, pp=pp: e.activation(out=vr[0:M, wslot, :], in_=pp[0:M, 0:512], func=AF.Copy), r=[pp], w=[vr])
                    if want_kv:
                        op(act, lambda e, pp=pp: e.activation(out=scC[0:M, 512:1024], in_=pp[0:M, 0:512], func=AF.Copy), r=[pp], w=[scC])
                        kb.dma("sp", lambda e: e.dma_start(out=k_out[li, kv_row:kv_row + M, :], in_=scC[0:M, 0:512]), r=[scC], is_out=True)
                        kb.dma("sp", lambda e: e.dma_start(out=v_out[li, kv_row:kv_row + M, :], in_=scC[0:M, 512:1024]), r=[scC], is_out=True)
                elif bi == 4:
                    b4 = pp
                    rope_and_ret_prep(b4, M, n)
                elif bi == 5:
                    op(act, lambda e, pp=pp: e.activation(out=vr_b[0:M, :], in_=pp[0:M, 0:256], func=AF.Copy), r=[pp], w=[vr_b])
                    op(act, lambda e, pp=pp: e.activation(out=sg_f[0:M, :], in_=pp[0:M, 256:512], func=AF.Silu), r=[pp], w=[sg_f])
                else:
                    g = (bi - 6) // 2
                    hh = (bi - 6) % 2
                    op(act, lambda e, pp=pp, g=g, hh=hh: e.activation(out=gates_b[0:M, g, hh * 512:(hh + 1) * 512], in_=pp[0:M, 0:512], func=AF.Sigmoid), r=[pp], w=[gates_b])
                yield

            tap = (li, n) == DBG_TILE
            if tap:
                dump("u_f", u_f[:, :], [u_f])
                dump("gates", gates_b[:, :, :].rearrange("p a b -> p (a b)"), [gates_b])
            if "s5" in STAGES:
                s5_stage(M)
                yield
            if "attn" in STAGES:
                yield from attn_stage(M, wslot, ws, boff, Wn)
            if "ret" in STAGES:
                ret_stage(M, sample)
                yield
            if tap:
                dump("yssm", yssm_b[:, :], [yssm_b])
                dump("yatt", yatt_b[:, :], [yatt_b])
                dump("yret", yret_b[:, :], [yret_b])
            if last_of_seq:
                kb.dma("sp", lambda e: e.dma_start(out=ssm_out[li, seq], in_=hst[:]), r=[hst], is_out=True)
                kb.dma("sp", lambda e: e.dma_start(out=ret_out[li, seq], in_=sret_f[:]), r=[sret_f], is_out=True)
            if "merge" in STAGES:
                yield from merge_stage(li, M, X1P[n % 2])

        def p2_tail(li, n):
            sample, M, row0 = tile_geom(n)
            dst_x = X1 if li == 0 else y_out
            XP = X1P[n % 2]
            peer_tail(li, M, XP)
            if "ple" in STAGES:
                ple_stage(li, M, row0, XP)
            kb.dma("sp", lambda e: e.dma_start(out=dst_x[row0:row0 + M, :], in_=XP[0:M, :]), r=[XP], is_out=True)

        def rope_and_ret_prep(b4, M, n):
            v4 = b4[0:M, 0:512].rearrange("p (g t f) -> p g t f", g=8, t=2)
            x1 = v4[:, :, 0, :]
            x2 = v4[:, :, 1, :]
            kb.dma("sp", lambda e: e.dma_start(out=cs_t[0:M, 0, :], in_=c_cos[0:M, n * 32:(n + 1) * 32]), w=[cs_t])
            kb.dma("sp", lambda e: e.dma_start(out=cs_t[0:M, 1, :], in_=c_sin[0:M, n * 32:(n + 1) * 32]), w=[cs_t])
            cosb = cs_t[0:M, 0, :].unsqueeze(1).to_broadcast([M, 8, 32])
            sinb = cs_t[0:M, 1, :].unsqueeze(1).to_broadcast([M, 8, 32])
            R = scC[0:M, 0:512].rearrange("p (g t f) -> p g t f", g=8, t=2)
            T = scC[0:M, 512:1024].rearrange("p (g t f) -> p g t f", g=8, t=2)
            op(dve, lambda e: e.tensor_tensor(out=R[:, :, 0, :], in0=x1, in1=cosb, op=ALU.mult), r=[b4, cs_t], w=[scC])
            op(dve, lambda e: e.tensor_tensor(out=T[:, :, 0, :], in0=x2, in1=sinb, op=ALU.mult), r=[b4, cs_t], w=[scC])
            op(dve, lambda e: e.tensor_tensor(out=R[:, :, 1, :], in0=x1, in1=sinb, op=ALU.mult), r=[b4, cs_t], w=[scC])
            op(dve, lambda e: e.tensor_tensor(out=T[:, :, 1, :], in0=x2, in1=cosb, op=ALU.mult), r=[b4, cs_t], w=[scC])
            op(dve, lambda e: e.tensor_tensor(out=R[:, :, 0, :], in0=R[:, :, 0, :], in1=T[:, :, 0, :], op=ALU.subtract), r=[scC], w=[scC])
            op(dve, lambda e: e.tensor_tensor(out=R[:, :, 1, :], in0=R[:, :, 1, :], in1=T[:, :, 1, :], op=ALU.add), r=[scC], w=[scC])
            op(act, lambda e: e.activation(out=qk_b[0:M, :], in_=scC[0:M, 0:512], func=AF.Copy), r=[scC], w=[qk_b])
            kwc = 8 if M == 64 else 4
            qv = scC[0:M, 0:256].rearrange("p (h d) -> p h d", h=4)
            kv = scC[0:M, 256:512].rearrange("p (h d) -> p h d", h=4)
            op(dve, lambda e: e.tensor_tensor(out=qs_b[0:M, :].rearrange("p (h d) -> p h d", h=4), in0=qv, in1=cret[0:M, 0:4].unsqueeze(2).to_broadcast([M, 4, 64]), op=ALU.mult), r=[scC, cret], w=[qs_b])
            op(dve, lambda e: e.tensor_tensor(out=kk_b[0:M, :].rearrange("p (h d) -> p h d", h=4), in0=kv, in1=cret[0:M, kwc:kwc + 4].unsqueeze(2).to_broadcast([M, 4, 64]), op=ALU.mult), r=[scC, cret], w=[kk_b])

        def s5_stage(M):
            transposes(u_b, 2, M, uT, uT[:, :, 0:M])
            bu = [ps2(), ps2()]
            for half in range(2):
                def f(e, half=half):
                    ins = None
                    for cb in range(2):
                        ins = e.matmul(bu[half][0:M, cb * 512:(cb + 1) * 512], lhsT=uT[:, cb, 0:M], rhs=bbd_b[:, cb, half, :], start=True, stop=True)
                    return ins
                op(pe, f, r=[uT, bbd_b], w=[bu[half]])
            t1 = scA[0:M, 0:1024]
            t2 = scA[0:M, 1024:2048]
            op(dve, lambda e: e.tensor_tensor(out=t1, in0=bu[0][0:M, :], in1=ainvk[0:M, 0, :], op=ALU.mult), r=[bu[0], ainvk], w=[scA])
            op(dve, lambda e: e.tensor_tensor(out=t2, in0=bu[1][0:M, :], in1=ainvk[0:M, 1, :], op=ALU.mult), r=[bu[1], ainvk], w=[scA])
            op(dve, lambda e: e.tensor_tensor(out=bigb[0:M, 0:1024], in0=t1, in1=t2, op=ALU.subtract), r=[scA], w=[bigb])
            op(dve, lambda e: e.tensor_tensor(out=t1, in0=bu[0][0:M, :], in1=ainvk[0:M, 1, :], op=ALU.mult), r=[bu[0], ainvk, bigb], w=[scA])
            op(dve, lambda e: e.tensor_tensor(out=t2, in0=bu[1][0:M, :], in1=ainvk[0:M, 0, :], op=ALU.mult), r=[bu[1], ainvk], w=[scA])
            op(dve, lambda e: e.tensor_tensor(out=bigb[0:M, 1024:2048], in0=t1, in1=t2, op=ALU.add), r=[scA], w=[bigb])
            cs = [ps2(), ps2()]
            for half in range(2):
                def f(e, half=half):
                    ins = None
                    for c in range(8):
                        ins = e.matmul(cs[half][:, c * 128:c * 128 + M], lhsT=bigb[0:M, half * 1024 + c * 128:half * 1024 + (c + 1) * 128], rhs=l2t[0:M, 0:M], start=True, stop=True)
                    return ins
                op(pe, f, r=[bigb, l2t], w=[cs[half]])
            tre = scA[:, 0:1024].rearrange("p (c i) -> p c i", c=8)[:, :, 0:M]
            tim = scA[:, 1024:2048].rearrange("p (c i) -> p c i", c=8)[:, :, 0:M]
            hre = scB[:, 0:1024].rearrange("p (c i) -> p c i", c=8)[:, :, 0:M]
            him = scB[:, 1024:2048].rearrange("p (c i) -> p c i", c=8)[:, :, 0:M]
            tmp = scC[:, 0:1024].rearrange("p (c i) -> p c i", c=8)[:, :, 0:M]
            csv = [cs[h_][:, :].rearrange("p (c i) -> p c i", c=8)[:, :, 0:M] for h_ in range(2)]
            op(dve, lambda e: e.tensor_tensor(out=tre, in0=csv[0], in1=hst[:, 0:8].unsqueeze(2).to_broadcast([128, 8, M]), op=ALU.add), r=[cs[0], hst], w=[scA])
            op(dve, lambda e: e.tensor_tensor(out=tim, in0=csv[1], in1=hst[:, 8:16].unsqueeze(2).to_broadcast([128, 8, M]), op=ALU.add), r=[cs[1], hst], w=[scA])
            a_re = a1t[:, 0, :, 0:M]
            a_im = a1t[:, 1, :, 0:M]
            op(dve, lambda e: e.tensor_tensor(out=hre, in0=tre, in1=a_re, op=ALU.mult), r=[scA, a1t], w=[scB])
            op(dve, lambda e: e.tensor_tensor(out=tmp, in0=tim, in1=a_im, op=ALU.mult), r=[scA, a1t], w=[scC])
            op(dve, lambda e: e.tensor_tensor(out=hre, in0=hre, in1=tmp, op=ALU.subtract), r=[scB, scC], w=[scB])
            op(dve, lambda e: e.tensor_tensor(out=him, in0=tre, in1=a_im, op=ALU.mult), r=[scA, a1t], w=[scB])
            op(dve, lambda e: e.tensor_tensor(out=tmp, in0=tim, in1=a_re, op=ALU.mult), r=[scA, a1t, scB], w=[scC])
            op(dve, lambda e: e.tensor_tensor(out=him, in0=him, in1=tmp, op=ALU.add), r=[scB, scC], w=[scB])
            hT_b = bigb[:, :].rearrange("p (c i) -> p c i", c=16)
            op(act, lambda e: e.activation(out=hT_b[:, 0:8, 0:M], in_=hre, func=AF.Copy), r=[scB], w=[bigb])
            op(act, lambda e: e.activation(out=hT_b[:, 8:16, 0:M], in_=him, func=AF.Copy, scale=-1.0), r=[scB], w=[bigb])
            op(dve, lambda e: e.tensor_copy(out=hst[:, 0:8], in_=hre[:, :, M - 1]), r=[scB], w=[hst])
            op(dve, lambda e: e.tensor_copy(out=hst[:, 8:16], in_=him[:, :, M - 1]), r=[scB], w=[hst])
            yp = ps1()

            def f(e):
                ins = None
                for c in range(8):
                    e.matmul(yp[0:M, 32 * c:32 * c + 32], lhsT=hT_b[:, c, 0:M], rhs=cm_b[:, c, 0, :], start=True, stop=False)
                    ins = e.matmul(yp[0:M, 32 * c:32 * c + 32], lhsT=hT_b[:, 8 + c, 0:M], rhs=cm_b[:, c, 1, :], start=False, stop=True)
                return ins
            op(pe, f, r=[bigb, cm_b], w=[yp])
            yf = scA[0:M, 0:256]
            zf = scA[0:M, 256:512]
            tz = scA[0:M, 512:768]
            op(dve, lambda e: e.tensor_tensor(out=yf, in0=u_f[0:M, :], in1=rows_s[0:M, R_DSK:R_DSK + 256], op=ALU.mult), r=[u_f, rows_s], w=[scA])
            op(dve, lambda e: e.tensor_tensor(out=yf, in0=yf, in1=yp[0:M, 0:256], op=ALU.add), r=[scA, yp], w=[scA])
            gelu_tanh(yf, zf, tz, [scA], [scA], scA)
            zb = u_b
            op(act, lambda e: e.activation(out=zb[0:M, :], in_=zf, func=AF.Copy), r=[scA, uT], w=[zb])
            transposes(zb, 2, M, uT, uT[:, :, 0:M])
            gp = ps1()

            def f2(e):
                ins = None
                for k in range(2):
                    ins = e.matmul(gp[0:M, 0:256], lhsT=uT[:, k, 0:M], rhs=wglu_b[:, k, :], start=(k == 0), stop=(k == 1))
                return ins
            op(pe, f2, r=[uT, wglu_b], w=[gp])
            op(dve, lambda e: e.tensor_tensor(out=tz, in0=gp[0:M, 0:256], in1=rows_s[0:M, R_BGLU:R_BGLU + 256], op=ALU.add), r=[gp, rows_s], w=[scA])
            op(act, lambda e: e.activation(out=tz, in_=tz, func=AF.Sigmoid), r=[scA], w=[scA])
            op(dve, lambda e: e.tensor_tensor(out=yssm_b[0:M, :], in0=zf, in1=tz, op=ALU.mult), r=[scA], w=[yssm_b])

        def attn_stage(M, wslot, ws, boff, Wn):
            transposes(qa_b, 4, M, qT2, qT2[:, :, 0:M])
            transposes(ka_b, 4, M, kt2, kt2[:, :, wslot * 128:wslot * 128 + M])
            c0 = ws * 128
            ops_ = ps_acc()
            S_sb = scA[0:M, 0:Wn]
            nblk = (Wn + 127) // 128
            for h in range(8):
                sp2 = ps2()

                def f(e, h=h, sp2=sp2):
                    n1 = min(512, Wn)
                    po = (h % 2) * 64
                    hp = h // 2
                    ins = e.matmul(sp2[0:M, 0:n1], lhsT=qT2[po:po + 64, hp, 0:M], rhs=kt2[po:po + 64, hp, c0:c0 + n1], start=True, stop=True)
                    if Wn > 512:
                        ins = e.matmul(sp2[0:M, 512:Wn], lhsT=qT2[po:po + 64, hp, 0:M], rhs=kt2[po:po + 64, hp, c0 + 512:c0 + Wn], start=True, stop=True)
                    return ins
                op(pe, f, r=[qT2, kt2], w=[sp2])
                op(dve, lambda e, h=h, sp2=sp2: e.scalar_tensor_tensor(out=S_sb, in0=sp2[0:M, 0:Wn], scalar=0.125, in1=bias_b[0:M, h, boff:boff + Wn], op0=ALU.mult, op1=ALU.add), r=[sp2, bias_b], w=[scA])
                op(dve, lambda e: e.reduce_max(out=sm[0:M, 16:17], in_=S_sb, axis=AX.X), r=[scA], w=[sm])
                op(dve, lambda e: e.tensor_scalar(out=sm[0:M, 17:18], in0=sm[0:M, 16:17], scalar1=-1.0, scalar2=None, op0=ALU.mult), r=[sm], w=[sm])
                op(act, lambda e, h=h: e.activation(out=pb_att[0:M, 0:Wn], in_=S_sb, func=AF.Exp, bias=sm[0:M, 17:18], accum_out=sm[0:M, 24 + h:25 + h]), r=[scA, sm], w=[pb_att, sm])
                pbt = psb()

                def f(e, pbt=pbt):
                    ins = None
                    for b in range(nblk):
                        bw = min(128, Wn - b * 128)
                        ins = e.transpose(out=pbt[0:bw, b * 128:b * 128 + M], in_=pb_att[0:M, b * 128:b * 128 + bw], identity=ident[0:M, 0:M])
                    return ins
                op(pe, f, r=[pb_att, ident], w=[pbt])
                lastw = Wn - (nblk - 1) * 128
                if lastw == 128:
                    op(dve, lambda e, pbt=pbt: e.tensor_copy(out=pT[:, 0:nblk, 0:M], in_=pbt[:, 0:nblk * 128].rearrange("p (b m) -> p b m", m=128)[:, :, 0:M]), r=[pbt], w=[pT])
                else:
                    op(dve, lambda e, pbt=pbt: e.tensor_copy(out=pT[:, 0:nblk - 1, 0:M], in_=pbt[:, 0:(nblk - 1) * 128].rearrange("p (b m) -> p b m", m=128)[:, :, 0:M]), r=[pbt], w=[pT])
                    op(dve, lambda e, pbt=pbt: e.tensor_copy(out=pT[0:lastw, nblk - 1, 0:M], in_=pbt[0:lastw, (nblk - 1) * 128:(nblk - 1) * 128 + M]), r=[pbt], w=[pT])

                def f(e, h=h):
                    ins = None
                    for b in range(nblk):
                        bw = min(128, Wn - b * 128)
                        ins = e.matmul(ops_[0:M, h * 64:(h + 1) * 64], lhsT=pT[0:bw, b, 0:M], rhs=vr[0:bw, ws + b, h * 64:(h + 1) * 64], start=(b == 0), stop=(b == nblk - 1))
                    return ins
                op(pe, f, r=[pT, vr], w=[ops_])
                yield
            op(dve, lambda e: e.reciprocal(out=sm[0:M, 32:40], in_=sm[0:M, 24:32]), r=[sm], w=[sm])
            op(dve, lambda e: e.tensor_tensor(out=yatt_b[0:M, :].rearrange("p (h d) -> p h d", h=8), in0=ops_[0:M, 0:512].rearrange("p (h d) -> p h d", h=8), in1=sm[0:M, 32:40].unsqueeze(2).to_broadcast([M, 8, 64]), op=ALU.mult), r=[ops_, sm], w=[yatt_b])

        def ret_stage(M, sample):
            pb = psb()

            def f(e):
                ins = None
                for g in range(8):
                    ins = e.transpose(out=pb[0:64, g * 128:g * 128 + M], in_=qk_b[0:M, g * 64:(g + 1) * 64], identity=ident[0:M, 0:M])
                return ins
            op(pe, f, r=[qk_b, ident], w=[pb])
            op(dve, lambda e: e.tensor_copy(out=qkT[:, :, 0:M], in_=pb[0:64, :].rearrange("p (h m) -> p h m", h=8)[:, :, 0:M]), r=[pb], w=[qkT])
            pb2 = psb()

            def f(e):
                ins = None
                for g in range(4):
                    ins = e.transpose(out=pb2[0:64, g * 128:g * 128 + M], in_=qs_b[0:M, g * 64:(g + 1) * 64], identity=ident[0:M, 0:M])
                return ins
            op(pe, f, r=[qs_b, ident], w=[pb2])
            op(dve, lambda e: e.tensor_copy(out=qsT[:, :, 0:M], in_=pb2[0:64, 0:512].rearrange("p (h m) -> p h m", h=4)[:, :, 0:M]), r=[pb2], w=[qsT])
            scp = ps1()

            def f(e):
                ins = None
                for h in range(4):
                    ins = e.matmul(scp[0:M, h * 128:h * 128 + M], lhsT=qkT[:, 4 + h, 0:M], rhs=qkT[:, h, 0:M], start=True, stop=True)
                return ins
            op(pe, f, r=[qkT], w=[scp])
            op(dve, lambda e: e.tensor_tensor(out=scT_b[0:M, :, 0:M], in0=scp[0:M, 0:512].rearrange("p (h i) -> p h i", h=4)[:, :, 0:M], in1=dtt[0:M, :, 0:M], op=ALU.mult), r=[scp, dtt], w=[scT_b])
            opp = ps1()

            def f(e):
                ins = None
                for h in range(4):
                    e.matmul(opp[0:M, h * 64:(h + 1) * 64], lhsT=scT_b[0:M, h, 0:M], rhs=vr_b[0:M, h * 64:(h + 1) * 64], start=True, stop=False)
                    ins = e.matmul(opp[0:M, h * 64:(h + 1) * 64], lhsT=qsT[:, h, 0:M], rhs=sret_b[:, h, :], start=False, stop=True)
                return ins
            op(pe, f, r=[scT_b, vr_b, qsT, sret_b], w=[opp])
            cp = ps1()

            def f(e):
                ins = None
                for h in range(4):
                    ins = e.matmul(cp[0:64, h * 64:(h + 1) * 64], lhsT=kk_b[0:M, h * 64:(h + 1) * 64], rhs=vr_b[0:M, h * 64:(h + 1) * 64], start=True, stop=True)
                return ins
            op(pe, f, r=[kk_b, vr_b], w=[cp])
            di = 1 if sample else 0
            op(dve, lambda e: e.tensor_tensor(out=sret_f[:], in0=sret_f[:], in1=dect[:, di, :], op=ALU.mult), r=[sret_f, dect], w=[sret_f])
            op(dve, lambda e: e.tensor_tensor(out=sret_f[:], in0=sret_f[:], in1=cp[0:64, 0:256], op=ALU.add), r=[sret_f, cp], w=[sret_f])
            op(act, lambda e: e.activation(out=sret_b[:].rearrange("p h e -> p (h e)"), in_=sret_f[:], func=AF.Copy), r=[sret_f], w=[sret_b])
            o3 = opp[0:M, 0:256].rearrange("p (h e) -> p h e", h=4)
            oc = scA[0:M, 0:256].rearrange("p (h e) -> p h e", h=4)
            sq = scA[0:M, 256:512].rearrange("p (h e) -> p h e", h=4)
            op(dve, lambda e: e.tensor_reduce(out=sm[0:M, 40:44], in_=o3, axis=AX.X, op=ALU.add), r=[opp], w=[sm])
            op(dve, lambda e: e.tensor_scalar(out=sm[0:M, 44:48], in0=sm[0:M, 40:44], scalar1=-1.0 / 64, scalar2=None, op0=ALU.mult), r=[sm], w=[sm])
            op(dve, lambda e: e.tensor_tensor(out=oc, in0=o3, in1=sm[0:M, 44:48].unsqueeze(2).to_broadcast([M, 4, 64]), op=ALU.add), r=[opp, sm], w=[scA])
            op(dve, lambda e: e.tensor_tensor(out=sq, in0=oc, in1=oc, op=ALU.mult), r=[scA], w=[scA])
            op(dve, lambda e: e.tensor_reduce(out=sm[0:M, 48:52], in_=sq, axis=AX.X, op=ALU.add), r=[scA], w=[sm])
            op(dve, lambda e: e.tensor_scalar(out=sm[0:M, 48:52], in0=sm[0:M, 48:52], scalar1=1.0 / 64, scalar2=LN_EPS, op0=ALU.mult, op1=ALU.add), r=[sm], w=[sm])
            op(act, lambda e: e.activation(out=sm[0:M, 52:56], in_=sm[0:M, 48:52], func=AF.Sqrt), r=[sm], w=[sm])
            op(dve, lambda e: e.reciprocal(out=sm[0:M, 56:60], in_=sm[0:M, 52:56]), r=[sm], w=[sm])
            op(dve, lambda e: e.tensor_tensor(out=oc, in0=oc, in1=sm[0:M, 56:60].unsqueeze(2).to_broadcast([M, 4, 64]), op=ALU.mult), r=[scA, sm], w=[scA])
            op(dve, lambda e: e.tensor_tensor(out=scA[0:M, 0:256], in0=scA[0:M, 0:256], in1=rows_s[0:M, R_GNG:R_GNG + 256], op=ALU.mult), r=[scA, rows_s], w=[scA])
            op(dve, lambda e: e.tensor_tensor(out=yret_b[0:M, :], in0=scA[0:M, 0:256], in1=sg_f[0:M, :], op=ALU.mult), r=[scA, sg_f], w=[yret_b])

        def merge_stage(li, M, XP):
            transposes(yssm_b, 2, M, yT, yT[:, 0:2, 0:M])
            transposes(yatt_b, 4, M, yT, yT[:, 2:6, 0:M])
            transposes(yret_b, 2, M, yT, yT[:, 6:8, 0:M])
            merged = scB[0:M, 0:1024]
            tmpm = scB[0:M, 1024:2048]
            for bi, (name, K, k0) in enumerate((("w_br_ssm", 2, 0), ("w_br_att", 4, 2), ("w_br_ret", 2, 6))):
                for hh in range(2):
                    wt = wget((name, li, 0, K, hh * 512, 512))
                    pp = ps1()

                    def f(e, wt=wt, pp=pp, K=K, k0=k0):
                        ins = None
                        for k in range(K):
                            ins = e.matmul(pp[0:M, 0:512], lhsT=yT[:, k0 + k, 0:M], rhs=wt[:, k, 0:512], start=(k == 0), stop=(k == K - 1))
                        return ins
                    op(pe, f, r=[yT, wt], w=[pp])
                    if bi == 0:
                        op(dve, lambda e, pp=pp, hh=hh: e.tensor_tensor(out=merged[:, hh * 512:(hh + 1) * 512], in0=pp[0:M, 0:512], in1=gates_b[0:M, 0, hh * 512:(hh + 1) * 512], op=ALU.mult), r=[pp, gates_b], w=[scB])
                    else:
                        op(dve, lambda e, pp=pp, hh=hh, bi=bi: e.tensor_tensor(out=tmpm[:, hh * 512:(hh + 1) * 512], in0=pp[0:M, 0:512], in1=gates_b[0:M, bi, hh * 512:(hh + 1) * 512], op=ALU.mult), r=[pp, gates_b], w=[scB])
                        op(dve, lambda e, hh=hh: e.tensor_tensor(out=merged[:, hh * 512:(hh + 1) * 512], in0=merged[:, hh * 512:(hh + 1) * 512], in1=tmpm[:, hh * 512:(hh + 1) * 512], op=ALU.add), r=[scB], w=[scB])
                    yield
            op(act, lambda e: e.activation(out=xb[0:M, :], in_=merged, func=AF.Copy), r=[scB], w=[xb])
            transposes(xb, 8, M, xT, xT[:, :, 0:M])
            for hh in range(2):
                wt = wget(("w_o", li, 0, 8, hh * 512, 512))
                pp = ps1()

                def f(e, wt=wt, pp=pp):
                    ins = None
                    for k in range(8):
                        ins = e.matmul(pp[0:M, 0:512], lhsT=xT[:, k, 0:M], rhs=wt[:, k, 0:512], start=(k == 0), stop=(k == 7))
                    return ins
                op(pe, f, r=[xT, wt], w=[pp])
                op(dve, lambda e, pp=pp, hh=hh: e.scalar_tensor_tensor(out=XP[0:M, hh * 512:(hh + 1) * 512], in0=x_f[0:M, hh * 512:(hh + 1) * 512], scalar=ALPHA, in1=pp[0:M, 0:512], op0=ALU.mult, op1=ALU.add), r=[x_f, pp], w=[XP])
                yield

        def peer_front(li, M, XP, EI, GT):
            layer_norm(XP, 0, XP, M)
            op(act, lambda e: e.activation(out=xb[0:M, :], in_=XP[0:M, :], func=AF.Copy), r=[XP], w=[xb])
            transposes(xb, 8, M, xT, xT[:, :, 0:M])
            qTp = bigb[:, :].rearrange("p (g m) -> p g m", g=16)
            for cb in range(4):
                wt = wget(("w_q", li, 0, 8, cb * 512, 512))
                pp = ps1()

                def f(e, wt=wt, pp=pp):
                    ins = None
                    for g in range(4):
                        for k in range(8):
                            ins = e.matmul(pp[:, g * 128:g * 128 + M], lhsT=wt[:, k, g * 128:(g + 1) * 128], rhs=xT[:, k, 0:M], start=(k == 0), stop=(k == 7))
                    return ins
                op(pe, f, r=[xT, wt], w=[pp])
                op(act, lambda e, pp=pp, cb=cb: e.activation(out=qTp[:, cb * 4:(cb + 1) * 4, 0:M], in_=pp[:, 0:512].rearrange("p (g m) -> p g m", g=4)[:, :, 0:M], func=AF.Copy), r=[pp], w=[bigb])
            s_f = scA
            for half in range(2):
                sp2 = ps2()

                def f(e, half=half, sp2=sp2):
                    ins = None
                    for gg in range(8):
                        g = half * 8 + gg
                        ins = e.matmul(sp2[0:M, gg * 128:(gg + 1) * 128], lhsT=qTp[:, g, 0:M], rhs=keys_b[:, g, :], start=True, stop=True)
                    return ins
                op(pe, f, r=[bigb, keys_b], w=[sp2])
                op(act, lambda e, half=half, sp2=sp2: e.activation(out=s_f[0:M, half * 1024:(half + 1) * 1024], in_=sp2[0:M, :], func=AF.Copy), r=[sp2], w=[scA])
            svs = [s_f[0:M, g * 128:(g + 1) * 128] for g in range(16)]
            for g in range(16):
                op(dve, lambda e, g=g: e.max(out=topa[0:M, g, 0:8], in_=svs[g]), r=[scA.lane(g)], w=[topa.lane(g)])
            for g in range(16):
                op(dve, lambda e, g=g: e.max_index(out=topi[0:M, g, 0:8], in_max=topa[0:M, g, 0:8], in_values=svs[g]), r=[scA.lane(g), topa.lane(g)], w=[topi.lane(g)])
            for g in range(16):
                op(dve, lambda e, g=g: e.match_replace(out=svs[g], in_to_replace=topa[0:M, g, 0:8], in_values=svs[g], imm_value=-1e30), r=[scA.lane(g), topa.lane(g)], w=[scA.lane(g)])
            for g in range(16):
                op(dve, lambda e, g=g: e.max(out=topa[0:M, g, 8:16], in_=svs[g]), r=[scA.lane(g)], w=[topa.lane(g)])
            for g in range(16):
                op(dve, lambda e, g=g: e.max_index(out=topi[0:M, g, 8:16], in_max=topa[0:M, g, 8:16], in_values=svs[g]), r=[scA.lane(g), topa.lane(g)], w=[topi.lane(g)])
            cand = scB[0:M, :].rearrange("p (h a b) -> p h a b", h=8, a=16)
            ta = topa[0:M, :, :].rearrange("p (h s) k -> p h s k", s=2)
            op(dve, lambda e: e.tensor_tensor(out=cand, in0=ta[:, :, 0, :].unsqueeze(3).to_broadcast([M, 8, 16, 16]), in1=ta[:, :, 1, :].unsqueeze(2).to_broadcast([M, 8, 16, 16]), op=ALU.add), r=[topa], w=[scB])
            cvs = [scB[0:M, h * 256:(h + 1) * 256] for h in range(8)]
            for h in range(8):
                op(dve, lambda e, h=h: e.max(out=top2[0:M, h, 0:8], in_=cvs[h]), r=[scB.lane(h)], w=[top2.lane(h)])
            for h in range(8):
                op(dve, lambda e, h=h: e.max_index(out=pos2[0:M, h, 0:8], in_max=top2[0:M, h, 0:8], in_values=cvs[h]), r=[scB.lane(h), top2.lane(h)], w=[pos2.lane(h)])
            for h in range(8):
                op(dve, lambda e, h=h: e.match_replace(out=cvs[h], in_to_replace=top2[0:M, h, 0:8], in_values=cvs[h], imm_value=-1e30), r=[scB.lane(h), top2.lane(h)], w=[scB.lane(h)])
            for h in range(8):
                op(dve, lambda e, h=h: e.max(out=top2[0:M, h, 8:16], in_=cvs[h]), r=[scB.lane(h)], w=[top2.lane(h)])
            for h in range(8):
                op(dve, lambda e, h=h: e.max_index(out=pos2[0:M, h, 8:16], in_max=top2[0:M, h, 8:16], in_values=cvs[h]), r=[scB.lane(h), top2.lane(h)], w=[pos2.lane(h)])
            posf = scC[0:M, 0:128].rearrange("p (h j) -> p h j", h=8)
            k1f = scC[0:M, 128:256].rearrange("p (h j) -> p h j", h=8)
            k2f = scC[0:M, 256:384].rearrange("p (h j) -> p h j", h=8)
            iaf = scC[0:M, 384:640].rearrange("p (g k) -> p g k", g=16)
            e1 = scC[0:M, 640:768].rearrange("p (h j) -> p h j", h=8)
            e2 = scC[0:M, 768:896].rearrange("p (h j) -> p h j", h=8)
            op(dve, lambda e: e.tensor_copy(out=posf, in_=pos2[0:M, :, :]), r=[pos2], w=[scC])
            op(dve, lambda e: e.tensor_copy(out=iaf, in_=topi[0:M, :, :]), r=[topi], w=[scC])
            op(dve, lambda e: e.tensor_scalar(out=k1f, in0=posf, scalar1=1.0 / 16, scalar2=None, op0=ALU.mult), r=[scC], w=[scC])
            op(dve, lambda e: e.tensor_copy(out=pos2[0:M, :, :], in_=k1f), r=[scC], w=[pos2])
            op(dve, lambda e: e.tensor_copy(out=k1f, in_=pos2[0:M, :, :]), r=[pos2], w=[scC])
            op(dve, lambda e: e.scalar_tensor_tensor(out=k2f, in0=k1f, scalar=16.0, in1=posf, op0=ALU.mult, op1=ALU.is_gt), r=[scC], w=[scC])
            op(dve, lambda e: e.tensor_tensor(out=k1f, in0=k1f, in1=k2f, op=ALU.subtract), r=[scC], w=[scC])
            op(dve, lambda e: e.scalar_tensor_tensor(out=k2f, in0=k1f, scalar=-16.0, in1=posf, op0=ALU.mult, op1=ALU.add), r=[scC], w=[scC])
            oh = scB[0:M, :].rearrange("p (h j k) -> p h j k", h=8, j=16)
            iab = iaf.rearrange("p (h s) k -> p h s k", s=2)
            iotab = iota16[0:M, :].unsqueeze(1).unsqueeze(1).to_broadcast([M, 8, 16, 16])
            for side, (kf, eo) in enumerate(((k1f, e1), (k2f, e2))):
                op(dve, lambda e, kf=kf: e.tensor_tensor(out=oh, in0=kf.unsqueeze(3).to_broadcast([M, 8, 16, 16]), in1=iotab, op=ALU.is_equal), r=[scC, iota16], w=[scB])
                op(dve, lambda e, side=side: e.tensor_tensor(out=oh, in0=oh, in1=iab[:, :, side, :].unsqueeze(2).to_broadcast([M, 8, 16, 16]), op=ALU.mult), r=[scB, scC], w=[scB])
                op(dve, lambda e, eo=eo: e.tensor_reduce(out=eo, in_=oh, axis=AX.X, op=ALU.add), r=[scB], w=[scC])
            op(dve, lambda e: e.scalar_tensor_tensor(out=e1, in0=e1, scalar=128.0, in1=e2, op0=ALU.mult, op1=ALU.add), r=[scC], w=[scC])
            op(dve, lambda e: e.tensor_copy(out=EI[0:M, :].rearrange("p (h j) -> p h j", h=8), in_=e1), r=[scC], w=[EI])
            gx = scC[0:M, 896:1024].rearrange("p (h j) -> p h j", h=8)
            op(dve, lambda e: e.tensor_tensor(out=gx, in0=top2[0:M, :, :], in1=top2[0:M, :, 0:1].to_broadcast([M, 8, 16]), op=ALU.subtract), r=[top2], w=[scC])
            op(act, lambda e: e.activation(out=gx, in_=gx, func=AF.Exp), r=[scC], w=[scC])
            op(dve, lambda e: e.tensor_reduce(out=sm[0:M, 8:16], in_=gx, axis=AX.X, op=ALU.add), r=[scC], w=[sm])
            op(dve, lambda e: e.reciprocal(out=sm[0:M, 8:16], in_=sm[0:M, 8:16]), r=[sm], w=[sm])
            op(dve, lambda e: e.tensor_tensor(out=GT[0:M, :].rearrange("p (h j) -> p h j", h=8), in0=gx, in1=sm[0:M, 8:16].unsqueeze(2).to_broadcast([M, 8, 16]), op=ALU.mult), r=[scC, sm], w=[GT])
        def peer_loop(li, M, XP, EI, GT):
            vacc = ps_vacc()
            gate = GT[0:M, :]
            NGRP = 2
            NG_ = 128 // NGRP

            def cols(g):
                return g * NGRP, (g + 1) * NGRP

            def gath(j):
                ub = ubuf[j % NGB]
                kb.dma(pool, lambda e: e.indirect_dma_start(out=ub[:, :], out_offset=None, in_=peer_uv[li], in_offset=bass.IndirectOffsetOnAxis(ap=EI[:, j:j + 1], axis=0)), r=[EI], w=[ub])

            def dot(j, g):
                ub = ubuf[j % NGB]
                op(dve, lambda e: e.scalar_tensor_tensor(out=ub[0:M, 0:1024], in0=ub[0:M, 0:1024], scalar=1.0, in1=XP[0:M, :], op0=ALU.mult, op1=ALU.mult, accum_out=pact[0:M, j:j + 1]), r=[ub, XP], w=[ub, pact.lane(g % 3)])

            def vcopy(j, g):
                ub = ubuf[j % NGB]
                vb = vring[j % NVR]
                op(act, lambda e: e.activation(out=vb[0:M, :], in_=ub[0:M, 1024:2048], func=AF.Copy), r=[ub], w=[vb])

            def c1(g):
                j0, j1 = cols(g)
                ln = g % 3
                op(dve, lambda e: e.scalar_tensor_tensor(out=ptmp[0:M, j0:j1], in0=pact[0:M, j0:j1], scalar=0.044715, in1=pact[0:M, j0:j1], op0=ALU.mult, op1=ALU.mult), r=[pact.lane(ln)], w=[ptmp.lane(ln)])

            def c2(g):
                j0, j1 = cols(g)
                ln = g % 3
                op(dve, lambda e: e.scalar_tensor_tensor(out=ptmp[0:M, j0:j1], in0=ptmp[0:M, j0:j1], scalar=1.0, in1=pact[0:M, j0:j1], op0=ALU.add, op1=ALU.mult), r=[ptmp.lane(ln), pact.lane(ln)], w=[ptmp.lane(ln)])

            def cg(g):
                j0, j1 = cols(g)
                ln = g % 3
                op(dve, lambda e: e.tensor_tensor(out=pag[0:M, j0:j1], in0=pact[0:M, j0:j1], in1=gate[:, j0:j1], op=ALU.mult), r=[pact.lane(ln), GT], w=[pag.lane(ln)])

            def sig(g):
                j0, j1 = cols(g)
                ln = g % 3
                op(act, lambda e: e.activation(out=ptmp[0:M, j0:j1], in_=ptmp[0:M, j0:j1], func=AF.Sigmoid, scale=1.5957691216), r=[ptmp.lane(ln)], w=[ptmp.lane(ln)])

            def mm_(g):
                j0, j1 = cols(g)
                ln = g % 3
                op(dve, lambda e: e.tensor_tensor(out=pwv[0:M, j0:j1], in0=ptmp[0:M, j0:j1], in1=pag[0:M, j0:j1], op=ALU.mult), r=[ptmp.lane(ln), pag.lane(ln)], w=[pwv.lane(ln)])

            def diag(j, g):
                dg = dgb[j % 4]
                op(dve, lambda e: e.tensor_scalar(out=dg[0:M, 0:M], in0=ident[0:M, 0:M], scalar1=pwv[0:M, j:j + 1], scalar2=None, op0=ALU.mult), r=[ident, pwv.lane(g % 3)], w=[dg])

            def vmm(j):
                dg = dgb[j % 4]
                vb = vring[j % NVR]

                def f(e):
                    e.matmul(vacc[0:M, 0:512], lhsT=dg[0:M, 0:M], rhs=vb[0:M, 0:512], start=(j == 0), stop=(j == 127))
                    return e.matmul(vacc[0:M, 512:1024], lhsT=dg[0:M, 0:M], rhs=vb[0:M, 512:1024], start=(j == 0), stop=(j == 127))
                op(pe, f, r=[dg, vb], w=[vacc])

            for it in range(NG_ + 2):
                g0_, g1_, g2_ = it, it - 1, it - 2
                A0 = g0_ < NG_
                A1 = 0 <= g1_ < NG_
                A2 = 0 <= g2_ < NG_
                if A0:
                    gath(2 * g0_)
                    gath(2 * g0_ + 1)
                    dot(2 * g0_, g0_)
                    vcopy(2 * g0_, g0_)
                if A1:
                    c1(g1_)
                if A2:
                    mm_(g2_)
                if A0:
                    dot(2 * g0_ + 1, g0_)
                    vcopy(2 * g0_ + 1, g0_)
                if A1:
                    c2(g1_)
                if A2:
                    diag(2 * g2_, g2_)
                if A1:
                    cg(g1_)
                if A2:
                    diag(2 * g2_ + 1, g2_)
                if A1:
                    sig(g1_)
                if A2:
                    vmm(2 * g2_)
                    vmm(2 * g2_ + 1)
                yield

        def peer_tail(li, M, XP):
            vacc = ps_vacc()
            xp2 = scB[0:M, 0:1024]
            op(dve, lambda e: e.scalar_tensor_tensor(out=xp2, in0=XP[0:M, :], scalar=ALPHA, in1=vacc[0:M, :], op0=ALU.mult, op1=ALU.add), r=[XP, vacc], w=[scB])
            layer_norm(scB, 1, XP, M, src_ap=xp2)

        def ple_stage(li, M, row0, XP):
            kb.dma("sp", lambda e: e.dma_start(out=pe_f[0:M, :], in_=pin[li, row0:row0 + M, :]), w=[pe_f])
            op(act, lambda e: e.activation(out=xb[0:M, :], in_=XP[0:M, :], func=AF.Copy), r=[XP], w=[xb])
            transposes(xb, 8, M, xT, xT[:, :, 0:M])
            sgm = scA[0:M, 0:1024]
            for hh in range(2):
                wt = wget(("w_g", li, 0, 8, hh * 512, 512))
                pp = ps1()

                def f(e, wt=wt, pp=pp):
                    ins = None
                    for k in range(8):
                        ins = e.matmul(pp[0:M, 0:512], lhsT=xT[:, k, 0:M], rhs=wt[:, k, 0:512], start=(k == 0), stop=(k == 7))
                    return ins
                op(pe, f, r=[xT, wt], w=[pp])
                op(act, lambda e, pp=pp, hh=hh: e.activation(out=sgm[:, hh * 512:(hh + 1) * 512], in_=pp[0:M, 0:512], func=AF.Sigmoid), r=[pp], w=[scA])
            op(act, lambda e: e.activation(out=u_b[0:M, :], in_=pe_f[0:M, :], func=AF.Copy), r=[pe_f], w=[u_b])
            transposes(u_b, 2, M, uT, uT[:, :, 0:M])
            for hh in range(2):
                wt = wget(("w_p", li, 0, 2, hh * 512, 512))
                pp = ps1()

                def f(e, wt=wt, pp=pp):
                    ins = None
                    for k in range(2):
                        ins = e.matmul(pp[0:M, 0:512], lhsT=uT[:, k, 0:M], rhs=wt[:, k, 0:512], start=(k == 0), stop=(k == 1))
                    return ins
                op(pe, f, r=[uT, wt], w=[pp])
                op(dve, lambda e, pp=pp, hh=hh: e.tensor_tensor(out=sgm[:, hh * 512:(hh + 1) * 512], in0=sgm[:, hh * 512:(hh + 1) * 512], in1=pp[0:M, 0:512], op=ALU.mult), r=[pp, scA], w=[scA])
            xp2 = scB[0:M, 0:1024]
            op(dve, lambda e: e.scalar_tensor_tensor(out=xp2, in0=XP[0:M, :], scalar=ALPHA, in1=sgm, op0=ALU.mult, op1=ALU.add), r=[XP, scA], w=[scB])
            layer_norm(scB, 2, XP, M, src_ap=xp2)

        def run_all(gen):
            for _ in gen:
                pass

        for li in range(DEPTH):
            if "prep" in STAGES:
                layer_prep(li)
            def replay(item):
                if item[0] == "wissue":
                    w_issue_upto(item[1])
                elif item[0] == "op":
                    kb.op(item[1], item[2], item[3], item[4])
                else:
                    kb.dma(item[1], item[2], item[3], item[4], is_out=item[5])

            def front(n_):
                sample_, M_, row0_ = tile_geom(n_)
                peer_front(li, M_, X1P[n_ % 2], EIP[n_ % 2], GTP[n_ % 2])

            assert "peer" in STAGES
            run_all(p1a(li, 0))
            front(0)
            for n in range(NT):
                sample, M, row0 = tile_geom(n)
                L = []
                if n + 1 < NT and PIPELINE:
                    kb.rec = L
                    run_all(p1a(li, n + 1))
                    front(n + 1)
                    kb.rec = None
                idx = 0
                for _ in peer_loop(li, M, X1P[n % 2], EIP[n % 2], GTP[n % 2]):
                    kb.iter += 1
                    budget = PIPE_EVERY
                    while idx < len(L) and budget > 0:
                        item = L[idx]
                        if item[0] != "wissue" and not kb.ready(item[1], item[3], item[4], PIPE_AGE):
                            break
                        replay(item)
                        idx += 1
                        budget -= 1
                if PIPE_VERBOSE and len(L):
                    print("pipeline li=%d n=%d: replayed %d of %d inside the loop" % (li, n, idx, len(L)))
                for item in L[idx:]:
                    replay(item)
                if n + 1 < NT and not PIPELINE:
                    run_all(p1a(li, n + 1))
                    front(n + 1)
                p2_tail(li, n)
        kb.finish()
    return nc


def _consts(NTP):
    NT = NTP + 2
    c = {}
    c["c_ident"] = np.eye(128, dtype=np.float32)
    j = np.arange(128)
    c["c_l2t"] = (j[:, None] <= j[None, :]).astype(np.float32)
    c["c_kk"] = np.broadcast_to(np.arange(1, 129, dtype=np.float32)[None, :], (128, 128)).copy()
    c["c_iota"] = np.broadcast_to(np.arange(16, dtype=np.float32)[None, :], (128, 16)).copy()
    i = np.arange(128)[:, None]
    jj = np.arange(640)[None, :]
    valid = np.where(i < 64, jj < 576, jj >= 64)
    c["c_mask"] = np.where(valid, 0.0, NEG).astype(np.float32)
    half = 32
    freqs = (10000.0 ** (-np.arange(half, dtype=np.float32) / half)).astype(np.float32)
    pos = np.zeros((128, NT), dtype=np.float32)
    for n in range(NT):
        pos[:, n] = (n * 128 + np.arange(128)) if n < NTP else (2048 + np.arange(128))
    ang = (pos[:, :, None].astype(np.float32) * freqs[None, None, :]).astype(np.float32)
    c["c_cos"] = np.cos(ang).astype(np.float32).reshape(128, NT * 32)
    c["c_sin"] = np.sin(ang).astype(np.float32).reshape(128, NT * 32)
    lg = np.log1p(-(2.0 ** (-5.0 - np.arange(4, dtype=np.float64))))
    ii = np.arange(128, dtype=np.float64)
    ret = np.zeros((128, 12), dtype=np.float64)
    ret[:, 0:4] = 0.125 * np.exp(lg[None, :] * (ii[:, None] + 1.0))
    ret[:, 4:8] = np.exp(lg[None, :] * (127.0 - ii[:, None]))
    ret[:, 8:12] = np.exp(lg[None, :] * np.maximum(63.0 - ii[:, None], 0.0))
    c["c_ret"] = ret.astype(np.float32)
    I = ii[None, :]
    J = ii[:, None]
    same = (np.floor(I / 64) == np.floor(J / 64))
    cross = (J < 64) & (I >= 64)
    dt = np.zeros((128, 4, 128), dtype=np.float64)
    for h in range(4):
        dt[:, h, :] = 0.125 * np.where(same, np.exp(lg[h] * np.abs(I - J)), np.where(cross, np.exp(lg[h] * (I - J)), 0.0))
    c["c_dt"] = dt.reshape(128, 512).astype(np.float32)
    dec = np.zeros((64, 2, 4, 64), dtype=np.float64)
    for h in range(4):
        dec[:, 0, h, :] = np.exp(lg[h] * 128.0)
        dec[:, 1, h, :] = np.exp(lg[h] * 64.0)
    c["c_dec"] = dec.reshape(64, 512).astype(np.float32)
    return c


def _prep_shared(inp):
    f = lambda a: np.ascontiguousarray(np.asarray(a, dtype=np.float32))
    sh = {}
    sh["w_in"] = f(inp["w_in"])
    sh["w_glu"] = f(inp["ssm_w_glu"])
    sh["w_br_ssm"] = f(inp["w_br_ssm"])
    sh["w_br_att"] = f(inp["w_br_att"])
    sh["w_br_ret"] = f(inp["w_br_ret"])
    sh["w_o"] = f(inp["w_o"])
    sh["w_q"] = f(inp["peer_w_q"])
    sh["w_g"] = f(inp["ple_w_g"])
    sh["w_p"] = f(inp["ple_w_p"])
    pu, pv = f(inp["peer_u"]), f(inp["peer_v"])
    for i in range(2):
        sh["peer_u%d" % i] = pu[i]
        sh["peer_v%d" % i] = pv[i]
    b_re, b_im = f(inp["ssm_b_re"]), f(inp["ssm_b_im"])
    c_re, c_im = f(inp["ssm_c_re"]), f(inp["ssm_c_im"])
    bbd = np.zeros((2, 256, 2048), dtype=np.float32)
    cm = np.zeros((2, 128, 8, 2, 32), dtype=np.float32)
    for li in range(2):
        for g in range(16):
            bbd[li, g * 16:(g + 1) * 16, g * 64:(g + 1) * 64] = b_re[li, g].T
            bbd[li, g * 16:(g + 1) * 16, 1024 + g * 64:1024 + (g + 1) * 64] = b_im[li, g].T
            r0 = (g % 2) * 64
            c0 = (g % 2) * 16
            cm[li, r0:r0 + 64, g // 2, 0, c0:c0 + 16] = c_re[li, g].T
            cm[li, r0:r0 + 64, g // 2, 1, c0:c0 + 16] = c_im[li, g].T
    sh["bbd"] = bbd
    sh["cmat"] = cm.reshape(2, 128, 512)
    sk = f(inp["peer_sub_keys"]).reshape(2, 16, 128, 128)
    sh["keysT"] = np.ascontiguousarray(sk.transpose(0, 3, 1, 2)).reshape(2, 128, 2048)
    par = np.zeros((2, 128, 24), dtype=np.float32)
    for li in range(2):
        par[li, :, 0:8] = f(inp["ssm_lam_re"])[li].reshape(8, 128).T
        par[li, :, 8:16] = f(inp["ssm_lam_im"])[li].reshape(8, 128).T
        par[li, :, 16:24] = np.repeat(f(inp["ssm_log_dt"])[li], 64).reshape(8, 128).T
    sh["s5par"] = par
    rows = np.zeros((2, NROW), dtype=np.float32)
    for li in range(2):
        for i_, key in enumerate(("ln1_g", "ln2_g", "ln3_g", "ln1_b", "ln2_b", "ln3_b")):
            rows[li, i_ * 1024:(i_ + 1) * 1024] = f(inp[key])[li]
        rows[li, 6144 + R_DSK:6144 + R_DSK + 256] = f(inp["ssm_d"])[li]
        rows[li, 6144 + R_BGLU:6144 + R_BGLU + 256] = f(inp["ssm_b_glu"])[li]
        rows[li, 6144 + R_GNG:6144 + R_GNG + 256] = f(inp["ret_gn_g"])[li]
    sh["rows"] = np.ascontiguousarray(np.broadcast_to(rows[:, None, :], (2, 128, NROW)))
    i = np.arange(128)[:, None]
    jj = np.arange(640)[None, :]
    idx = np.clip(i - jj + 512, -256, 256) + 256
    tab = f(inp["att_rel_bias"])
    b2 = tab[:, idx, :]
    sh["bias2"] = np.ascontiguousarray(b2.transpose(0, 1, 3, 2)).reshape(2, 128, 8 * 640)
    return sh


def _prep_core(inp, c, NTP, seq_len):
    f = lambda a: np.asarray(a, dtype=np.float32)
    b = c % 4
    s0 = 2 * c
    d = {}
    d["xin"] = np.ascontiguousarray(np.concatenate([f(inp["x_prompt"])[b, :seq_len], f(inp["x_sample"])[s0], f(inp["x_sample"])[s0 + 1]], axis=0))
    d["pin"] = np.ascontiguousarray(np.concatenate([f(inp["p_prompt"])[:, b, :seq_len], f(inp["p_sample"])[:, s0], f(inp["p_sample"])[:, s0 + 1]], axis=1))
    st = np.zeros((2, 2, 128, 16), dtype=np.float32)
    for li in range(2):
        for s in range(2):
            st[li, s, :, 0:8] = f(inp["state_ssm_re"])[li, s0 + s].reshape(8, 128).T
            st[li, s, :, 8:16] = f(inp["state_ssm_im"])[li, s0 + s].reshape(8, 128).T
    d["st_ssm"] = st
    d["cache_k"] = np.ascontiguousarray(f(inp["cache_attn_k"])[:, s0:s0 + 2].reshape(2, 2, 512, 512))
    d["cache_v"] = np.ascontiguousarray(f(inp["cache_attn_v"])[:, s0:s0 + 2].reshape(2, 2, 512, 512))
    sr = f(inp["state_ret"])[:, s0:s0 + 2]
    d["st_ret"] = np.ascontiguousarray(sr.transpose(0, 1, 3, 2, 4)).reshape(2, 2, 64, 256)
    return d


_CACHE = {}


def run_cores(inp, NTP=NTP_FULL, n_cores=8, dbg=None):
    key = (NTP, tuple(sorted(dbg.keys())) if dbg else None, tuple(sorted(STAGES)))
    if key not in _CACHE:
        _CACHE[key] = build_program(NTP, dbg)
    nc = _CACHE[key]
    sh = _prep_shared(inp)
    sh.update(_consts(NTP))
    in_maps = []
    for c in range(n_cores):
        m = dict(sh)
        m.update(_prep_core(inp, c, NTP, NTP * 128))
        in_maps.append(m)
    res = run_bass_kernel_spmd(nc, in_maps, core_ids=list(range(n_cores)))
    return res.results


def kernel(**inputs):
    NTP = NTP_FULL
    res = run_cores(inputs, NTP, 8)
    y_p = np.zeros((4, 4096, D), np.float32)
    y_s = np.zeros((16, 64, D), np.float32)
    ssm_re_p = np.zeros((2, 4, 16, 64), np.float32)
    ssm_im_p = np.zeros((2, 4, 16, 64), np.float32)
    k_p = np.zeros((2, 4, 512, 8, 64), np.float32)
    v_p = np.zeros((2, 4, 512, 8, 64), np.float32)
    ret_p = np.zeros((2, 4, 4, 64, 64), np.float32)
    ssm_re_s = np.zeros((2, 16, 16, 64), np.float32)
    ssm_im_s = np.zeros((2, 16, 16, 64), np.float32)
    k_s = np.zeros((2, 16, 64, 8, 64), np.float32)
    v_s = np.zeros((2, 16, 64, 8, 64), np.float32)
    ret_s = np.zeros((2, 16, 4, 64, 64), np.float32)
    for c in range(8):
        r = res[c]
        yo = np.asarray(r["y_out"])
        so = np.asarray(r["ssm_out"])
        ko = np.asarray(r["k_out"])
        vo = np.asarray(r["v_out"])
        ro = np.asarray(r["ret_out"])
        if c < 4:
            y_p[c] = yo[0:4096]
            for li in range(2):
                ssm_re_p[li, c] = so[li, 0][:, 0:8].T.reshape(16, 64)
                ssm_im_p[li, c] = so[li, 0][:, 8:16].T.reshape(16, 64)
                k_p[li, c] = ko[li, 0:512].reshape(512, 8, 64)
                v_p[li, c] = vo[li, 0:512].reshape(512, 8, 64)
                ret_p[li, c] = ro[li, 0].reshape(64, 4, 64).transpose(1, 0, 2)
        for s in range(2):
            q = 2 * c + s
            y_s[q] = yo[4096 + 64 * s:4096 + 64 * (s + 1)]
            for li in range(2):
                ssm_re_s[li, q] = so[li, 1 + s][:, 0:8].T.reshape(16, 64)
                ssm_im_s[li, q] = so[li, 1 + s][:, 8:16].T.reshape(16, 64)
                k_s[li, q] = ko[li, 512 + 64 * s:512 + 64 * (s + 1)].reshape(64, 8, 64)
                v_s[li, q] = vo[li, 512 + 64 * s:512 + 64 * (s + 1)].reshape(64, 8, 64)
                ret_s[li, q] = ro[li, 1 + s].reshape(64, 4, 64).transpose(1, 0, 2)
    return (y_p, y_s, ssm_re_p, ssm_im_p, k_p, v_p, ret_p, ssm_re_s, ssm_im_s, k_s, v_s, ret_s)
```

```python
import numpy as np
import concourse.bass as bass
import concourse.mybir as mybir
from concourse.bass_utils import run_bass_kernel_spmd
from contextlib import ExitStack

F32 = mybir.dt.float32
BF16 = mybir.dt.bfloat16
U32 = mybir.dt.uint32
ALU = mybir.AluOpType
AF = mybir.ActivationFunctionType
AX = mybir.AxisListType

D = 1024
DEPTH = 2
NTP_FULL = 32
ALPHA = float((2 * DEPTH) ** 0.25)
LN_EPS = 1e-5
NEG = -30000.0
IN_BLOCKS = [(0, 256), (256, 512), (768, 512), (1280, 512), (1792, 512), (2304, 512),
             (2816, 512), (3328, 512), (3840, 512), (4352, 512), (4864, 512), (5376, 512)]
R_DSK, R_BGLU, R_GNG = 0, 256, 512
NROW = 6912


STRICT = True


class Dep:
    def __init__(self, parent=None):
        self.w = None
        self.r = {}
        self.parent = parent
        self.kids = {}

    def lane(self, key):
        if key not in self.kids:
            self.kids[key] = Dep(self)
        return self.kids[key]


class Tn(Dep):
    def __init__(self, t, name=""):
        super().__init__(None)
        self.t = t
        self.name = name

    def __getitem__(self, k):
        return self.t[k]


class KB:
    NS = 8

    def __init__(self, nc, es):
        self.nc = nc
        self.es = es
        self.engs = {"pe": nc.tensor, "dve": nc.vector, "act": nc.scalar, "pool": nc.gpsimd, "sp": nc.sync}
        self.sem = {e: es.enter_context(nc.semaphore("s_" + e)) for e in self.engs}
        self.cnt = {e: 0 for e in self.engs}
        self.seen = {e: {} for e in self.engs}
        self.dsem = {q: [es.enter_context(nc.semaphore("d_%s%d" % (q, i))) for i in range(self.NS)] for q in ("sp", "pool")}
        self.dcnt = {q: [0] * self.NS for q in ("sp", "pool")}
        self.dnext = {q: 0 for q in ("sp", "pool")}
        self.out_tokens = []
        self.rec = None
        self.iter = 0
        self.tok_iter = {}

    def sb(self, name, shape, dtype):
        return Tn(self.es.enter_context(self.nc.sbuf_tensor(name, shape, dtype)), name)

    def _wait(self, eng, tok):
        sem, val, key = tok
        if self.seen[eng].get(key, 0) >= val:
            return
        self.engs[eng].wait_ge(sem, val)
        self.seen[eng][key] = val

    def _deps1(self, eng, b, writing):
        strict = STRICT and eng != "pe"
        if b.w is not None and (b.w[2] != eng or (strict if writing else eng != "pe")):
            self._wait(eng, b.w)
        if writing:
            for k, t in b.r.items():
                if strict or k != eng:
                    self._wait(eng, t)

    def _deps(self, eng, r, w):
        for lst, writing in ((r, False), (w, True)):
            for b in lst:
                self._deps1(eng, b, writing)
                if b.parent is not None:
                    self._deps1(eng, b.parent, writing)
                for k_ in b.kids.values():
                    self._deps1(eng, k_, writing)

    def _mark(self, tok, r, w):
        for b in r:
            b.r[tok[2]] = tok
        for b in w:
            b.w = tok
            b.r = {}
            for k_ in b.kids.values():
                k_.w = tok
                k_.r = {}

    def ready(self, eng, r, w, age):
        ok = [True]

        def chk(b, writing):
            toks = []
            if b.w is not None:
                toks.append(b.w)
            if writing:
                toks.extend(b.r.values())
            for t in toks:
                sem, val, key = t
                if key == eng or self.seen[eng].get(key, 0) >= val:
                    continue
                if self.iter - self.tok_iter.get((key, val), -10 ** 9) < age:
                    ok[0] = False
        for lst, writing in ((r, False), (w, True)):
            for b in lst:
                chk(b, writing)
                if b.parent is not None:
                    chk(b.parent, writing)
                for k_ in b.kids.values():
                    chk(k_, writing)
        return ok[0]

    def op(self, eng, fn, r=(), w=()):
        r = _flat(r)
        w = _flat(w)
        if self.rec is not None:
            self.rec.append(("op", eng, fn, r, w, False))
            return None
        self._deps(eng, r, w)
        ins = fn(self.engs[eng])
        self.cnt[eng] += 1
        ins.then_inc(self.sem[eng], 1)
        tok = (self.sem[eng], self.cnt[eng], eng)
        self.tok_iter[(eng, self.cnt[eng])] = self.iter
        self._mark(tok, r, w)
        return tok

    def dma(self, q, fn, r=(), w=(), is_out=False):
        r = _flat(r)
        w = _flat(w)
        if self.rec is not None:
            self.rec.append(("dma", q, fn, r, w, is_out))
            return None
        self._deps(q, r, w)
        slot = self.dnext[q]
        self.dnext[q] = (slot + 1) % self.NS
        sem = self.dsem[q][slot]
        key = (q, slot)
        if self.dcnt[q][slot] > 0:
            self._wait(q, (sem, 16 * self.dcnt[q][slot], key))
        ins = fn(self.engs[q])
        self.dcnt[q][slot] += 1
        ins.then_inc(sem, 16)
        tok = (sem, 16 * self.dcnt[q][slot], key)
        self.tok_iter[(key, 16 * self.dcnt[q][slot])] = self.iter
        self._mark(tok, r, w)
        if is_out:
            self.out_tokens.append(tok)
        return tok

    def finish(self):
        for q in ("sp", "pool"):
            for slot in range(self.NS):
                if self.dcnt[q][slot] > 0:
                    self._wait("sp", (self.dsem[q][slot], 16 * self.dcnt[q][slot], (q, slot)))


def _flat(lst):
    out = []
    for x in lst:
        if x is None:
            continue
        if isinstance(x, (list, tuple)):
            out.extend(_flat(x))
        else:
            out.append(x)
    return out


STAGES = {"prep", "inproj", "s5", "attn", "ret", "merge", "peer", "ple"}
PIPELINE = True
PIPE_EVERY = 8
PIPE_VERBOSE = False
PIPE_AGE = 1
PIPE_MID = False
PIPE_OOO = True
PIPE_WINDOW = 40
DBG_TILE = (0, 0)


def build_program(NTP=NTP_FULL, dbg=None):
    NT = NTP + 2
    NTOK = NTP * 128 + 128
    nc = bass.Bass("TRN2", target_bir_lowering=False)
    es = ExitStack()

    def din(name, shape, dt=F32):
        return nc.dram_tensor(name, list(shape), dt, kind="ExternalInput").ap()

    def dout(name, shape, dt=F32):
        return nc.dram_tensor(name, list(shape), dt, kind="ExternalOutput").ap()

    def dint(name, shape, dt=F32):
        return nc.dram_tensor(name, list(shape), dt, kind="Internal").ap()

    xin = din("xin", [NTOK, D])
    pin = din("pin", [2, NTOK, 256])
    st_ssm = din("st_ssm", [2, 2, 128, 16])
    cache_k = din("cache_k", [2, 2, 512, 512])
    cache_v = din("cache_v", [2, 2, 512, 512])
    st_ret = din("st_ret", [2, 2, 64, 256])
    W = {
        "w_in": din("w_in", [2, 1024, 5888]), "w_glu": din("w_glu", [2, 256, 256]),
        "w_br_ssm": din("w_br_ssm", [2, 256, 1024]), "w_br_att": din("w_br_att", [2, 512, 1024]),
        "w_br_ret": din("w_br_ret", [2, 256, 1024]), "w_o": din("w_o", [2, 1024, 1024]),
        "w_q": din("w_q", [2, 1024, 2048]), "w_g": din("w_g", [2, 1024, 1024]),
        "w_p": din("w_p", [2, 256, 1024]), "bbd": din("bbd", [2, 256, 2048]),
    }
    peer_u = [din("peer_u%d" % i, [16384, 1024]) for i in range(2)]
    peer_v = [din("peer_v%d" % i, [16384, 1024]) for i in range(2)]
    cmat = din("cmat", [2, 128, 512])
    keysT = din("keysT", [2, 128, 2048])
    s5par = din("s5par", [2, 128, 24])
    rows = din("rows", [2, 128, NROW])
    bias2 = din("bias2", [2, 128, 8 * 640])
    c_ident = din("c_ident", [128, 128])
    c_l2t = din("c_l2t", [128, 128])
    c_kk = din("c_kk", [128, 128])
    c_iota = din("c_iota", [128, 16])
    c_mask = din("c_mask", [128, 640])
    c_cos = din("c_cos", [128, NT * 32])
    c_sin = din("c_sin", [128, NT * 32])
    c_ret = din("c_ret", [128, 12])
    c_dt = din("c_dt", [128, 512])
    c_dec = din("c_dec", [64, 512])

    y_out = dout("y_out", [NTOK, D])
    ssm_out = dout("ssm_out", [2, 3, 128, 16])
    k_out = dout("k_out", [2, 640, 512])
    v_out = dout("v_out", [2, 640, 512])
    ret_out = dout("ret_out", [2, 3, 64, 256])
    dbg_aps = {}
    if dbg:
        for name, (shape, dt_) in dbg.items():
            dbg_aps[name] = dout("dbg_" + name, shape, dt_)

    X1 = dint("X1", [NTOK, D])
    WB = {k: dint("wb_" + k, list(v.shape), BF16) for k, v in W.items()}
    peer_uv = [dint("peer_uv%d" % i, [16384, 2048], BF16) for i in range(2)]

    with es:
        kb = KB(nc, es)
        pe, dve, act, pool = "pe", "dve", "act", "pool"

        Fd = [es.enter_context(nc.psum_tensor("F%d" % i, [128, 1024], F32)) for i in range(3)]
        B0 = es.enter_context(nc.psum_tensor("B0", [128, 1024], BF16))
        FA = es.enter_context(nc.psum_tensor("FA", [128, 512], F32))
        Fs = [Tn(Fd[i // 2][:, (i % 2) * 512:(i % 2) * 512 + 512], "F%d_%d" % (i // 2, i % 2)) for i in range(6)]
        Bs = [Tn(B0, "B0")]
        FAs = Tn(FA, "FA")
        psst = {"f1": 0, "f2": 0}

        class PS:
            def __init__(self, ap, deps):
                self.ap = ap
                self.deps = deps

            def __getitem__(self, k):
                return self.ap[k]

        def ps1():
            i = psst["f1"] % 4
            psst["f1"] += 1
            return PS(Fs[i].t, [Fs[i]])

        def ps2():
            i = psst["f2"] % 2
            psst["f2"] += 1
            return PS(Fd[i], [Fs[2 * i], Fs[2 * i + 1]])

        def ps_acc():
            return PS(FA, [FAs])

        def ps_vacc():
            return PS(Fd[2], [Fs[4], Fs[5]])

        def psb():
            return PS(B0, [Bs[0]])

        def D_(x):
            return x.deps if isinstance(x, PS) else x

        def op(eng, fn, r=(), w=()):
            return kb.op(eng, fn, [D_(x) for x in r], [D_(x) for x in w])

        identF = kb.sb("identF", [128, 128], F32)
        ident = kb.sb("ident", [128, 128], BF16)
        l2t = kb.sb("l2t", [128, 128], BF16)
        iota16 = kb.sb("iota16", [128, 16], F32)
        cs_t = kb.sb("cs_t", [128, 2, 32], F32)
        cret = kb.sb("cret", [128, 12], F32)
        dtt = kb.sb("dtt", [128, 4, 128], F32)
        dect = kb.sb("dect", [64, 2, 256], F32)
        ctmp = kb.sb("ctmp", [128, 128], F32)

        def ld(dst, dst_ap, src, q="sp"):
            kb.dma(q, lambda e: e.dma_start(out=dst_ap, in_=src), r=[], w=[dst])

        ld(identF, identF[:], c_ident)
        ld(ctmp, ctmp[:], c_l2t)
        ld(iota16, iota16[:], c_iota)
        ld(cret, cret[:], c_ret)
        ld(dtt, dtt[:], c_dt.rearrange("p (h i) -> p h i", h=4))
        ld(dect, dect[:], c_dec.rearrange("p (a c) -> p a c", a=2))
        op(dve, lambda e: e.tensor_copy(out=ident[:], in_=identF[:]), r=[identF], w=[ident])
        op(dve, lambda e: e.tensor_copy(out=l2t[:], in_=ctmp[:]), r=[ctmp], w=[l2t])

        wb_dep = Dep()
        for k, src in W.items():
            tot = 1
            for s_ in src.shape:
                tot *= s_
            sf = src.rearrange("l a b -> (l a b)").rearrange("(r c) -> r c", c=2048)
            df = WB[k].rearrange("l a b -> (l a b)").rearrange("(r c) -> r c", c=2048)
            nrow = tot // 2048
            for r0 in range(0, nrow, 256):
                r1 = min(nrow, r0 + 256)
                kb.dma(pool, lambda e, a=df[r0:r1, :], b=sf[r0:r1, :]: e.dma_start(out=a, in_=b), r=[], w=[])
        for li_ in range(2):
            for hh_, src in enumerate((peer_u[li_], peer_v[li_])):
                for r0 in range(0, 16384, 512):
                    kb.dma(pool, lambda e, a=peer_uv[li_][r0:r0 + 512, hh_ * 1024:(hh_ + 1) * 1024], b=src[r0:r0 + 512, :]: e.dma_start(out=a, in_=b), r=[], w=[])
        cast_tokens = [(kb.dsem[pool][s], 16 * kb.dcnt[pool][s], (pool, s)) for s in range(kb.NS) if kb.dcnt[pool][s] > 0]
        for t in cast_tokens:
            kb._wait("sp", t)

        NWS = 2
        wst = [kb.sb("wst%d" % i, [128, 8, 512], BF16) for i in range(NWS)]
        wq = []
        wstate = {"issued": 0, "used": 0}

        def w_issue():
            i = wstate["issued"]
            name, li, k0, K, c0, N = wq[i]
            slot = wst[i % NWS]
            src = WB[name][li, k0 * 128:(k0 + K) * 128, c0:c0 + N].rearrange("(k p) n -> p k n", p=128)
            kb.dma("sp", lambda e: e.dma_start(out=slot[:, 0:K, 0:N], in_=src), r=[], w=[slot])
            wstate["issued"] += 1

        def w_issue_upto(i):
            while wstate["issued"] < min(len(wq), i + NWS):
                w_issue()

        def wget(spec):
            i = wstate["used"]
            assert wq[i] == spec, (wq[i], spec)
            wstate["used"] += 1
            if kb.rec is not None:
                kb.rec.append(("wissue", i))
            else:
                w_issue_upto(i)
            return wst[i % NWS]

        def tile_specs(li):
            sp = []
            if "inproj" in STAGES:
                sp += [("w_in", li, 0, 8, c0, n) for (c0, n) in IN_BLOCKS]
            if "merge" in STAGES:
                for name, K in (("w_br_ssm", 2), ("w_br_att", 4), ("w_br_ret", 2)):
                    sp += [(name, li, 0, K, 0, 512), (name, li, 0, K, 512, 512)]
                sp += [("w_o", li, 0, 8, 0, 512), ("w_o", li, 0, 8, 512, 512)]
            if "peer" in STAGES:
                sp += [("w_q", li, 0, 8, c * 512, 512) for c in range(4)]
            if "ple" in STAGES:
                sp += [("w_g", li, 0, 8, 0, 512), ("w_g", li, 0, 8, 512, 512)]
                sp += [("w_p", li, 0, 2, 0, 512), ("w_p", li, 0, 2, 512, 512)]
            return sp

        def specs_p1a(li):
            sp = []
            if "inproj" in STAGES:
                sp += [("w_in", li, 0, 8, c0, n) for (c0, n) in IN_BLOCKS]
            if "merge" in STAGES:
                for name, K in (("w_br_ssm", 2), ("w_br_att", 4), ("w_br_ret", 2)):
                    sp += [(name, li, 0, K, 0, 512), (name, li, 0, K, 512, 512)]
                sp += [("w_o", li, 0, 8, 0, 512), ("w_o", li, 0, 8, 512, 512)]
            return sp

        for li in range(DEPTH):
            wq.extend(specs_p1a(li))
            wq.extend([("w_q", li, 0, 8, c * 512, 512) for c in range(4)])
            for n in range(NT):
                if n + 1 < NT:
                    wq.extend(specs_p1a(li))
                    wq.extend([("w_q", li, 0, 8, c * 512, 512) for c in range(4)])
                if "ple" in STAGES:
                    wq.extend([("w_g", li, 0, 8, 0, 512), ("w_g", li, 0, 8, 512, 512), ("w_p", li, 0, 2, 0, 512), ("w_p", li, 0, 2, 512, 512)])

        rows_g = kb.sb("rows_g", [128, 3, 1024], BF16)
        rows_b = kb.sb("rows_b", [128, 3, 1024], BF16)
        rows_s = kb.sb("rows_s", [128, 768], BF16)
        bbd_b = kb.sb("bbd_b", [128, 2, 2, 512], BF16)
        cm_f = kb.sb("cm_f", [128, 512], F32)
        cm_b = kb.sb("cm_b", [128, 8, 2, 32], BF16)
        wglu_b = kb.sb("wglu_b", [128, 2, 256], BF16)
        keys_b = kb.sb("keys_b", [128, 16, 128], BF16)
        bias_b = kb.sb("bias_b", [128, 8, 640], BF16)
        ainvk = kb.sb("ainvk", [128, 2, 1024], BF16)
        a1t = kb.sb("a1t", [128, 2, 8, 128], F32)
        par = kb.sb("par", [128, 24], F32)
        sp_ = {n_: kb.sb("sp_" + n_, [128, 8], F32) for n_ in
               ("dt", "ldr", "th", "c1", "s1", "t0", "t1", "t2", "are", "aim", "kre", "kim", "l2", "nldr")}
        hst = kb.sb("hst", [128, 16], F32)
        sret_f = kb.sb("sret_f", [64, 256], F32)
        sret_b = kb.sb("sret_b", [64, 4, 64], BF16)
        kt2 = kb.sb("kt2", [128, 4, 9 * 128], BF16)
        vr = kb.sb("vr", [128, 9, 512], BF16)
        scA = kb.sb("scA", [128, 2048], F32)
        scB = kb.sb("scB", [128, 2048], F32)
        scC = kb.sb("scC", [128, 1024], F32)
        x_f = kb.sb("x_f", [128, D], F32)
        x1_f = kb.sb("x1_f", [128, D], F32)
        xpre = kb.sb("xpre", [128, D], F32)
        xb = kb.sb("xb", [128, D], BF16)
        xT = kb.sb("xT", [128, 8, 128], BF16)
        gates_b = kb.sb("gates_b", [128, 3, 1024], BF16)
        u_f = kb.sb("u_f", [128, 256], F32)
        u_b = kb.sb("u_b", [128, 256], BF16)
        qa_b = kb.sb("qa_b", [128, 512], BF16)
        ka_b = kb.sb("ka_b", [128, 512], BF16)
        vr_b = kb.sb("vr_b", [128, 256], BF16)
        sg_f = kb.sb("sg_f", [128, 256], F32)
        sm = kb.sb("sm", [128, 64], F32)
        bigb = kb.sb("bigb", [128, 2048], BF16)
        uT = kb.sb("uT", [128, 2, 128], BF16)
        yssm_b = kb.sb("yssm_b", [128, 256], BF16)
        yatt_b = kb.sb("yatt_b", [128, 512], BF16)
        yret_b = kb.sb("yret_b", [128, 256], BF16)
        qT2 = kb.sb("qT2", [128, 4, 128], BF16)
        pb_att = kb.sb("pb_att", [128, 640], BF16)
        pT = kb.sb("pT", [128, 5, 128], BF16)
        qkT = kb.sb("qkT", [64, 8, 128], BF16)
        qsT = kb.sb("qsT", [64, 4, 128], BF16)
        qk_b = kb.sb("qk_b", [128, 512], BF16)
        qs_b = kb.sb("qs_b", [128, 256], BF16)
        kk_b = kb.sb("kk_b", [128, 256], BF16)
        scT_b = kb.sb("scT_b", [128, 4, 128], BF16)
        yT = kb.sb("yT", [128, 8, 128], BF16)
        topa = kb.sb("topa", [128, 16, 16], F32)
        topi = kb.sb("topi", [128, 16, 16], U32)
        top2 = kb.sb("top2", [128, 8, 16], F32)
        pos2 = kb.sb("pos2", [128, 8, 16], U32)
        eidx = kb.sb("eidx", [128, 128], U32)
        eidx_b = kb.sb("eidx_b", [128, 128], U32)
        gate_a = kb.sb("gate_a", [128, 128], F32)
        gate_b = kb.sb("gate_b", [128, 128], F32)
        pact = kb.sb("pact", [128, 128], F32)
        pwv = kb.sb("pwv", [128, 128], F32)
        ptmp = kb.sb("ptmp", [128, 128], F32)
        pag = kb.sb("pag", [128, 128], F32)
        NGB = 8
        ubuf = [kb.sb("ubuf%d" % i, [128, 2 * D], BF16) for i in range(NGB)]
        NVR = 6
        vring = [kb.sb("vring%d" % i, [128, D], BF16) for i in range(NVR)]
        vbuf = ubuf
        dgb = [kb.sb("dgb%d" % i, [128, 128], BF16) for i in range(4)]
        pe_f = kb.sb("pe_f", [128, 256], F32)
        print("SBUF bytes remaining per partition:", nc.sbuf_bytes_remaining)

        def dump(name, src_ap, deps):
            if name in dbg_aps:
                kb.dma("sp", lambda e: e.dma_start(out=dbg_aps[name], in_=src_ap), r=deps, w=[], is_out=True)

        op(pool, lambda e: e.memset(eidx[:], 0), w=[eidx])
        op(pool, lambda e: e.memset(eidx_b[:], 0), w=[eidx_b])
        X1P = [x1_f, xpre]
        EIP = [eidx, eidx_b]
        GTP = [gate_a, gate_b]

        def transposes(src, nblk, M, dst, dst_slices, cw=128, src_cols=None):
            pb = psb()

            def f(e):
                ins = None
                for i in range(nblk):
                    c0 = i * cw if src_cols is None else src_cols[i]
                    ins = e.transpose(out=pb[0:cw, i * 128:i * 128 + M], in_=src[0:M, c0:c0 + cw], identity=ident[0:M, 0:M])
                return ins
            op(pe, f, r=[src, ident], w=[pb])
            op(dve, lambda e: e.tensor_copy(out=dst_slices, in_=pb[0:cw, 0:nblk * 128].rearrange("p (b m) -> p b m", m=128)[:, :, 0:M]),
               r=[pb], w=[dst])

        def layer_norm(src, idx, dst, M, src_ap=None):
            junk = scA
            sap = src[0:M, :] if src_ap is None else src_ap
            op(act, lambda e: e.activation(out=scA[0:M, 0:1024], in_=sap, func=AF.Identity, accum_out=sm[0:M, 0:1]), r=[src], w=[junk, sm])
            op(act, lambda e: e.activation(out=scA[0:M, 0:1024], in_=sap, func=AF.Square, accum_out=sm[0:M, 1:2]), r=[src], w=[junk, sm])
            op(dve, lambda e: e.tensor_scalar(out=sm[0:M, 2:3], in0=sm[0:M, 0:1], scalar1=1.0 / D, scalar2=None, op0=ALU.mult), r=[sm], w=[sm])
            op(dve, lambda e: e.tensor_tensor(out=sm[0:M, 3:4], in0=sm[0:M, 2:3], in1=sm[0:M, 2:3], op=ALU.mult), r=[sm], w=[sm])
            op(dve, lambda e: e.scalar_tensor_tensor(out=sm[0:M, 4:5], in0=sm[0:M, 1:2], scalar=1.0 / D, in1=sm[0:M, 3:4], op0=ALU.mult, op1=ALU.subtract), r=[sm], w=[sm])
            op(dve, lambda e: e.tensor_scalar(out=sm[0:M, 4:5], in0=sm[0:M, 4:5], scalar1=LN_EPS, scalar2=None, op0=ALU.add), r=[sm], w=[sm])
            op(act, lambda e: e.activation(out=sm[0:M, 5:6], in_=sm[0:M, 4:5], func=AF.Sqrt), r=[sm], w=[sm])
            op(dve, lambda e: e.reciprocal(out=sm[0:M, 6:7], in_=sm[0:M, 5:6]), r=[sm], w=[sm])
            op(dve, lambda e: e.scalar_tensor_tensor(out=sm[0:M, 7:8], in0=sm[0:M, 2:3], scalar=-1.0, in1=sm[0:M, 6:7], op0=ALU.mult, op1=ALU.mult), r=[sm], w=[sm])
            op(act, lambda e: e.activation(out=scA[0:M, 0:1024], in_=sap, func=AF.Identity, scale=sm[0:M, 6:7], bias=sm[0:M, 7:8]), r=[src, sm], w=[junk])
            op(dve, lambda e: e.tensor_tensor(out=scA[0:M, 0:1024], in0=scA[0:M, 0:1024], in1=rows_g[0:M, idx, :], op=ALU.mult), r=[junk, rows_g], w=[junk])
            op(dve, lambda e: e.tensor_tensor(out=dst[0:M, :], in0=scA[0:M, 0:1024], in1=rows_b[0:M, idx, :], op=ALU.add), r=[junk, rows_b], w=[dst])

        def gelu_tanh(src_ap, dst_ap, tmp_ap, deps_r, deps_w, tmp_dep):
            op(dve, lambda e: e.tensor_tensor(out=tmp_ap, in0=src_ap, in1=src_ap, op=ALU.mult), r=deps_r, w=[tmp_dep])
            op(dve, lambda e: e.tensor_scalar(out=tmp_ap, in0=tmp_ap, scalar1=0.044715, scalar2=1.0, op0=ALU.mult, op1=ALU.add), r=[tmp_dep], w=[tmp_dep])
            op(dve, lambda e: e.tensor_tensor(out=tmp_ap, in0=tmp_ap, in1=src_ap, op=ALU.mult), r=[tmp_dep] + deps_r, w=[tmp_dep])
            op(act, lambda e: e.activation(out=tmp_ap, in_=tmp_ap, func=AF.Sigmoid, scale=1.5957691216), r=[tmp_dep], w=[tmp_dep])
            op(dve, lambda e: e.tensor_tensor(out=dst_ap, in0=tmp_ap, in1=src_ap, op=ALU.mult), r=[tmp_dep] + deps_r, w=deps_w)

        def layer_prep(li):
            kb.dma(pool, lambda e: e.dma_start(out=rows_g[:], in_=rows[li][:, 0:3072].rearrange("p (a d) -> p a d", a=3)), w=[rows_g])
            kb.dma(pool, lambda e: e.dma_start(out=rows_b[:], in_=rows[li][:, 3072:6144].rearrange("p (a d) -> p a d", a=3)), w=[rows_b])
            kb.dma(pool, lambda e: e.dma_start(out=rows_s[:], in_=rows[li][:, 6144:6912]), w=[rows_s])
            for k_ in range(2):
                for part_ in range(2):
                    c0_ = part_ * 1024 + k_ * 512
                    kb.dma("sp", lambda e, k_=k_, part_=part_, c0_=c0_: e.dma_start(out=bbd_b[:, k_, part_, :], in_=WB["bbd"][li, k_ * 128:(k_ + 1) * 128, c0_:c0_ + 512]), w=[bbd_b])
            kb.dma("sp", lambda e: e.dma_start(out=wglu_b[:], in_=WB["w_glu"][li].rearrange("(k p) n -> p k n", p=128)), w=[wglu_b])
            kb.dma("sp", lambda e: e.dma_start(out=cm_f[:], in_=cmat[li]), w=[cm_f])
            op(dve, lambda e: e.tensor_copy(out=cm_b[:].rearrange("p a b c -> p (a b c)"), in_=cm_f[:]), r=[cm_f], w=[cm_b])
            kb.dma("sp", lambda e: e.dma_start(out=scA[:, :], in_=keysT[li]), w=[scA])
            op(dve, lambda e: e.tensor_copy(out=keys_b[:].rearrange("p a b -> p (a b)"), in_=scA[:, :]), r=[scA], w=[keys_b])
            kb.dma("sp", lambda e: e.dma_start(out=scB[:, 0:640], in_=c_mask), w=[scB])
            for h in range(8):
                kb.dma("sp", lambda e, h=h: e.dma_start(out=scC[:, 0:640], in_=bias2[li][:, h * 640:(h + 1) * 640]), w=[scC])
                op(dve, lambda e, h=h: e.tensor_tensor(out=bias_b[:, h, :], in0=scC[:, 0:640], in1=scB[:, 0:640], op=ALU.add), r=[scC, scB], w=[bias_b])
            kb.dma("sp", lambda e: e.dma_start(out=par[:], in_=s5par[li]), w=[par])
            P = sp_
            lre, lim, ldt = par[:, 0:8], par[:, 8:16], par[:, 16:24]
            op(act, lambda e: e.activation(out=P["dt"][:], in_=ldt, func=AF.Exp), r=[par], w=[P["dt"]])
            op(dve, lambda e: e.tensor_tensor(out=P["ldr"][:], in0=lre, in1=P["dt"][:], op=ALU.mult), r=[par, P["dt"]], w=[P["ldr"]])
            op(dve, lambda e: e.tensor_tensor(out=P["th"][:], in0=lim, in1=P["dt"][:], op=ALU.mult), r=[par, P["dt"]], w=[P["th"]])
            op(dve, lambda e: e.tensor_scalar(out=P["nldr"][:], in0=P["ldr"][:], scalar1=-1.0, scalar2=None, op0=ALU.mult), r=[P["ldr"]], w=[P["nldr"]])

            def sin_reduced(dst, shift):
                TWO_PI = 2.0 * np.pi
                op(dve, lambda e: e.tensor_scalar(out=P["t0"][:], in0=P["th"][:], scalar1=float(shift), scalar2=1.0 / TWO_PI, op0=ALU.add, op1=ALU.mult), r=[P["th"]], w=[P["t0"]])
                tI = topi
                op(dve, lambda e: e.tensor_copy(out=tI[:, 0, 0:8], in_=P["t0"][:]), r=[P["t0"]], w=[tI])
                op(dve, lambda e: e.tensor_copy(out=P["t1"][:], in_=tI[:, 0, 0:8]), r=[tI], w=[P["t1"]])
                op(dve, lambda e: e.tensor_tensor(out=P["t2"][:], in0=P["t0"][:], in1=P["t1"][:], op=ALU.subtract), r=[P["t0"], P["t1"]], w=[P["t2"]])
                op(dve, lambda e: e.tensor_scalar(out=P["t1"][:], in0=P["t2"][:], scalar1=0.5, scalar2=None, op0=ALU.is_gt), r=[P["t2"]], w=[P["t1"]])
                op(dve, lambda e: e.tensor_tensor(out=P["t2"][:], in0=P["t2"][:], in1=P["t1"][:], op=ALU.subtract), r=[P["t2"], P["t1"]], w=[P["t2"]])
                op(dve, lambda e: e.tensor_scalar(out=P["t2"][:], in0=P["t2"][:], scalar1=TWO_PI, scalar2=3.1415925, op0=ALU.mult, op1=ALU.min), r=[P["t2"]], w=[P["t2"]])
                op(dve, lambda e: e.tensor_scalar(out=P["t2"][:], in0=P["t2"][:], scalar1=-3.1415925, scalar2=None, op0=ALU.max), r=[P["t2"]], w=[P["t2"]])
                op(act, lambda e: e.activation(out=dst[:], in_=P["t2"][:], func=AF.Sin), r=[P["t2"]], w=[dst])
            sin_reduced(P["s1"], 0.0)
            sin_reduced(P["c1"], np.pi / 2)
            Ere = scA[:, 0:1024].rearrange("p (c k) -> p c k", c=8)
            Eim = scA[:, 1024:2048].rearrange("p (c k) -> p c k", c=8)
            Tm = scB[:, 0:1024].rearrange("p (c k) -> p c k", c=8)
            Tm2 = scB[:, 1024:2048].rearrange("p (c k) -> p c k", c=8)
            op(dve, lambda e: e.tensor_copy(out=Ere[:, :, 0:1], in_=P["c1"][:].unsqueeze(2)), r=[P["c1"]], w=[scA])
            op(dve, lambda e: e.tensor_copy(out=Eim[:, :, 0:1], in_=P["s1"][:].unsqueeze(2)), r=[P["s1"]], w=[scA])
            n_have = 1
            while n_have < 128:
                m = n_have
                br = Ere[:, :, m - 1:m].to_broadcast([128, 8, m])
                bi = Eim[:, :, m - 1:m].to_broadcast([128, 8, m])
                op(dve, lambda e: e.tensor_tensor(out=Tm[:, :, 0:m], in0=Ere[:, :, 0:m], in1=br, op=ALU.mult), r=[scA], w=[scB])
                op(dve, lambda e: e.tensor_tensor(out=Tm2[:, :, 0:m], in0=Eim[:, :, 0:m], in1=bi, op=ALU.mult), r=[scA], w=[scB])
                op(dve, lambda e: e.tensor_tensor(out=Tm[:, :, 0:m], in0=Tm[:, :, 0:m], in1=Tm2[:, :, 0:m], op=ALU.subtract), r=[scB], w=[scB])
                op(dve, lambda e: e.tensor_tensor(out=Tm2[:, :, 0:m], in0=Ere[:, :, 0:m], in1=bi, op=ALU.mult), r=[scA, scB], w=[scB])
                op(dve, lambda e: e.tensor_tensor(out=Eim[:, :, m:2 * m], in0=Eim[:, :, 0:m], in1=br, op=ALU.mult), r=[scA], w=[scA])
                op(dve, lambda e: e.tensor_tensor(out=Eim[:, :, m:2 * m], in0=Eim[:, :, m:2 * m], in1=Tm2[:, :, 0:m], op=ALU.add), r=[scA, scB], w=[scA])
                op(dve, lambda e: e.tensor_copy(out=Ere[:, :, m:2 * m], in_=Tm[:, :, 0:m]), r=[scB], w=[scA])
                n_have *= 2
            MP = scB[:, 0:1024].rearrange("p (c k) -> p c k", c=8)
            MN = scB[:, 1024:2048].rearrange("p (c k) -> p c k", c=8)
            kb.dma("sp", lambda e: e.dma_start(out=scC[:, 0:128], in_=c_kk), w=[scC])
            kkb = scC[:, 0:128].unsqueeze(1).to_broadcast([128, 8, 128])
            op(dve, lambda e: e.tensor_tensor(out=MP, in0=kkb, in1=P["ldr"][:].unsqueeze(2).to_broadcast([128, 8, 128]), op=ALU.mult), r=[scC, P["ldr"]], w=[scB])
            op(act, lambda e: e.activation(out=MN, in_=MP, func=AF.Exp, scale=-1.0), r=[scB], w=[scB])
            op(act, lambda e: e.activation(out=MP, in_=MP, func=AF.Exp), r=[scB], w=[scB])
            op(dve, lambda e: e.tensor_tensor(out=a1t[:, 0, :, :], in0=MP, in1=Ere, op=ALU.mult), r=[scA, scB], w=[a1t])
            op(dve, lambda e: e.tensor_tensor(out=a1t[:, 1, :, :], in0=MP, in1=Eim, op=ALU.mult), r=[scA, scB], w=[a1t])
            op(dve, lambda e: e.tensor_scalar(out=P["are"][:], in0=a1t[:, 0, :, 0], scalar1=-1.0, scalar2=None, op0=ALU.add), r=[a1t], w=[P["are"]])
            op(dve, lambda e: e.tensor_copy(out=P["aim"][:], in_=a1t[:, 1, :, 0]), r=[a1t], w=[P["aim"]])
            op(dve, lambda e: e.tensor_tensor(out=P["l2"][:], in0=lre, in1=lre, op=ALU.mult), r=[par], w=[P["l2"]])
            op(dve, lambda e: e.tensor_tensor(out=P["t0"][:], in0=lim, in1=lim, op=ALU.mult), r=[par], w=[P["t0"]])
            op(dve, lambda e: e.tensor_tensor(out=P["l2"][:], in0=P["l2"][:], in1=P["t0"][:], op=ALU.add), r=[P["l2"], P["t0"]], w=[P["l2"]])
            op(dve, lambda e: e.reciprocal(out=P["l2"][:], in_=P["l2"][:]), r=[P["l2"]], w=[P["l2"]])
            op(dve, lambda e: e.tensor_tensor(out=P["t0"][:], in0=P["are"][:], in1=lre, op=ALU.mult), r=[P["are"], par], w=[P["t0"]])
            op(dve, lambda e: e.tensor_tensor(out=P["t1"][:], in0=P["aim"][:], in1=lim, op=ALU.mult), r=[P["aim"], par], w=[P["t1"]])
            op(dve, lambda e: e.tensor_tensor(out=P["t0"][:], in0=P["t0"][:], in1=P["t1"][:], op=ALU.add), r=[P["t0"], P["t1"]], w=[P["t0"]])
            op(dve, lambda e: e.tensor_tensor(out=P["kre"][:], in0=P["t0"][:], in1=P["l2"][:], op=ALU.mult), r=[P["t0"], P["l2"]], w=[P["kre"]])
            op(dve, lambda e: e.tensor_tensor(out=P["t0"][:], in0=P["aim"][:], in1=lre, op=ALU.mult), r=[P["aim"], par], w=[P["t0"]])
            op(dve, lambda e: e.tensor_tensor(out=P["t1"][:], in0=P["are"][:], in1=lim, op=ALU.mult), r=[P["are"], par], w=[P["t1"]])
            op(dve, lambda e: e.tensor_tensor(out=P["t0"][:], in0=P["t0"][:], in1=P["t1"][:], op=ALU.subtract), r=[P["t0"], P["t1"]], w=[P["t0"]])
            op(dve, lambda e: e.tensor_tensor(out=P["kim"][:], in0=P["t0"][:], in1=P["l2"][:], op=ALU.mult), r=[P["t0"], P["l2"]], w=[P["kim"]])
            GR = scC[:, :].rearrange("p (c k) -> p c k", c=8)
            kreb = P["kre"][:].unsqueeze(2).to_broadcast([128, 8, 128])
            kimb = P["kim"][:].unsqueeze(2).to_broadcast([128, 8, 128])
            for part in range(2):
                if part == 0:
                    op(dve, lambda e: e.tensor_tensor(out=GR, in0=Ere, in1=kreb, op=ALU.mult), r=[scA, P["kre"]], w=[scC])
                    op(dve, lambda e: e.tensor_tensor(out=MP, in0=Eim, in1=kimb, op=ALU.mult), r=[scA, P["kim"]], w=[scB])
                    op(dve, lambda e: e.tensor_tensor(out=GR, in0=GR, in1=MP, op=ALU.add), r=[scC, scB], w=[scC])
                else:
                    op(dve, lambda e: e.tensor_tensor(out=GR, in0=Ere, in1=kimb, op=ALU.mult), r=[scA, P["kim"]], w=[scC])
                    op(dve, lambda e: e.tensor_tensor(out=MP, in0=Eim, in1=kreb, op=ALU.mult), r=[scA, P["kre"]], w=[scB])
                    op(dve, lambda e: e.tensor_tensor(out=GR, in0=GR, in1=MP, op=ALU.subtract), r=[scC, scB], w=[scC])
                op(dve, lambda e: e.tensor_tensor(out=GR, in0=GR, in1=MN, op=ALU.mult), r=[scC, scB], w=[scC])
                for half in range(2):
                    pp = ps1()

                    def f(e, half=half, pp=pp):
                        ins = None
                        for cc in range(4):
                            c = half * 4 + cc
                            ins = e.transpose(out=pp[:, cc * 128:(cc + 1) * 128], in_=scC[:, c * 128:(c + 1) * 128], identity=identF[:])
                        return ins
                    op(pe, f, r=[scC, identF], w=[pp])
                    op(act, lambda e, half=half, pp=pp, part=part: e.activation(out=ainvk[:, part, half * 512:(half + 1) * 512], in_=pp[:, 0:512], func=AF.Copy), r=[pp], w=[ainvk])

        def tile_geom(n):
            sample = n >= NTP
            M = 64 if sample else 128
            row0 = NTP * 128 + (n - NTP) * 64 if sample else n * 128
            return sample, M, row0

        def p1a(li, n):
            sample = n >= NTP
            M = 64 if sample else 128
            row0 = NTP * 128 + (n - NTP) * 64 if sample else n * 128
            seq = (n - NTP + 1) if sample else 0
            last_of_seq = sample or (n == NTP - 1)
            src_x = xin if li == 0 else X1
            dst_x = X1 if li == 0 else y_out
            kb.dma("sp", lambda e: e.dma_start(out=x_f[0:M, :], in_=src_x[row0:row0 + M, :]), w=[x_f])
            if n == 0:
                op(pool, lambda e: e.memset(hst[:], 0.0), w=[hst])
                op(pool, lambda e: e.memset(sret_f[:], 0.0), w=[sret_f])
                op(pool, lambda e: e.memset(sret_b[:], 0.0), w=[sret_b])
            if sample:
                s = n - NTP
                kb.dma("sp", lambda e: e.dma_start(out=hst[:], in_=st_ssm[li, s]), w=[hst])
                kb.dma("sp", lambda e: e.dma_start(out=sret_f[:], in_=st_ret[li, s]), w=[sret_f])
                op(act, lambda e: e.activation(out=sret_b[:].rearrange("p h e -> p (h e)"), in_=sret_f[:], func=AF.Copy), r=[sret_f], w=[sret_b])
                kc_b = bigb[:, :].rearrange("p (t c) -> p t c", t=4)
                kb.dma(pool, lambda e: e.dma_start(out=kc_b, in_=cache_k[li, s].rearrange("(t p) c -> p t c", p=128)), w=[bigb])
                kb.dma(pool, lambda e: e.dma_start(out=vr[:, 0:4, :], in_=cache_v[li, s].rearrange("(t p) c -> p t c", p=128)), w=[vr])
                for t_ in range(4):
                    pb = psb()

                    def f(e, t_=t_, pb=pb):
                        ins = None
                        for hp in range(4):
                            ins = e.transpose(out=pb[:, hp * 128:(hp + 1) * 128], in_=kc_b[:, t_, hp * 128:(hp + 1) * 128], identity=ident[:])
                        return ins
                    op(pe, f, r=[bigb, ident], w=[pb])
                    op(dve, lambda e, t_=t_, pb=pb: e.tensor_copy(out=kt2[:, :, t_ * 128:(t_ + 1) * 128], in_=pb[:, 0:512].rearrange("p (h m) -> p h m", h=4)), r=[pb], w=[kt2])
            if sample:
                wslot, ws, boff = 4, 0, 0
                Wn = 576
            else:
                wslot = 4 + (n % 5)
                if n > 0 and n % 5 == 0:
                    op(pool, lambda e: e.tensor_copy(out=kt2[:, :, 0:512], in_=kt2[:, :, 640:1152]), r=[kt2], w=[kt2])
                    op(pool, lambda e: e.tensor_copy(out=vr[:, 0:4, :], in_=vr[:, 5:9, :]), r=[vr], w=[vr])
                nvalid = min(n, 4)
                ws = wslot - nvalid
                boff = (4 - nvalid) * 128
                Wn = (nvalid + 1) * 128

            op(act, lambda e: e.activation(out=xb[0:M, :], in_=x_f[0:M, :], func=AF.Copy), r=[x_f], w=[xb])
            transposes(xb, 8, M, xT, xT[:, :, 0:M])
            yield

            want_kv = sample or (n >= NTP - 4)
            kv_row = (512 + (n - NTP) * 64) if sample else (n - (NTP - 4)) * 128
            b4 = None
            for bi, (c0, N) in enumerate(IN_BLOCKS if "inproj" in STAGES else []):
                wt = wget(("w_in", li, 0, 8, c0, N))
                pp = ps1()

                def f(e, wt=wt, pp=pp, N=N):
                    ins = None
                    for k in range(8):
                        ins = e.matmul(pp[0:M, 0:N], lhsT=xT[:, k, 0:M], rhs=wt[:, k, 0:N], start=(k == 0), stop=(k == 7))
                    return ins
                op(pe, f, r=[xT, wt], w=[pp])
                if bi == 0:
                    op(act, lambda e, pp=pp: e.activation(out=u_f[0:M, :], in_=pp[0:M, 0:256], func=AF.Copy), r=[pp], w=[u_f])
                    op(act, lambda e, pp=pp: e.activation(out=u_b[0:M, :], in_=pp[0:M, 0:256], func=AF.Copy), r=[pp], w=[u_b])
                elif bi == 1:
                    op(act, lambda e, pp=pp: e.activation(out=qa_b[0:M, :], in_=pp[0:M, 0:512], func=AF.Copy), r=[pp], w=[qa_b])
                elif bi == 2:
                    op(act, lambda e, pp=pp: e.activation(out=ka_b[0:M, :], in_=pp[0:M, 0:512], func=AF.Copy), r=[pp], w=[ka_b])
                    if want_kv:
                        op(act, lambda e, pp=pp: e.activation(out=scC[0:M, 0:512], in_=pp[0:M, 0:512], func=AF.Copy), r=[pp], w=[scC])
                elif bi == 3:
                    op(act, lambda e, pp=pp: e.activation(out=vr[0:M, wslot, :], in_=pp[0:M, 0:512], func=AF.Copy), r=[pp], w=[vr])
                    if want_kv:
                        op(act, lambda e, pp=pp: e.activation(out=scC[0:M, 512:1024], in_=pp[0:M, 0:512], func=AF.Copy), r=[pp], w=[scC])
                        kb.dma("sp", lambda e: e.dma_start(out=k_out[li, kv_row:kv_row + M, :], in_=scC[0:M, 0:512]), r=[scC], is_out=True)
                        kb.dma("sp", lambda e: e.dma_start(out=v_out[li, kv_row:kv_row + M, :], in_=scC[0:M, 512:1024]), r=[scC], is_out=True)
                elif bi == 4:
                    b4 = pp
                    rope_and_ret_prep(b4, M, n)
                elif bi == 5:
                    op(act, lambda e, pp=pp: e.activation(out=vr_b[0:M, :], in_=pp[0:M, 0:256], func=AF.Copy), r=[pp], w=[vr_b])
                    op(act, lambda e, pp=pp: e.activation(out=sg_f[0:M, :], in_=pp[0:M, 256:512], func=AF.Silu), r=[pp], w=[sg_f])
                else:
                    g = (bi - 6) // 2
                    hh = (bi - 6) % 2
                    op(act, lambda e, pp=pp, g=g, hh=hh: e.activation(out=gates_b[0:M, g, hh * 512:(hh + 1) * 512], in_=pp[0:M, 0:512], func=AF.Sigmoid), r=[pp], w=[gates_b])
                yield

            tap = (li, n) == DBG_TILE
            if tap:
                dump("u_f", u_f[:, :], [u_f])
                dump("gates", gates_b[:, :, :].rearrange("p a b -> p (a b)"), [gates_b])
            if "s5" in STAGES:
                s5_stage(M)
                yield
            if "attn" in STAGES:
                yield from attn_stage(M, wslot, ws, boff, Wn)
            if "ret" in STAGES:
                ret_stage(M, sample)
                yield
            if tap:
                dump("yssm", yssm_b[:, :], [yssm_b])
                dump("yatt", yatt_b[:, :], [yatt_b])
                dump("yret", yret_b[:, :], [yret_b])
            if last_of_seq:
                kb.dma("sp", lambda e: e.dma_start(out=ssm_out[li, seq], in_=hst[:]), r=[hst], is_out=True)
                kb.dma("sp", lambda e: e.dma_start(out=ret_out[li, seq], in_=sret_f[:]), r=[sret_f], is_out=True)
            if "merge" in STAGES:
                yield from merge_stage(li, M, X1P[n % 2])

        def p2_tail(li, n):
            sample, M, row0 = tile_geom(n)
            dst_x = X1 if li == 0 else y_out
            XP = X1P[n % 2]
            peer_tail(li, M, XP)
            if "ple" in STAGES:
                ple_stage(li, M, row0, XP)
            kb.dma("sp", lambda e: e.dma_start(out=dst_x[row0:row0 + M, :], in_=XP[0:M, :]), r=[XP], is_out=True)

        def rope_and_ret_prep(b4, M, n):
            v4 = b4[0:M, 0:512].rearrange("p (g t f) -> p g t f", g=8, t=2)
            x1 = v4[:, :, 0, :]
            x2 = v4[:, :, 1, :]
            kb.dma("sp", lambda e: e.dma_start(out=cs_t[0:M, 0, :], in_=c_cos[0:M, n * 32:(n + 1) * 32]), w=[cs_t])
            kb.dma("sp", lambda e: e.dma_start(out=cs_t[0:M, 1, :], in_=c_sin[0:M, n * 32:(n + 1) * 32]), w=[cs_t])
            cosb = cs_t[0:M, 0, :].unsqueeze(1).to_broadcast([M, 8, 32])
            sinb = cs_t[0:M, 1, :].unsqueeze(1).to_broadcast([M, 8, 32])
            R = scC[0:M, 0:512].rearrange("p (g t f) -> p g t f", g=8, t=2)
            T = scC[0:M, 512:1024].rearrange("p (g t f) -> p g t f", g=8, t=2)
            op(dve, lambda e: e.tensor_tensor(out=R[:, :, 0, :], in0=x1, in1=cosb, op=ALU.mult), r=[b4, cs_t], w=[scC])
            op(dve, lambda e: e.tensor_tensor(out=T[:, :, 0, :], in0=x2, in1=sinb, op=ALU.mult), r=[b4, cs_t], w=[scC])
            op(dve, lambda e: e.tensor_tensor(out=R[:, :, 1, :], in0=x1, in1=sinb, op=ALU.mult), r=[b4, cs_t], w=[scC])
            op(dve, lambda e: e.tensor_tensor(out=T[:, :, 1, :], in0=x2, in1=cosb, op=ALU.mult), r=[b4, cs_t], w=[scC])
            op(dve, lambda e: e.tensor_tensor(out=R[:, :, 0, :], in0=R[:, :, 0, :], in1=T[:, :, 0, :], op=ALU.subtract), r=[scC], w=[scC])
            op(dve, lambda e: e.tensor_tensor(out=R[:, :, 1, :], in0=R[:, :, 1, :], in1=T[:, :, 1, :], op=ALU.add), r=[scC], w=[scC])
            op(act, lambda e: e.activation(out=qk_b[0:M, :], in_=scC[0:M, 0:512], func=AF.Copy), r=[scC], w=[qk_b])
            kwc = 8 if M == 64 else 4
            qv = scC[0:M, 0:256].rearrange("p (h d) -> p h d", h=4)
            kv = scC[0:M, 256:512].rearrange("p (h d) -> p h d", h=4)
            op(dve, lambda e: e.tensor_tensor(out=qs_b[0:M, :].rearrange("p (h d) -> p h d", h=4), in0=qv, in1=cret[0:M, 0:4].unsqueeze(2).to_broadcast([M, 4, 64]), op=ALU.mult), r=[scC, cret], w=[qs_b])
            op(dve, lambda e: e.tensor_tensor(out=kk_b[0:M, :].rearrange("p (h d) -> p h d", h=4), in0=kv, in1=cret[0:M, kwc:kwc + 4].unsqueeze(2).to_broadcast([M, 4, 64]), op=ALU.mult), r=[scC, cret], w=[kk_b])

        def s5_stage(M):
            transposes(u_b, 2, M, uT, uT[:, :, 0:M])
            bu = [ps2(), ps2()]
            for half in range(2):
                def f(e, half=half):
                    ins = None
                    for cb in range(2):
                        ins = e.matmul(bu[half][0:M, cb * 512:(cb + 1) * 512], lhsT=uT[:, cb, 0:M], rhs=bbd_b[:, cb, half, :], start=True, stop=True)
                    return ins
                op(pe, f, r=[uT, bbd_b], w=[bu[half]])
            t1 = scA[0:M, 0:1024]
            t2 = scA[0:M, 1024:2048]
            op(dve, lambda e: e.tensor_tensor(out=t1, in0=bu[0][0:M, :], in1=ainvk[0:M, 0, :], op=ALU.mult), r=[bu[0], ainvk], w=[scA])
            op(dve, lambda e: e.tensor_tensor(out=t2, in0=bu[1][0:M, :], in1=ainvk[0:M, 1, :], op=ALU.mult), r=[bu[1], ainvk], w=[scA])
            op(dve, lambda e: e.tensor_tensor(out=bigb[0:M, 0:1024], in0=t1, in1=t2, op=ALU.subtract), r=[scA], w=[bigb])
            op(dve, lambda e: e.tensor_tensor(out=t1, in0=bu[0][0:M, :], in1=ainvk[0:M, 1, :], op=ALU.mult), r=[bu[0], ainvk, bigb], w=[scA])
            op(dve, lambda e: e.tensor_tensor(out=t2, in0=bu[1][0:M, :], in1=ainvk[0:M, 0, :], op=ALU.mult), r=[bu[1], ainvk], w=[scA])
            op(dve, lambda e: e.tensor_tensor(out=bigb[0:M, 1024:2048], in0=t1, in1=t2, op=ALU.add), r=[scA], w=[bigb])
            cs = [ps2(), ps2()]
            for half in range(2):
                def f(e, half=half):
                    ins = None
                    for c in range(8):
                        ins = e.matmul(cs[half][:, c * 128:c * 128 + M], lhsT=bigb[0:M, half * 1024 + c * 128:half * 1024 + (c + 1) * 128], rhs=l2t[0:M, 0:M], start=True, stop=True)
                    return ins
                op(pe, f, r=[bigb, l2t], w=[cs[half]])
            tre = scA[:, 0:1024].rearrange("p (c i) -> p c i", c=8)[:, :, 0:M]
            tim = scA[:, 1024:2048].rearrange("p (c i) -> p c i", c=8)[:, :, 0:M]
            hre = scB[:, 0:1024].rearrange("p (c i) -> p c i", c=8)[:, :, 0:M]
            him = scB[:, 1024:2048].rearrange("p (c i) -> p c i", c=8)[:, :, 0:M]
            tmp = scC[:, 0:1024].rearrange("p (c i) -> p c i", c=8)[:, :, 0:M]
            csv = [cs[h_][:, :].rearrange("p (c i) -> p c i", c=8)[:, :, 0:M] for h_ in range(2)]
            op(dve, lambda e: e.tensor_tensor(out=tre, in0=csv[0], in1=hst[:, 0:8].unsqueeze(2).to_broadcast([128, 8, M]), op=ALU.add), r=[cs[0], hst], w=[scA])
            op(dve, lambda e: e.tensor_tensor(out=tim, in0=csv[1], in1=hst[:, 8:16].unsqueeze(2).to_broadcast([128, 8, M]), op=ALU.add), r=[cs[1], hst], w=[scA])
            a_re = a1t[:, 0, :, 0:M]
            a_im = a1t[:, 1, :, 0:M]
            op(dve, lambda e: e.tensor_tensor(out=hre, in0=tre, in1=a_re, op=ALU.mult), r=[scA, a1t], w=[scB])
            op(dve, lambda e: e.tensor_tensor(out=tmp, in0=tim, in1=a_im, op=ALU.mult), r=[scA, a1t], w=[scC])
            op(dve, lambda e: e.tensor_tensor(out=hre, in0=hre, in1=tmp, op=ALU.subtract), r=[scB, scC], w=[scB])
            op(dve, lambda e: e.tensor_tensor(out=him, in0=tre, in1=a_im, op=ALU.mult), r=[scA, a1t], w=[scB])
            op(dve, lambda e: e.tensor_tensor(out=tmp, in0=tim, in1=a_re, op=ALU.mult), r=[scA, a1t, scB], w=[scC])
            op(dve, lambda e: e.tensor_tensor(out=him, in0=him, in1=tmp, op=ALU.add), r=[scB, scC], w=[scB])
            hT_b = bigb[:, :].rearrange("p (c i) -> p c i", c=16)
            op(act, lambda e: e.activation(out=hT_b[:, 0:8, 0:M], in_=hre, func=AF.Copy), r=[scB], w=[bigb])
            op(act, lambda e: e.activation(out=hT_b[:, 8:16, 0:M], in_=him, func=AF.Copy, scale=-1.0), r=[scB], w=[bigb])
            op(dve, lambda e: e.tensor_copy(out=hst[:, 0:8], in_=hre[:, :, M - 1]), r=[scB], w=[hst])
            op(dve, lambda e: e.tensor_copy(out=hst[:, 8:16], in_=him[:, :, M - 1]), r=[scB], w=[hst])
            yp = ps1()

            def f(e):
                ins = None
                for c in range(8):
                    e.matmul(yp[0:M, 32 * c:32 * c + 32], lhsT=hT_b[:, c, 0:M], rhs=cm_b[:, c, 0, :], start=True, stop=False)
                    ins = e.matmul(yp[0:M, 32 * c:32 * c + 32], lhsT=hT_b[:, 8 + c, 0:M], rhs=cm_b[:, c, 1, :], start=False, stop=True)
                return ins
            op(pe, f, r=[bigb, cm_b], w=[yp])
            yf = scA[0:M, 0:256]
            zf = scA[0:M, 256:512]
            tz = scA[0:M, 512:768]
            op(dve, lambda e: e.tensor_tensor(out=yf, in0=u_f[0:M, :], in1=rows_s[0:M, R_DSK:R_DSK + 256], op=ALU.mult), r=[u_f, rows_s], w=[scA])
            op(dve, lambda e: e.tensor_tensor(out=yf, in0=yf, in1=yp[0:M, 0:256], op=ALU.add), r=[scA, yp], w=[scA])
            gelu_tanh(yf, zf, tz, [scA], [scA], scA)
            zb = u_b
            op(act, lambda e: e.activation(out=zb[0:M, :], in_=zf, func=AF.Copy), r=[scA, uT], w=[zb])
            transposes(zb, 2, M, uT, uT[:, :, 0:M])
            gp = ps1()

            def f2(e):
                ins = None
                for k in range(2):
                    ins = e.matmul(gp[0:M, 0:256], lhsT=uT[:, k, 0:M], rhs=wglu_b[:, k, :], start=(k == 0), stop=(k == 1))
                return ins
            op(pe, f2, r=[uT, wglu_b], w=[gp])
            op(dve, lambda e: e.tensor_tensor(out=tz, in0=gp[0:M, 0:256], in1=rows_s[0:M, R_BGLU:R_BGLU + 256], op=ALU.add), r=[gp, rows_s], w=[scA])
            op(act, lambda e: e.activation(out=tz, in_=tz, func=AF.Sigmoid), r=[scA], w=[scA])
            op(dve, lambda e: e.tensor_tensor(out=yssm_b[0:M, :], in0=zf, in1=tz, op=ALU.mult), r=[scA], w=[yssm_b])

        def attn_stage(M, wslot, ws, boff, Wn):
            transposes(qa_b, 4, M, qT2, qT2[:, :, 0:M])
            transposes(ka_b, 4, M, kt2, kt2[:, :, wslot * 128:wslot * 128 + M])
            c0 = ws * 128
            ops_ = ps_acc()
            S_sb = scA[0:M, 0:Wn]
            nblk = (Wn + 127) // 128
            for h in range(8):
                sp2 = ps2()

                def f(e, h=h, sp2=sp2):
                    n1 = min(512, Wn)
                    po = (h % 2) * 64
                    hp = h // 2
                    ins = e.matmul(sp2[0:M, 0:n1], lhsT=qT2[po:po + 64, hp, 0:M], rhs=kt2[po:po + 64, hp, c0:c0 + n1], start=True, stop=True)
                    if Wn > 512:
                        ins = e.matmul(sp2[0:M, 512:Wn], lhsT=qT2[po:po + 64, hp, 0:M], rhs=kt2[po:po + 64, hp, c0 + 512:c0 + Wn], start=True, stop=True)
                    return ins
                op(pe, f, r=[qT2, kt2], w=[sp2])
                op(dve, lambda e, h=h, sp2=sp2: e.scalar_tensor_tensor(out=S_sb, in0=sp2[0:M, 0:Wn], scalar=0.125, in1=bias_b[0:M, h, boff:boff + Wn], op0=ALU.mult, op1=ALU.add), r=[sp2, bias_b], w=[scA])
                op(dve, lambda e: e.reduce_max(out=sm[0:M, 16:17], in_=S_sb, axis=AX.X), r=[scA], w=[sm])
                op(dve, lambda e: e.tensor_scalar(out=sm[0:M, 17:18], in0=sm[0:M, 16:17], scalar1=-1.0, scalar2=None, op0=ALU.mult), r=[sm], w=[sm])
                op(act, lambda e, h=h: e.activation(out=pb_att[0:M, 0:Wn], in_=S_sb, func=AF.Exp, bias=sm[0:M, 17:18], accum_out=sm[0:M, 24 + h:25 + h]), r=[scA, sm], w=[pb_att, sm])
                pbt = psb()

                def f(e, pbt=pbt):
                    ins = None
                    for b in range(nblk):
                        bw = min(128, Wn - b * 128)
                        ins = e.transpose(out=pbt[0:bw, b * 128:b * 128 + M], in_=pb_att[0:M, b * 128:b * 128 + bw], identity=ident[0:M, 0:M])
                    return ins
                op(pe, f, r=[pb_att, ident], w=[pbt])
                lastw = Wn - (nblk - 1) * 128
                if lastw == 128:
                    op(dve, lambda e, pbt=pbt: e.tensor_copy(out=pT[:, 0:nblk, 0:M], in_=pbt[:, 0:nblk * 128].rearrange("p (b m) -> p b m", m=128)[:, :, 0:M]), r=[pbt], w=[pT])
                else:
                    op(dve, lambda e, pbt=pbt: e.tensor_copy(out=pT[:, 0:nblk - 1, 0:M], in_=pbt[:, 0:(nblk - 1) * 128].rearrange("p (b m) -> p b m", m=128)[:, :, 0:M]), r=[pbt], w=[pT])
                    op(dve, lambda e, pbt=pbt: e.tensor_copy(out=pT[0:lastw, nblk - 1, 0:M], in_=pbt[0:lastw, (nblk - 1) * 128:(nblk - 1) * 128 + M]), r=[pbt], w=[pT])

                def f(e, h=h):
                    ins = None
                    for b in range(nblk):
                        bw = min(128, Wn - b * 128)
                        ins = e.matmul(ops_[0:M, h * 64:(h + 1) * 64], lhsT=pT[0:bw, b, 0:M], rhs=vr[0:bw, ws + b, h * 64:(h + 1) * 64], start=(b == 0), stop=(b == nblk - 1))
                    return ins
                op(pe, f, r=[pT, vr], w=[ops_])
                yield
            op(dve, lambda e: e.reciprocal(out=sm[0:M, 32:40], in_=sm[0:M, 24:32]), r=[sm], w=[sm])
            op(dve, lambda e: e.tensor_tensor(out=yatt_b[0:M, :].rearrange("p (h d) -> p h d", h=8), in0=ops_[0:M, 0:512].rearrange("p (h d) -> p h d", h=8), in1=sm[0:M, 32:40].unsqueeze(2).to_broadcast([M, 8, 64]), op=ALU.mult), r=[ops_, sm], w=[yatt_b])

        def ret_stage(M, sample):
            pb = psb()

            def f(e):
                ins = None
                for g in range(8):
                    ins = e.transpose(out=pb[0:64, g * 128:g * 128 + M], in_=qk_b[0:M, g * 64:(g + 1) * 64], identity=ident[0:M, 0:M])
                return ins
            op(pe, f, r=[qk_b, ident], w=[pb])
            op(dve, lambda e: e.tensor_copy(out=qkT[:, :, 0:M], in_=pb[0:64, :].rearrange("p (h m) -> p h m", h=8)[:, :, 0:M]), r=[pb], w=[qkT])
            pb2 = psb()

            def f(e):
                ins = None
                for g in range(4):
                    ins = e.transpose(out=pb2[0:64, g * 128:g * 128 + M], in_=qs_b[0:M, g * 64:(g + 1) * 64], identity=ident[0:M, 0:M])
                return ins
            op(pe, f, r=[qs_b, ident], w=[pb2])
            op(dve, lambda e: e.tensor_copy(out=qsT[:, :, 0:M], in_=pb2[0:64, 0:512].rearrange("p (h m) -> p h m", h=4)[:, :, 0:M]), r=[pb2], w=[qsT])
            scp = ps1()

            def f(e):
                ins = None
                for h in range(4):
                    ins = e.matmul(scp[0:M, h * 128:h * 128 + M], lhsT=qkT[:, 4 + h, 0:M], rhs=qkT[:, h, 0:M], start=True, stop=True)
                return ins
            op(pe, f, r=[qkT], w=[scp])
            op(dve, lambda e: e.tensor_tensor(out=scT_b[0:M, :, 0:M], in0=scp[0:M, 0:512].rearrange("p (h i) -> p h i", h=4)[:, :, 0:M], in1=dtt[0:M, :, 0:M], op=ALU.mult), r=[scp, dtt], w=[scT_b])
            opp = ps1()

            def f(e):
                ins = None
                for h in range(4):
                    e.matmul(opp[0:M, h * 64:(h + 1) * 64], lhsT=scT_b[0:M, h, 0:M], rhs=vr_b[0:M, h * 64:(h + 1) * 64], start=True, stop=False)
                    ins = e.matmul(opp[0:M, h * 64:(h + 1) * 64], lhsT=qsT[:, h, 0:M], rhs=sret_b[:, h, :], start=False, stop=True)
                return ins
            op(pe, f, r=[scT_b, vr_b, qsT, sret_b], w=[opp])
            cp = ps1()

            def f(e):
                ins = None
                for h in range(4):
                    ins = e.matmul(cp[0:64, h * 64:(h + 1) * 64], lhsT=kk_b[0:M, h * 64:(h + 1) * 64], rhs=vr_b[0:M, h * 64:(h + 1) * 64], start=True, stop=True)
                return ins
            op(pe, f, r=[kk_b, vr_b], w=[cp])
            di = 1 if sample else 0
            op(dve, lambda e: e.tensor_tensor(out=sret_f[:], in0=sret_f[:], in1=dect[:, di, :], op=ALU.mult), r=[sret_f, dect], w=[sret_f])
            op(dve, lambda e: e.tensor_tensor(out=sret_f[:], in0=sret_f[:], in1=cp[0:64, 0:256], op=ALU.add), r=[sret_f, cp], w=[sret_f])
            op(act, lambda e: e.activation(out=sret_b[:].rearrange("p h e -> p (h e)"), in_=sret_f[:], func=AF.Copy), r=[sret_f], w=[sret_b])
            o3 = opp[0:M, 0:256].rearrange("p (h e) -> p h e", h=4)
            oc = scA[0:M, 0:256].rearrange("p (h e) -> p h e", h=4)
            sq = scA[0:M, 256:512].rearrange("p (h e) -> p h e", h=4)
            op(dve, lambda e: e.tensor_reduce(out=sm[0:M, 40:44], in_=o3, axis=AX.X, op=ALU.add), r=[opp], w=[sm])
            op(dve, lambda e: e.tensor_scalar(out=sm[0:M, 44:48], in0=sm[0:M, 40:44], scalar1=-1.0 / 64, scalar2=None, op0=ALU.mult), r=[sm], w=[sm])
            op(dve, lambda e: e.tensor_tensor(out=oc, in0=o3, in1=sm[0:M, 44:48].unsqueeze(2).to_broadcast([M, 4, 64]), op=ALU.add), r=[opp, sm], w=[scA])
            op(dve, lambda e: e.tensor_tensor(out=sq, in0=oc, in1=oc, op=ALU.mult), r=[scA], w=[scA])
            op(dve, lambda e: e.tensor_reduce(out=sm[0:M, 48:52], in_=sq, axis=AX.X, op=ALU.add), r=[scA], w=[sm])
            op(dve, lambda e: e.tensor_scalar(out=sm[0:M, 48:52], in0=sm[0:M, 48:52], scalar1=1.0 / 64, scalar2=LN_EPS, op0=ALU.mult, op1=ALU.add), r=[sm], w=[sm])
            op(act, lambda e: e.activation(out=sm[0:M, 52:56], in_=sm[0:M, 48:52], func=AF.Sqrt), r=[sm], w=[sm])
            op(dve, lambda e: e.reciprocal(out=sm[0:M, 56:60], in_=sm[0:M, 52:56]), r=[sm], w=[sm])
            op(dve, lambda e: e.tensor_tensor(out=oc, in0=oc, in1=sm[0:M, 56:60].unsqueeze(2).to_broadcast([M, 4, 64]), op=ALU.mult), r=[scA, sm], w=[scA])
            op(dve, lambda e: e.tensor_tensor(out=scA[0:M, 0:256], in0=scA[0:M, 0:256], in1=rows_s[0:M, R_GNG:R_GNG + 256], op=ALU.mult), r=[scA, rows_s], w=[scA])
            op(dve, lambda e: e.tensor_tensor(out=yret_b[0:M, :], in0=scA[0:M, 0:256], in1=sg_f[0:M, :], op=ALU.mult), r=[scA, sg_f], w=[yret_b])

        def merge_stage(li, M, XP):
            transposes(yssm_b, 2, M, yT, yT[:, 0:2, 0:M])
            transposes(yatt_b, 4, M, yT, yT[:, 2:6, 0:M])
            transposes(yret_b, 2, M, yT, yT[:, 6:8, 0:M])
            merged = scB[0:M, 0:1024]
            tmpm = scB[0:M, 1024:2048]
            for bi, (name, K, k0) in enumerate((("w_br_ssm", 2, 0), ("w_br_att", 4, 2), ("w_br_ret", 2, 6))):
                for hh in range(2):
                    wt = wget((name, li, 0, K, hh * 512, 512))
                    pp = ps1()

                    def f(e, wt=wt, pp=pp, K=K, k0=k0):
                        ins = None
                        for k in range(K):
                            ins = e.matmul(pp[0:M, 0:512], lhsT=yT[:, k0 + k, 0:M], rhs=wt[:, k, 0:512], start=(k == 0), stop=(k == K - 1))
                        return ins
                    op(pe, f, r=[yT, wt], w=[pp])
                    if bi == 0:
                        op(dve, lambda e, pp=pp, hh=hh: e.tensor_tensor(out=merged[:, hh * 512:(hh + 1) * 512], in0=pp[0:M, 0:512], in1=gates_b[0:M, 0, hh * 512:(hh + 1) * 512], op=ALU.mult), r=[pp, gates_b], w=[scB])
                    else:
                        op(dve, lambda e, pp=pp, hh=hh, bi=bi: e.tensor_tensor(out=tmpm[:, hh * 512:(hh + 1) * 512], in0=pp[0:M, 0:512], in1=gates_b[0:M, bi, hh * 512:(hh + 1) * 512], op=ALU.mult), r=[pp, gates_b], w=[scB])
                        op(dve, lambda e, hh=hh: e.tensor_tensor(out=merged[:, hh * 512:(hh + 1) * 512], in0=merged[:, hh * 512:(hh + 1) * 512], in1=tmpm[:, hh * 512:(hh + 1) * 512], op=ALU.add), r=[scB], w=[scB])
                    yield
            op(act, lambda e: e.activation(out=xb[0:M, :], in_=merged, func=AF.Copy), r=[scB], w=[xb])
            transposes(xb, 8, M, xT, xT[:, :, 0:M])
            for hh in range(2):
                wt = wget(("w_o", li, 0, 8, hh * 512, 512))
                pp = ps1()

                def f(e, wt=wt, pp=pp):
                    ins = None
                    for k in range(8):
                        ins = e.matmul(pp[0:M, 0:512], lhsT=xT[:, k, 0:M], rhs=wt[:, k, 0:512], start=(k == 0), stop=(k == 7))
                    return ins
                op(pe, f, r=[xT, wt], w=[pp])
                op(dve, lambda e, pp=pp, hh=hh: e.scalar_tensor_tensor(out=XP[0:M, hh * 512:(hh + 1) * 512], in0=x_f[0:M, hh * 512:(hh + 1) * 512], scalar=ALPHA, in1=pp[0:M, 0:512], op0=ALU.mult, op1=ALU.add), r=[x_f, pp], w=[XP])
                yield

        def peer_front(li, M, XP, EI, GT):
            layer_norm(XP, 0, XP, M)
            op(act, lambda e: e.activation(out=xb[0:M, :], in_=XP[0:M, :], func=AF.Copy), r=[XP], w=[xb])
            transposes(xb, 8, M, xT, xT[:, :, 0:M])
            qTp = bigb[:, :].rearrange("p (g m) -> p g m", g=16)
            for cb in range(4):
                wt = wget(("w_q", li, 0, 8, cb * 512, 512))
                pp = ps1()

                def f(e, wt=wt, pp=pp):
                    ins = None
                    for g in range(4):
                        for k in range(8):
                            ins = e.matmul(pp[:, g * 128:g * 128 + M], lhsT=wt[:, k, g * 128:(g + 1) * 128], rhs=xT[:, k, 0:M], start=(k == 0), stop=(k == 7))
                    return ins
                op(pe, f, r=[xT, wt], w=[pp])
                op(act, lambda e, pp=pp, cb=cb: e.activation(out=qTp[:, cb * 4:(cb + 1) * 4, 0:M], in_=pp[:, 0:512].rearrange("p (g m) -> p g m", g=4)[:, :, 0:M], func=AF.Copy), r=[pp], w=[bigb])
            s_f = scA
            for half in range(2):
                sp2 = ps2()

                def f(e, half=half, sp2=sp2):
                    ins = None
                    for gg in range(8):
                        g = half * 8 + gg
                        ins = e.matmul(sp2[0:M, gg * 128:(gg + 1) * 128], lhsT=qTp[:, g, 0:M], rhs=keys_b[:, g, :], start=True, stop=True)
                    return ins
                op(pe, f, r=[bigb, keys_b], w=[sp2])
                op(act, lambda e, half=half, sp2=sp2: e.activation(out=s_f[0:M, half * 1024:(half + 1) * 1024], in_=sp2[0:M, :], func=AF.Copy), r=[sp2], w=[scA])
            svs = [s_f[0:M, g * 128:(g + 1) * 128] for g in range(16)]
            for g in range(16):
                op(dve, lambda e, g=g: e.max(out=topa[0:M, g, 0:8], in_=svs[g]), r=[scA.lane(g)], w=[topa.lane(g)])
            for g in range(16):
                op(dve, lambda e, g=g: e.max_index(out=topi[0:M, g, 0:8], in_max=topa[0:M, g, 0:8], in_values=svs[g]), r=[scA.lane(g), topa.lane(g)], w=[topi.lane(g)])
            for g in range(16):
                op(dve, lambda e, g=g: e.match_replace(out=svs[g], in_to_replace=topa[0:M, g, 0:8], in_values=svs[g], imm_value=-1e30), r=[scA.lane(g), topa.lane(g)], w=[scA.lane(g)])
            for g in range(16):
                op(dve, lambda e, g=g: e.max(out=topa[0:M, g, 8:16], in_=svs[g]), r=[scA.lane(g)], w=[topa.lane(g)])
            for g in range(16):
                op(dve, lambda e, g=g: e.max_index(out=topi[0:M, g, 8:16], in_max=topa[0:M, g, 8:16], in_values=svs[g]), r=[scA.lane(g), topa.lane(g)], w=[topi.lane(g)])
            cand = scB[0:M, :].rearrange("p (h a b) -> p h a b", h=8, a=16)
            ta = topa[0:M, :, :].rearrange("p (h s) k -> p h s k", s=2)
            op(dve, lambda e: e.tensor_tensor(out=cand, in0=ta[:, :, 0, :].unsqueeze(3).to_broadcast([M, 8, 16, 16]), in1=ta[:, :, 1, :].unsqueeze(2).to_broadcast([M, 8, 16, 16]), op=ALU.add), r=[topa], w=[scB])
            cvs = [scB[0:M, h * 256:(h + 1) * 256] for h in range(8)]
            for h in range(8):
                op(dve, lambda e, h=h: e.max(out=top2[0:M, h, 0:8], in_=cvs[h]), r=[scB.lane(h)], w=[top2.lane(h)])
            for h in range(8):
                op(dve, lambda e, h=h: e.max_index(out=pos2[0:M, h, 0:8], in_max=top2[0:M, h, 0:8], in_values=cvs[h]), r=[scB.lane(h), top2.lane(h)], w=[pos2.lane(h)])
            for h in range(8):
                op(dve, lambda e, h=h: e.match_replace(out=cvs[h], in_to_replace=top2[0:M, h, 0:8], in_values=cvs[h], imm_value=-1e30), r=[scB.lane(h), top2.lane(h)], w=[scB.lane(h)])
            for h in range(8):
                op(dve, lambda e, h=h: e.max(out=top2[0:M, h, 8:16], in_=cvs[h]), r=[scB.lane(h)], w=[top2.lane(h)])
            for h in range(8):
                op(dve, lambda e, h=h: e.max_index(out=pos2[0:M, h, 8:16], in_max=top2[0:M, h, 8:16], in_values=cvs[h]), r=[scB.lane(h), top2.lane(h)], w=[pos2.lane(h)])
            posf = scC[0:M, 0:128].rearrange("p (h j) -> p h j", h=8)
            k1f = scC[0:M, 128:256].rearrange("p (h j) -> p h j", h=8)
            k2f = scC[0:M, 256:384].rearrange("p (h j) -> p h j", h=8)
            iaf = scC[0:M, 384:640].rearrange("p (g k) -> p g k", g=16)
            e1 = scC[0:M, 640:768].rearrange("p (h j) -> p h j", h=8)
            e2 = scC[0:M, 768:896].rearrange("p (h j) -> p h j", h=8)
            op(dve, lambda e: e.tensor_copy(out=posf, in_=pos2[0:M, :, :]), r=[pos2], w=[scC])
            op(dve, lambda e: e.tensor_copy(out=iaf, in_=topi[0:M, :, :]), r=[topi], w=[scC])
            op(dve, lambda e: e.tensor_scalar(out=k1f, in0=posf, scalar1=1.0 / 16, scalar2=None, op0=ALU.mult), r=[scC], w=[scC])
            op(dve, lambda e: e.tensor_copy(out=pos2[0:M, :, :], in_=k1f), r=[scC], w=[pos2])
            op(dve, lambda e: e.tensor_copy(out=k1f, in_=pos2[0:M, :, :]), r=[pos2], w=[scC])
            op(dve, lambda e: e.scalar_tensor_tensor(out=k2f, in0=k1f, scalar=16.0, in1=posf, op0=ALU.mult, op1=ALU.is_gt), r=[scC], w=[scC])
            op(dve, lambda e: e.tensor_tensor(out=k1f, in0=k1f, in1=k2f, op=ALU.subtract), r=[scC], w=[scC])
            op(dve, lambda e: e.scalar_tensor_tensor(out=k2f, in0=k1f, scalar=-16.0, in1=posf, op0=ALU.mult, op1=ALU.add), r=[scC], w=[scC])
            oh = scB[0:M, :].rearrange("p (h j k) -> p h j k", h=8, j=16)
            iab = iaf.rearrange("p (h s) k -> p h s k", s=2)
            iotab = iota16[0:M, :].unsqueeze(1).unsqueeze(1).to_broadcast([M, 8, 16, 16])
            for side, (kf, eo) in enumerate(((k1f, e1), (k2f, e2))):
                op(dve, lambda e, kf=kf: e.tensor_tensor(out=oh, in0=kf.unsqueeze(3).to_broadcast([M, 8, 16, 16]), in1=iotab, op=ALU.is_equal), r=[scC, iota16], w=[scB])
                op(dve, lambda e, side=side: e.tensor_tensor(out=oh, in0=oh, in1=iab[:, :, side, :].unsqueeze(2).to_broadcast([M, 8, 16, 16]), op=ALU.mult), r=[scB, scC], w=[scB])
                op(dve, lambda e, eo=eo: e.tensor_reduce(out=eo, in_=oh, axis=AX.X, op=ALU.add), r=[scB], w=[scC])
            op(dve, lambda e: e.scalar_tensor_tensor(out=e1, in0=e1, scalar=128.0, in1=e2, op0=ALU.mult, op1=ALU.add), r=[scC], w=[scC])
            op(dve, lambda e: e.tensor_copy(out=EI[0:M, :].rearrange("p (h j) -> p h j", h=8), in_=e1), r=[scC], w=[EI])
            gx = scC[0:M, 896:1024].rearrange("p (h j) -> p h j", h=8)
            op(dve, lambda e: e.tensor_tensor(out=gx, in0=top2[0:M, :, :], in1=top2[0:M, :, 0:1].to_broadcast([M, 8, 16]), op=ALU.subtract), r=[top2], w=[scC])
            op(act, lambda e: e.activation(out=gx, in_=gx, func=AF.Exp), r=[scC], w=[scC])
            op(dve, lambda e: e.tensor_reduce(out=sm[0:M, 8:16], in_=gx, axis=AX.X, op=ALU.add), r=[scC], w=[sm])
            op(dve, lambda e: e.reciprocal(out=sm[0:M, 8:16], in_=sm[0:M, 8:16]), r=[sm], w=[sm])
            op(dve, lambda e: e.tensor_tensor(out=GT[0:M, :].rearrange("p (h j) -> p h j", h=8), in0=gx, in1=sm[0:M, 8:16].unsqueeze(2).to_broadcast([M, 8, 16]), op=ALU.mult), r=[scC, sm], w=[GT])
        def peer_loop(li, M, XP, EI, GT):
            vacc = ps_vacc()
            gate = GT[0:M, :]
            NGRP = 2
            NG_ = 128 // NGRP

            def cols(g):
                return g * NGRP, (g + 1) * NGRP

            def gath(j):
                ub = ubuf[j % NGB]
                kb.dma(pool, lambda e: e.indirect_dma_start(out=ub[:, :], out_offset=None, in_=peer_uv[li], in_offset=bass.IndirectOffsetOnAxis(ap=EI[:, j:j + 1], axis=0)), r=[EI], w=[ub])

            def dot(j, g):
                ub = ubuf[j % NGB]
                op(dve, lambda e: e.scalar_tensor_tensor(out=ub[0:M, 0:1024], in0=ub[0:M, 0:1024], scalar=1.0, in1=XP[0:M, :], op0=ALU.mult, op1=ALU.mult, accum_out=pact[0:M, j:j + 1]), r=[ub, XP], w=[ub, pact.lane(g % 3)])

            def vcopy(j, g):
                ub = ubuf[j % NGB]
                vb = vring[j % NVR]
                op(act, lambda e: e.activation(out=vb[0:M, :], in_=ub[0:M, 1024:2048], func=AF.Copy), r=[ub], w=[vb])

            def c1(g):
                j0, j1 = cols(g)
                ln = g % 3
                op(dve, lambda e: e.scalar_tensor_tensor(out=ptmp[0:M, j0:j1], in0=pact[0:M, j0:j1], scalar=0.044715, in1=pact[0:M, j0:j1], op0=ALU.mult, op1=ALU.mult), r=[pact.lane(ln)], w=[ptmp.lane(ln)])

            def c2(g):
                j0, j1 = cols(g)
                ln = g % 3
                op(dve, lambda e: e.scalar_tensor_tensor(out=ptmp[0:M, j0:j1], in0=ptmp[0:M, j0:j1], scalar=1.0, in1=pact[0:M, j0:j1], op0=ALU.add, op1=ALU.mult), r=[ptmp.lane(ln), pact.lane(ln)], w=[ptmp.lane(ln)])

            def cg(g):
                j0, j1 = cols(g)
                ln = g % 3
                op(dve, lambda e: e.tensor_tensor(out=pag[0:M, j0:j1], in0=pact[0:M, j0:j1], in1=gate[:, j0:j1], op=ALU.mult), r=[pact.lane(ln), GT], w=[pag.lane(ln)])

            def sig(g):
                j0, j1 = cols(g)
                ln = g % 3
                op(act, lambda e: e.activation(out=ptmp[0:M, j0:j1], in_=ptmp[0:M, j0:j1], func=AF.Sigmoid, scale=1.5957691216), r=[ptmp.lane(ln)], w=[ptmp.lane(ln)])

            def mm_(g):
                j0, j1 = cols(g)
                ln = g % 3
                op(dve, lambda e: e.tensor_tensor(out=pwv[0:M, j0:j1], in0=ptmp[0:M, j0:j1], in1=pag[0:M, j0:j1], op=ALU.mult), r=[ptmp.lane(ln), pag.lane(ln)], w=[pwv.lane(ln)])

            def diag(j, g):
                dg = dgb[j % 4]
                op(dve, lambda e: e.tensor_scalar(out=dg[0:M, 0:M], in0=ident[0:M, 0:M], scalar1=pwv[0:M, j:j + 1], scalar2=None, op0=ALU.mult), r=[ident, pwv.lane(g % 3)], w=[dg])

            def vmm(j):
                dg = dgb[j % 4]
                vb = vring[j % NVR]

                def f(e):
                    e.matmul(vacc[0:M, 0:512], lhsT=dg[0:M, 0:M], rhs=vb[0:M, 0:512], start=(j == 0), stop=(j == 127))
                    return e.matmul(vacc[0:M, 512:1024], lhsT=dg[0:M, 0:M], rhs=vb[0:M, 512:1024], start=(j == 0), stop=(j == 127))
                op(pe, f, r=[dg, vb], w=[vacc])

            for it in range(NG_ + 2):
                g0_, g1_, g2_ = it, it - 1, it - 2
                A0 = g0_ < NG_
                A1 = 0 <= g1_ < NG_
                A2 = 0 <= g2_ < NG_
                if A0:
                    gath(2 * g0_)
                    gath(2 * g0_ + 1)
                    dot(2 * g0_, g0_)
                    vcopy(2 * g0_, g0_)
                if A1:
                    c1(g1_)
                if A2:
                    mm_(g2_)
                if PIPE_MID:
                    yield
                if A0:
                    dot(2 * g0_ + 1, g0_)
                    vcopy(2 * g0_ + 1, g0_)
                if A1:
                    c2(g1_)
                if A2:
                    diag(2 * g2_, g2_)
                if A1:
                    cg(g1_)
                if A2:
                    diag(2 * g2_ + 1, g2_)
                if A1:
                    sig(g1_)
                if A2:
                    vmm(2 * g2_)
                    vmm(2 * g2_ + 1)
                yield

        def peer_tail(li, M, XP):
            vacc = ps_vacc()
            xp2 = scB[0:M, 0:1024]
            op(dve, lambda e: e.scalar_tensor_tensor(out=xp2, in0=XP[0:M, :], scalar=ALPHA, in1=vacc[0:M, :], op0=ALU.mult, op1=ALU.add), r=[XP, vacc], w=[scB])
            layer_norm(scB, 1, XP, M, src_ap=xp2)

        def ple_stage(li, M, row0, XP):
            kb.dma("sp", lambda e: e.dma_start(out=pe_f[0:M, :], in_=pin[li, row0:row0 + M, :]), w=[pe_f])
            op(act, lambda e: e.activation(out=xb[0:M, :], in_=XP[0:M, :], func=AF.Copy), r=[XP], w=[xb])
            transposes(xb, 8, M, xT, xT[:, :, 0:M])
            sgm = scA[0:M, 0:1024]
            for hh in range(2):
                wt = wget(("w_g", li, 0, 8, hh * 512, 512))
                pp = ps1()

                def f(e, wt=wt, pp=pp):
                    ins = None
                    for k in range(8):
                        ins = e.matmul(pp[0:M, 0:512], lhsT=xT[:, k, 0:M], rhs=wt[:, k, 0:512], start=(k == 0), stop=(k == 7))
                    return ins
                op(pe, f, r=[xT, wt], w=[pp])
                op(act, lambda e, pp=pp, hh=hh: e.activation(out=sgm[:, hh * 512:(hh + 1) * 512], in_=pp[0:M, 0:512], func=AF.Sigmoid), r=[pp], w=[scA])
            op(act, lambda e: e.activation(out=u_b[0:M, :], in_=pe_f[0:M, :], func=AF.Copy), r=[pe_f], w=[u_b])
            transposes(u_b, 2, M, uT, uT[:, :, 0:M])
            for hh in range(2):
                wt = wget(("w_p", li, 0, 2, hh * 512, 512))
                pp = ps1()

                def f(e, wt=wt, pp=pp):
                    ins = None
                    for k in range(2):
                        ins = e.matmul(pp[0:M, 0:512], lhsT=uT[:, k, 0:M], rhs=wt[:, k, 0:512], start=(k == 0), stop=(k == 1))
                    return ins
                op(pe, f, r=[uT, wt], w=[pp])
                op(dve, lambda e, pp=pp, hh=hh: e.tensor_tensor(out=sgm[:, hh * 512:(hh + 1) * 512], in0=sgm[:, hh * 512:(hh + 1) * 512], in1=pp[0:M, 0:512], op=ALU.mult), r=[pp, scA], w=[scA])
            xp2 = scB[0:M, 0:1024]
            op(dve, lambda e: e.scalar_tensor_tensor(out=xp2, in0=XP[0:M, :], scalar=ALPHA, in1=sgm, op0=ALU.mult, op1=ALU.add), r=[XP, scA], w=[scB])
            layer_norm(scB, 2, XP, M, src_ap=xp2)

        def run_all(gen):
            for _ in gen:
                pass

        for li in range(DEPTH):
            if "prep" in STAGES:
                layer_prep(li)
            def replay(item):
                if item[0] == "wissue":
                    w_issue_upto(item[1])
                elif item[0] == "op":
                    kb.op(item[1], item[2], item[3], item[4])
                else:
                    kb.dma(item[1], item[2], item[3], item[4], is_out=item[5])

            def front(n_):
                sample_, M_, row0_ = tile_geom(n_)
                peer_front(li, M_, X1P[n_ % 2], EIP[n_ % 2], GTP[n_ % 2])

            assert "peer" in STAGES
            run_all(p1a(li, 0))
            front(0)
            for n in range(NT):
                sample, M, row0 = tile_geom(n)
                L = []
                if n + 1 < NT and PIPELINE:
                    kb.rec = L
                    run_all(p1a(li, n + 1))
                    front(n + 1)
                    kb.rec = None
                pend = list(L)
                nrep = 0

                def regions(lst):
                    out = set()
                    for b in lst:
                        out.add(id(b))
                    return out

                def expand(lst):
                    out = set()
                    for b in lst:
                        out.add(id(b))
                        if b.parent is not None:
                            out.add(id(b.parent))
                        for k_ in b.kids.values():
                            out.add(id(k_))
                    return out

                for _ in peer_loop(li, M, X1P[n % 2], EIP[n % 2], GTP[n % 2]):
                    kb.iter += 1
                    budget = PIPE_EVERY
                    blk_r = set()
                    blk_w = set()
                    keep = []
                    scanned = 0
                    stop = False
                    for item in pend:
                        if stop or budget <= 0 or scanned >= PIPE_WINDOW:
                            keep.append(item)
                            continue
                        scanned += 1
                        if item[0] == "wissue":
                            if keep:
                                stop = True
                                keep.append(item)
                            else:
                                replay(item)
                                nrep += 1
                            continue
                        r_, w_ = item[3], item[4]
                        er, ew = expand(r_), expand(w_)
                        conflict = bool(ew & (blk_r | blk_w)) or bool(er & blk_w)
                        if (not conflict) and kb.ready(item[1], r_, w_, PIPE_AGE):
                            replay(item)
                            nrep += 1
                            budget -= 1
                        else:
                            keep.append(item)
                            blk_r |= regions(r_)
                            blk_w |= regions(w_)
                            if not PIPE_OOO:
                                stop = True
                    pend = keep
                if PIPE_VERBOSE and len(L):
                    print("pipeline li=%d n=%d: replayed %d of %d inside the loop" % (li, n, nrep, len(L)))
                for item in pend:
                    replay(item)
                if n + 1 < NT and not PIPELINE:
                    run_all(p1a(li, n + 1))
                    front(n + 1)
                p2_tail(li, n)
        kb.finish()
    return nc


def _consts(NTP):
    NT = NTP + 2
    c = {}
    c["c_ident"] = np.eye(128, dtype=np.float32)
    j = np.arange(128)
    c["c_l2t"] = (j[:, None] <= j[None, :]).astype(np.float32)
    c["c_kk"] = np.broadcast_to(np.arange(1, 129, dtype=np.float32)[None, :], (128, 128)).copy()
    c["c_iota"] = np.broadcast_to(np.arange(16, dtype=np.float32)[None, :], (128, 16)).copy()
    i = np.arange(128)[:, None]
    jj = np.arange(640)[None, :]
    valid = np.where(i < 64, jj < 576, jj >= 64)
    c["c_mask"] = np.where(valid, 0.0, NEG).astype(np.float32)
    half = 32
    freqs = (10000.0 ** (-np.arange(half, dtype=np.float32) / half)).astype(np.float32)
    pos = np.zeros((128, NT), dtype=np.float32)
    for n in range(NT):
        pos[:, n] = (n * 128 + np.arange(128)) if n < NTP else (2048 + np.arange(128))
    ang = (pos[:, :, None].astype(np.float32) * freqs[None, None, :]).astype(np.float32)
    c["c_cos"] = np.cos(ang).astype(np.float32).reshape(128, NT * 32)
    c["c_sin"] = np.sin(ang).astype(np.float32).reshape(128, NT * 32)
    lg = np.log1p(-(2.0 ** (-5.0 - np.arange(4, dtype=np.float64))))
    ii = np.arange(128, dtype=np.float64)
    ret = np.zeros((128, 12), dtype=np.float64)
    ret[:, 0:4] = 0.125 * np.exp(lg[None, :] * (ii[:, None] + 1.0))
    ret[:, 4:8] = np.exp(lg[None, :] * (127.0 - ii[:, None]))
    ret[:, 8:12] = np.exp(lg[None, :] * np.maximum(63.0 - ii[:, None], 0.0))
    c["c_ret"] = ret.astype(np.float32)
    I = ii[None, :]
    J = ii[:, None]
    same = (np.floor(I / 64) == np.floor(J / 64))
    cross = (J < 64) & (I >= 64)
    dt = np.zeros((128, 4, 128), dtype=np.float64)
    for h in range(4):
        dt[:, h, :] = 0.125 * np.where(same, np.exp(lg[h] * np.abs(I - J)), np.where(cross, np.exp(lg[h] * (I - J)), 0.0))
    c["c_dt"] = dt.reshape(128, 512).astype(np.float32)
    dec = np.zeros((64, 2, 4, 64), dtype=np.float64)
    for h in range(4):
        dec[:, 0, h, :] = np.exp(lg[h] * 128.0)
        dec[:, 1, h, :] = np.exp(lg[h] * 64.0)
    c["c_dec"] = dec.reshape(64, 512).astype(np.float32)
    return c


def _prep_shared(inp):
    f = lambda a: np.ascontiguousarray(np.asarray(a, dtype=np.float32))
    sh = {}
    sh["w_in"] = f(inp["w_in"])
    sh["w_glu"] = f(inp["ssm_w_glu"])
    sh["w_br_ssm"] = f(inp["w_br_ssm"])
    sh["w_br_att"] = f(inp["w_br_att"])
    sh["w_br_ret"] = f(inp["w_br_ret"])
    sh["w_o"] = f(inp["w_o"])
    sh["w_q"] = f(inp["peer_w_q"])
    sh["w_g"] = f(inp["ple_w_g"])
    sh["w_p"] = f(inp["ple_w_p"])
    pu, pv = f(inp["peer_u"]), f(inp["peer_v"])
    for i in range(2):
        sh["peer_u%d" % i] = pu[i]
        sh["peer_v%d" % i] = pv[i]
    b_re, b_im = f(inp["ssm_b_re"]), f(inp["ssm_b_im"])
    c_re, c_im = f(inp["ssm_c_re"]), f(inp["ssm_c_im"])
    bbd = np.zeros((2, 256, 2048), dtype=np.float32)
    cm = np.zeros((2, 128, 8, 2, 32), dtype=np.float32)
    for li in range(2):
        for g in range(16):
            bbd[li, g * 16:(g + 1) * 16, g * 64:(g + 1) * 64] = b_re[li, g].T
            bbd[li, g * 16:(g + 1) * 16, 1024 + g * 64:1024 + (g + 1) * 64] = b_im[li, g].T
            r0 = (g % 2) * 64
            c0 = (g % 2) * 16
            cm[li, r0:r0 + 64, g // 2, 0, c0:c0 + 16] = c_re[li, g].T
            cm[li, r0:r0 + 64, g // 2, 1, c0:c0 + 16] = c_im[li, g].T
    sh["bbd"] = bbd
    sh["cmat"] = cm.reshape(2, 128, 512)
    sk = f(inp["peer_sub_keys"]).reshape(2, 16, 128, 128)
    sh["keysT"] = np.ascontiguousarray(sk.transpose(0, 3, 1, 2)).reshape(2, 128, 2048)
    par = np.zeros((2, 128, 24), dtype=np.float32)
    for li in range(2):
        par[li, :, 0:8] = f(inp["ssm_lam_re"])[li].reshape(8, 128).T
        par[li, :, 8:16] = f(inp["ssm_lam_im"])[li].reshape(8, 128).T
        par[li, :, 16:24] = np.repeat(f(inp["ssm_log_dt"])[li], 64).reshape(8, 128).T
    sh["s5par"] = par
    rows = np.zeros((2, NROW), dtype=np.float32)
    for li in range(2):
        for i_, key in enumerate(("ln1_g", "ln2_g", "ln3_g", "ln1_b", "ln2_b", "ln3_b")):
            rows[li, i_ * 1024:(i_ + 1) * 1024] = f(inp[key])[li]
        rows[li, 6144 + R_DSK:6144 + R_DSK + 256] = f(inp["ssm_d"])[li]
        rows[li, 6144 + R_BGLU:6144 + R_BGLU + 256] = f(inp["ssm_b_glu"])[li]
        rows[li, 6144 + R_GNG:6144 + R_GNG + 256] = f(inp["ret_gn_g"])[li]
    sh["rows"] = np.ascontiguousarray(np.broadcast_to(rows[:, None, :], (2, 128, NROW)))
    i = np.arange(128)[:, None]
    jj = np.arange(640)[None, :]
    idx = np.clip(i - jj + 512, -256, 256) + 256
    tab = f(inp["att_rel_bias"])
    b2 = tab[:, idx, :]
    sh["bias2"] = np.ascontiguousarray(b2.transpose(0, 1, 3, 2)).reshape(2, 128, 8 * 640)
    return sh


def _prep_core(inp, c, NTP, seq_len):
    f = lambda a: np.asarray(a, dtype=np.float32)
    b = c % 4
    s0 = 2 * c
    d = {}
    d["xin"] = np.ascontiguousarray(np.concatenate([f(inp["x_prompt"])[b, :seq_len], f(inp["x_sample"])[s0], f(inp["x_sample"])[s0 + 1]], axis=0))
    d["pin"] = np.ascontiguousarray(np.concatenate([f(inp["p_prompt"])[:, b, :seq_len], f(inp["p_sample"])[:, s0], f(inp["p_sample"])[:, s0 + 1]], axis=1))
    st = np.zeros((2, 2, 128, 16), dtype=np.float32)
    for li in range(2):
        for s in range(2):
            st[li, s, :, 0:8] = f(inp["state_ssm_re"])[li, s0 + s].reshape(8, 128).T
            st[li, s, :, 8:16] = f(inp["state_ssm_im"])[li, s0 + s].reshape(8, 128).T
    d["st_ssm"] = st
    d["cache_k"] = np.ascontiguousarray(f(inp["cache_attn_k"])[:, s0:s0 + 2].reshape(2, 2, 512, 512))
    d["cache_v"] = np.ascontiguousarray(f(inp["cache_attn_v"])[:, s0:s0 + 2].reshape(2, 2, 512, 512))
    sr = f(inp["state_ret"])[:, s0:s0 + 2]
    d["st_ret"] = np.ascontiguousarray(sr.transpose(0, 1, 3, 2, 4)).reshape(2, 2, 64, 256)
    return d


_CACHE = {}


def run_cores(inp, NTP=NTP_FULL, n_cores=8, dbg=None):
    key = (NTP, tuple(sorted(dbg.keys())) if dbg else None, tuple(sorted(STAGES)))
    if key not in _CACHE:
        _CACHE[key] = build_program(NTP, dbg)
    nc = _CACHE[key]
    sh = _prep_shared(inp)
    sh.update(_consts(NTP))
    in_maps = []
    for c in range(n_cores):
        m = dict(sh)
        m.update(_prep_core(inp, c, NTP, NTP * 128))
        in_maps.append(m)
    res = run_bass_kernel_spmd(nc, in_maps, core_ids=list(range(n_cores)))
    return res.results


def kernel(**inputs):
    NTP = NTP_FULL
    res = run_cores(inputs, NTP, 8)
    y_p = np.zeros((4, 4096, D), np.float32)
    y_s = np.zeros((16, 64, D), np.float32)
    ssm_re_p = np.zeros((2, 4, 16, 64), np.float32)
    ssm_im_p = np.zeros((2, 4, 16, 64), np.float32)
    k_p = np.zeros((2, 4, 512, 8, 64), np.float32)
    v_p = np.zeros((2, 4, 512, 8, 64), np.float32)
    ret_p = np.zeros((2, 4, 4, 64, 64), np.float32)
    ssm_re_s = np.zeros((2, 16, 16, 64), np.float32)
    ssm_im_s = np.zeros((2, 16, 16, 64), np.float32)
    k_s = np.zeros((2, 16, 64, 8, 64), np.float32)
    v_s = np.zeros((2, 16, 64, 8, 64), np.float32)
    ret_s = np.zeros((2, 16, 4, 64, 64), np.float32)
    for c in range(8):
        r = res[c]
        yo = np.asarray(r["y_out"])
        so = np.asarray(r["ssm_out"])
        ko = np.asarray(r["k_out"])
        vo = np.asarray(r["v_out"])
        ro = np.asarray(r["ret_out"])
        if c < 4:
            y_p[c] = yo[0:4096]
            for li in range(2):
                ssm_re_p[li, c] = so[li, 0][:, 0:8].T.reshape(16, 64)
                ssm_im_p[li, c] = so[li, 0][:, 8:16].T.reshape(16, 64)
                k_p[li, c] = ko[li, 0:512].reshape(512, 8, 64)
                v_p[li, c] = vo[li, 0:512].reshape(512, 8, 64)
                ret_p[li, c] = ro[li, 0].reshape(64, 4, 64).transpose(1, 0, 2)
        for s in range(2):
            q = 2 * c + s
            y_s[q] = yo[4096 + 64 * s:4096 + 64 * (s + 1)]
            for li in range(2):
                ssm_re_s[li, q] = so[li, 1 + s][:, 0:8].T.reshape(16, 64)
                ssm_im_s[li, q] = so[li, 1 + s][:, 8:16].T.reshape(16, 64)
                k_s[li, q] = ko[li, 512 + 64 * s:512 + 64 * (s + 1)].reshape(64, 8, 64)
                v_s[li, q] = vo[li, 512 + 64 * s:512 + 64 * (s + 1)].reshape(64, 8, 64)
                ret_s[li, q] = ro[li, 1 + s].reshape(64, 4, 64).transpose(1, 0, 2)
    return (y_p, y_s, ssm_re_p, ssm_im_p, k_p, v_p, ret_p, ssm_re_s, ssm_im_s, k_s, v_s, ret_s)
```

```python
import numpy as np
import concourse.bass as bass
import concourse.mybir as mybir
from concourse.bass_utils import run_bass_kernel_spmd
from contextlib import ExitStack

F32 = mybir.dt.float32
BF16 = mybir.dt.bfloat16
U32 = mybir.dt.uint32
ALU = mybir.AluOpType
AF = mybir.ActivationFunctionType
AX = mybir.AxisListType

D = 1024
DEPTH = 2
NTP_FULL = 32
ALPHA = float((2 * DEPTH) ** 0.25)
LN_EPS = 1e-5
NEG = -30000.0
IN_BLOCKS = [(0, 256), (256, 512), (768, 512), (1280, 512), (1792, 512), (2304, 512),
             (2816, 512), (3328, 512), (3840, 512), (4352, 512), (4864, 512), (5376, 512)]
R_DSK, R_BGLU, R_GNG = 0, 256, 512
NROW = 6912


STRICT = True


class Dep:
    def __init__(self, parent=None):
        self.w = None
        self.r = {}
        self.parent = parent
        self.kids = {}

    def lane(self, key):
        if key not in self.kids:
            self.kids[key] = Dep(self)
        return self.kids[key]


class Tn(Dep):
    def __init__(self, t, name=""):
        super().__init__(None)
        self.t = t
        self.name = name

    def __getitem__(self, k):
        return self.t[k]


class KB:
    NS = 8

    def __init__(self, nc, es):
        self.nc = nc
        self.es = es
        self.engs = {"pe": nc.tensor, "dve": nc.vector, "act": nc.scalar, "pool": nc.gpsimd, "sp": nc.sync}
        self.sem = {e: es.enter_context(nc.semaphore("s_" + e)) for e in self.engs}
        self.cnt = {e: 0 for e in self.engs}
        self.seen = {e: {} for e in self.engs}
        self.dsem = {q: [es.enter_context(nc.semaphore("d_%s%d" % (q, i))) for i in range(self.NS)] for q in ("sp", "pool")}
        self.dcnt = {q: [0] * self.NS for q in ("sp", "pool")}
        self.dnext = {q: 0 for q in ("sp", "pool")}
        self.out_tokens = []
        self.rec = None
        self.iter = 0
        self.tok_iter = {}

    def sb(self, name, shape, dtype):
        return Tn(self.es.enter_context(self.nc.sbuf_tensor(name, shape, dtype)), name)

    def _wait(self, eng, tok):
        sem, val, key = tok
        if self.seen[eng].get(key, 0) >= val:
            return
        self.engs[eng].wait_ge(sem, val)
        self.seen[eng][key] = val

    def _deps1(self, eng, b, writing):
        strict = STRICT and eng != "pe"
        if b.w is not None and (b.w[2] != eng or (strict if writing else eng != "pe")):
            self._wait(eng, b.w)
        if writing:
            for k, t in b.r.items():
                if strict or k != eng:
                    self._wait(eng, t)

    def _deps(self, eng, r, w):
        for lst, writing in ((r, False), (w, True)):
            for b in lst:
                self._deps1(eng, b, writing)
                if b.parent is not None:
                    self._deps1(eng, b.parent, writing)
                for k_ in b.kids.values():
                    self._deps1(eng, k_, writing)

    def _mark(self, tok, r, w):
        for b in r:
            b.r[tok[2]] = tok
        for b in w:
            b.w = tok
            b.r = {}
            for k_ in b.kids.values():
                k_.w = tok
                k_.r = {}

    def ready(self, eng, r, w, age):
        ok = [True]

        def chk(b, writing):
            toks = []
            if b.w is not None:
                toks.append(b.w)
            if writing:
                toks.extend(b.r.values())
            for t in toks:
                sem, val, key = t
                if key == eng or self.seen[eng].get(key, 0) >= val:
                    continue
                if self.iter - self.tok_iter.get((key, val), -10 ** 9) < age:
                    ok[0] = False
        for lst, writing in ((r, False), (w, True)):
            for b in lst:
                chk(b, writing)
                if b.parent is not None:
                    chk(b.parent, writing)
                for k_ in b.kids.values():
                    chk(k_, writing)
        return ok[0]

    def op(self, eng, fn, r=(), w=()):
        r = _flat(r)
        w = _flat(w)
        if self.rec is not None:
            self.rec.append(("op", eng, fn, r, w, False))
            return None
        self._deps(eng, r, w)
        ins = fn(self.engs[eng])
        self.cnt[eng] += 1
        ins.then_inc(self.sem[eng], 1)
        tok = (self.sem[eng], self.cnt[eng], eng)
        self.tok_iter[(eng, self.cnt[eng])] = self.iter
        self._mark(tok, r, w)
        return tok

    def dma(self, q, fn, r=(), w=(), is_out=False):
        r = _flat(r)
        w = _flat(w)
        if self.rec is not None:
            self.rec.append(("dma", q, fn, r, w, is_out))
            return None
        self._deps(q, r, w)
        slot = self.dnext[q]
        self.dnext[q] = (slot + 1) % self.NS
        sem = self.dsem[q][slot]
        key = (q, slot)
        if self.dcnt[q][slot] > 0:
            self._wait(q, (sem, 16 * self.dcnt[q][slot], key))
        ins = fn(self.engs[q])
        self.dcnt[q][slot] += 1
        ins.then_inc(sem, 16)
        tok = (sem, 16 * self.dcnt[q][slot], key)
        self.tok_iter[(key, 16 * self.dcnt[q][slot])] = self.iter
        self._mark(tok, r, w)
        if is_out:
            self.out_tokens.append(tok)
        return tok

    def finish(self):
        for q in ("sp", "pool"):
            for slot in range(self.NS):
                if self.dcnt[q][slot] > 0:
                    self._wait("sp", (self.dsem[q][slot], 16 * self.dcnt[q][slot], (q, slot)))


def _flat(lst):
    out = []
    for x in lst:
        if x is None:
            continue
        if isinstance(x, (list, tuple)):
            out.extend(_flat(x))
        else:
            out.append(x)
    return out


STAGES = {"prep", "inproj", "s5", "attn", "ret", "merge", "peer", "ple"}
PIPELINE = True
PIPE_EVERY = 8
PIPE_VERBOSE = False
PIPE_AGE = 1
PIPE_MID = False
PIPE_OOO = True
PIPE_WINDOW = 40
DBG_TILE = (0, 0)


def build_program(NTP=NTP_FULL, dbg=None):
    NT = NTP + 2
    NTOK = NTP * 128 + 128
    nc = bass.Bass("TRN2", target_bir_lowering=False)
    es = ExitStack()

    def din(name, shape, dt=F32):
        return nc.dram_tensor(name, list(shape), dt, kind="ExternalInput").ap()

    def dout(name, shape, dt=F32):
        return nc.dram_tensor(name, list(shape), dt, kind="ExternalOutput").ap()

    def dint(name, shape, dt=F32):
        return nc.dram_tensor(name, list(shape), dt, kind="Internal").ap()

    xin = din("xin", [NTOK, D])
    pin = din("pin", [2, NTOK, 256])
    st_ssm = din("st_ssm", [2, 2, 128, 16])
    cache_k = din("cache_k", [2, 2, 512, 512])
    cache_v = din("cache_v", [2, 2, 512, 512])
    st_ret = din("st_ret", [2, 2, 64, 256])
    W = {
        "w_in": din("w_in", [2, 1024, 5888]), "w_glu": din("w_glu", [2, 256, 256]),
        "w_br_ssm": din("w_br_ssm", [2, 256, 1024]), "w_br_att": din("w_br_att", [2, 512, 1024]),
        "w_br_ret": din("w_br_ret", [2, 256, 1024]), "w_o": din("w_o", [2, 1024, 1024]),
        "w_q": din("w_q", [2, 1024, 2048]), "w_g": din("w_g", [2, 1024, 1024]),
        "w_p": din("w_p", [2, 256, 1024]), "bbd": din("bbd", [2, 256, 2048]),
    }
    peer_u = [din("peer_u%d" % i, [16384, 1024]) for i in range(2)]
    peer_v = [din("peer_v%d" % i, [16384, 1024]) for i in range(2)]
    cmat = din("cmat", [2, 128, 512])
    keysT = din("keysT", [2, 128, 2048])
    s5par = din("s5par", [2, 128, 24])
    rows = din("rows", [2, 128, NROW])
    bias2 = din("bias2", [2, 128, 8 * 640])
    c_ident = din("c_ident", [128, 128])
    c_l2t = din("c_l2t", [128, 128])
    c_kk = din("c_kk", [128, 128])
    c_iota = din("c_iota", [128, 16])
    c_mask = din("c_mask", [128, 640])
    c_cos = din("c_cos", [128, NT * 32])
    c_sin = din("c_sin", [128, NT * 32])
    c_ret = din("c_ret", [128, 12])
    c_dt = din("c_dt", [128, 512])
    c_dec = din("c_dec", [64, 512])

    y_out = dout("y_out", [NTOK, D])
    ssm_out = dout("ssm_out", [2, 3, 128, 16])
    k_out = dout("k_out", [2, 640, 512])
    v_out = dout("v_out", [2, 640, 512])
    ret_out = dout("ret_out", [2, 3, 64, 256])
    dbg_aps = {}
    if dbg:
        for name, (shape, dt_) in dbg.items():
            dbg_aps[name] = dout("dbg_" + name, shape, dt_)

    X1 = dint("X1", [NTOK, D])
    WB = {k: dint("wb_" + k, list(v.shape), BF16) for k, v in W.items()}
    peer_uv = [dint("peer_uv%d" % i, [16384, 2048], BF16) for i in range(2)]

    with es:
        kb = KB(nc, es)
        pe, dve, act, pool = "pe", "dve", "act", "pool"

        Fd = [es.enter_context(nc.psum_tensor("F%d" % i, [128, 1024], F32)) for i in range(3)]
        B0 = es.enter_context(nc.psum_tensor("B0", [128, 1024], BF16))
        FA = es.enter_context(nc.psum_tensor("FA", [128, 512], F32))
        Fs = [Tn(Fd[i // 2][:, (i % 2) * 512:(i % 2) * 512 + 512], "F%d_%d" % (i // 2, i % 2)) for i in range(6)]
        Bs = [Tn(B0, "B0")]
        FAs = Tn(FA, "FA")
        psst = {"f1": 0, "f2": 0}

        class PS:
            def __init__(self, ap, deps):
                self.ap = ap
                self.deps = deps

            def __getitem__(self, k):
                return self.ap[k]

        def ps1():
            i = psst["f1"] % 4
            psst["f1"] += 1
            return PS(Fs[i].t, [Fs[i]])

        def ps2():
            i = psst["f2"] % 2
            psst["f2"] += 1
            return PS(Fd[i], [Fs[2 * i], Fs[2 * i + 1]])

        def ps_acc():
            return PS(FA, [FAs])

        def ps_vacc():
            return PS(Fd[2], [Fs[4], Fs[5]])

        def psb():
            return PS(B0, [Bs[0]])

        def D_(x):
            return x.deps if isinstance(x, PS) else x

        def op(eng, fn, r=(), w=()):
            return kb.op(eng, fn, [D_(x) for x in r], [D_(x) for x in w])

        identF = kb.sb("identF", [128, 128], F32)
        ident = kb.sb("ident", [128, 128], BF16)
        l2t = kb.sb("l2t", [128, 128], BF16)
        iota16 = kb.sb("iota16", [128, 16], F32)
        cs_t = kb.sb("cs_t", [128, 2, 32], F32)
        cret = kb.sb("cret", [128, 12], F32)
        dtt = kb.sb("dtt", [128, 4, 128], F32)
        dect = kb.sb("dect", [64, 2, 256], F32)
        ctmp = kb.sb("ctmp", [128, 128], F32)

        def ld(dst, dst_ap, src, q="sp"):
            kb.dma(q, lambda e: e.dma_start(out=dst_ap, in_=src), r=[], w=[dst])

        ld(identF, identF[:], c_ident)
        ld(ctmp, ctmp[:], c_l2t)
        ld(iota16, iota16[:], c_iota)
        ld(cret, cret[:], c_ret)
        ld(dtt, dtt[:], c_dt.rearrange("p (h i) -> p h i", h=4))
        ld(dect, dect[:], c_dec.rearrange("p (a c) -> p a c", a=2))
        op(dve, lambda e: e.tensor_copy(out=ident[:], in_=identF[:]), r=[identF], w=[ident])
        op(dve, lambda e: e.tensor_copy(out=l2t[:], in_=ctmp[:]), r=[ctmp], w=[l2t])

        wb_dep = Dep()
        for k, src in W.items():
            tot = 1
            for s_ in src.shape:
                tot *= s_
            sf = src.rearrange("l a b -> (l a b)").rearrange("(r c) -> r c", c=2048)
            df = WB[k].rearrange("l a b -> (l a b)").rearrange("(r c) -> r c", c=2048)
            nrow = tot // 2048
            for r0 in range(0, nrow, 256):
                r1 = min(nrow, r0 + 256)
                kb.dma(pool, lambda e, a=df[r0:r1, :], b=sf[r0:r1, :]: e.dma_start(out=a, in_=b), r=[], w=[])
        for li_ in range(2):
            for hh_, src in enumerate((peer_u[li_], peer_v[li_])):
                for r0 in range(0, 16384, 512):
                    kb.dma(pool, lambda e, a=peer_uv[li_][r0:r0 + 512, hh_ * 1024:(hh_ + 1) * 1024], b=src[r0:r0 + 512, :]: e.dma_start(out=a, in_=b), r=[], w=[])
        cast_tokens = [(kb.dsem[pool][s], 16 * kb.dcnt[pool][s], (pool, s)) for s in range(kb.NS) if kb.dcnt[pool][s] > 0]
        for t in cast_tokens:
            kb._wait("sp", t)

        NWS = 3
        wst = [kb.sb("wst%d" % i, [128, 8, 512], BF16) for i in range(NWS)]
        wq = []
        wstate = {"issued": 0, "used": 0}

        def w_issue():
            i = wstate["issued"]
            name, li, k0, K, c0, N = wq[i]
            slot = wst[i % NWS]
            src = WB[name][li, k0 * 128:(k0 + K) * 128, c0:c0 + N].rearrange("(k p) n -> p k n", p=128)
            kb.dma("sp", lambda e: e.dma_start(out=slot[:, 0:K, 0:N], in_=src), r=[], w=[slot])
            wstate["issued"] += 1

        def w_issue_upto(i):
            while wstate["issued"] < min(len(wq), i + NWS):
                w_issue()

        def wget(spec):
            i = wstate["used"]
            assert wq[i] == spec, (wq[i], spec)
            wstate["used"] += 1
            if kb.rec is not None:
                kb.rec.append(("wissue", i))
            else:
                w_issue_upto(i)
            return wst[i % NWS]

        def tile_specs(li):
            sp = []
            if "inproj" in STAGES:
                sp += [("w_in", li, 0, 8, c0, n) for (c0, n) in IN_BLOCKS]
            if "merge" in STAGES:
                for name, K in (("w_br_ssm", 2), ("w_br_att", 4), ("w_br_ret", 2)):
                    sp += [(name, li, 0, K, 0, 512), (name, li, 0, K, 512, 512)]
                sp += [("w_o", li, 0, 8, 0, 512), ("w_o", li, 0, 8, 512, 512)]
            if "peer" in STAGES:
                sp += [("w_q", li, 0, 8, c * 512, 512) for c in range(4)]
            if "ple" in STAGES:
                sp += [("w_g", li, 0, 8, 0, 512), ("w_g", li, 0, 8, 512, 512)]
                sp += [("w_p", li, 0, 2, 0, 512), ("w_p", li, 0, 2, 512, 512)]
            return sp

        def specs_p1a(li):
            sp = []
            if "inproj" in STAGES:
                sp += [("w_in", li, 0, 8, c0, n) for (c0, n) in IN_BLOCKS]
            if "merge" in STAGES:
                for name, K in (("w_br_ssm", 2), ("w_br_att", 4), ("w_br_ret", 2)):
                    sp += [(name, li, 0, K, 0, 512), (name, li, 0, K, 512, 512)]
                sp += [("w_o", li, 0, 8, 0, 512), ("w_o", li, 0, 8, 512, 512)]
            return sp

        for li in range(DEPTH):
            wq.extend(specs_p1a(li))
            wq.extend([("w_q", li, 0, 8, c * 512, 512) for c in range(4)])
            for n in range(NT):
                if n + 1 < NT:
                    wq.extend(specs_p1a(li))
                    wq.extend([("w_q", li, 0, 8, c * 512, 512) for c in range(4)])
                if "ple" in STAGES:
                    wq.extend([("w_g", li, 0, 8, 0, 512), ("w_g", li, 0, 8, 512, 512), ("w_p", li, 0, 2, 0, 512), ("w_p", li, 0, 2, 512, 512)])

        rows_g = kb.sb("rows_g", [128, 3, 1024], BF16)
        rows_b = kb.sb("rows_b", [128, 3, 1024], BF16)
        rows_s = kb.sb("rows_s", [128, 768], BF16)
        bbd_b = kb.sb("bbd_b", [128, 2, 2, 512], BF16)
        cm_f = kb.sb("cm_f", [128, 512], F32)
        cm_b = kb.sb("cm_b", [128, 8, 2, 32], BF16)
        wglu_b = kb.sb("wglu_b", [128, 2, 256], BF16)
        keys_b = kb.sb("keys_b", [128, 16, 128], BF16)
        bias_b = kb.sb("bias_b", [128, 8, 640], BF16)
        ainvk = kb.sb("ainvk", [128, 2, 1024], BF16)
        a1t = kb.sb("a1t", [128, 2, 8, 128], F32)
        par = kb.sb("par", [128, 24], F32)
        sp_ = {n_: kb.sb("sp_" + n_, [128, 8], F32) for n_ in
               ("dt", "ldr", "th", "c1", "s1", "t0", "t1", "t2", "are", "aim", "kre", "kim", "l2", "nldr")}
        hst = kb.sb("hst", [128, 16], F32)
        sret_f = kb.sb("sret_f", [64, 256], F32)
        sret_b = kb.sb("sret_b", [64, 4, 64], BF16)
        kt2 = kb.sb("kt2", [128, 4, 9 * 128], BF16)
        vr = kb.sb("vr", [128, 9, 512], BF16)
        scA = kb.sb("scA", [128, 2048], F32)
        scB = kb.sb("scB", [128, 2048], F32)
        scC = kb.sb("scC", [128, 1024], F32)
        x_f = kb.sb("x_f", [128, D], F32)
        x1_f = kb.sb("x1_f", [128, D], F32)
        xpre = kb.sb("xpre", [128, D], F32)
        xb = kb.sb("xb", [128, D], BF16)
        xT = kb.sb("xT", [128, 8, 128], BF16)
        gates_b = kb.sb("gates_b", [128, 3, 1024], BF16)
        u_f = kb.sb("u_f", [128, 256], F32)
        u_b = kb.sb("u_b", [128, 256], BF16)
        qa_b = kb.sb("qa_b", [128, 512], BF16)
        ka_b = kb.sb("ka_b", [128, 512], BF16)
        vr_b = kb.sb("vr_b", [128, 256], BF16)
        sg_f = kb.sb("sg_f", [128, 256], F32)
        sm = kb.sb("sm", [128, 64], F32)
        bigb = kb.sb("bigb", [128, 2048], BF16)
        uT = kb.sb("uT", [128, 2, 128], BF16)
        yssm_b = kb.sb("yssm_b", [128, 256], BF16)
        yatt_b = kb.sb("yatt_b", [128, 512], BF16)
        yret_b = kb.sb("yret_b", [128, 256], BF16)
        qT2 = kb.sb("qT2", [128, 4, 128], BF16)
        pb_att = kb.sb("pb_att", [128, 640], BF16)
        pT = kb.sb("pT", [128, 5, 128], BF16)
        qkT = kb.sb("qkT", [64, 8, 128], BF16)
        qsT = kb.sb("qsT", [64, 4, 128], BF16)
        qk_b = kb.sb("qk_b", [128, 512], BF16)
        qs_b = kb.sb("qs_b", [128, 256], BF16)
        kk_b = kb.sb("kk_b", [128, 256], BF16)
        scT_b = kb.sb("scT_b", [128, 4, 128], BF16)
        yT = kb.sb("yT", [128, 8, 128], BF16)
        topa = kb.sb("topa", [128, 16, 16], F32)
        topi = kb.sb("topi", [128, 16, 16], U32)
        top2 = kb.sb("top2", [128, 8, 16], F32)
        pos2 = kb.sb("pos2", [128, 8, 16], U32)
        eidx = kb.sb("eidx", [128, 128], U32)
        eidx_b = kb.sb("eidx_b", [128, 128], U32)
        gate_a = kb.sb("gate_a", [128, 128], F32)
        gate_b = kb.sb("gate_b", [128, 128], F32)
        pact = kb.sb("pact", [128, 128], F32)
        pwv = kb.sb("pwv", [128, 128], F32)
        ptmp = kb.sb("ptmp", [128, 128], F32)
        pag = kb.sb("pag", [128, 128], F32)
        NGB = 6
        ubuf = [kb.sb("ubuf%d" % i, [128, 2 * D], BF16) for i in range(NGB)]
        NVR = 6
        vring = [kb.sb("vring%d" % i, [128, D], BF16) for i in range(NVR)]
        vbuf = ubuf
        dgb = [kb.sb("dgb%d" % i, [128, 128], BF16) for i in range(4)]
        pe_f = kb.sb("pe_f", [128, 256], F32)
        print("SBUF bytes remaining per partition:", nc.sbuf_bytes_remaining)

        def dump(name, src_ap, deps):
            if name in dbg_aps:
                kb.dma("sp", lambda e: e.dma_start(out=dbg_aps[name], in_=src_ap), r=deps, w=[], is_out=True)

        op(pool, lambda e: e.memset(eidx[:], 0), w=[eidx])
        op(pool, lambda e: e.memset(eidx_b[:], 0), w=[eidx_b])
        X1P = [x1_f, xpre]
        EIP = [eidx, eidx_b]
        GTP = [gate_a, gate_b]

        def transposes(src, nblk, M, dst, dst_slices, cw=128, src_cols=None):
            pb = psb()

            def f(e):
                ins = None
                for i in range(nblk):
                    c0 = i * cw if src_cols is None else src_cols[i]
                    ins = e.transpose(out=pb[0:cw, i * 128:i * 128 + M], in_=src[0:M, c0:c0 + cw], identity=ident[0:M, 0:M])
                return ins
            op(pe, f, r=[src, ident], w=[pb])
            op(dve, lambda e: e.tensor_copy(out=dst_slices, in_=pb[0:cw, 0:nblk * 128].rearrange("p (b m) -> p b m", m=128)[:, :, 0:M]),
               r=[pb], w=[dst])

        def layer_norm(src, idx, dst, M, src_ap=None):
            junk = scA
            sap = src[0:M, :] if src_ap is None else src_ap
            op(act, lambda e: e.activation(out=scA[0:M, 0:1024], in_=sap, func=AF.Identity, accum_out=sm[0:M, 0:1]), r=[src], w=[junk, sm])
            op(act, lambda e: e.activation(out=scA[0:M, 0:1024], in_=sap, func=AF.Square, accum_out=sm[0:M, 1:2]), r=[src], w=[junk, sm])
            op(dve, lambda e: e.tensor_scalar(out=sm[0:M, 2:3], in0=sm[0:M, 0:1], scalar1=1.0 / D, scalar2=None, op0=ALU.mult), r=[sm], w=[sm])
            op(dve, lambda e: e.tensor_tensor(out=sm[0:M, 3:4], in0=sm[0:M, 2:3], in1=sm[0:M, 2:3], op=ALU.mult), r=[sm], w=[sm])
            op(dve, lambda e: e.scalar_tensor_tensor(out=sm[0:M, 4:5], in0=sm[0:M, 1:2], scalar=1.0 / D, in1=sm[0:M, 3:4], op0=ALU.mult, op1=ALU.subtract), r=[sm], w=[sm])
            op(dve, lambda e: e.tensor_scalar(out=sm[0:M, 4:5], in0=sm[0:M, 4:5], scalar1=LN_EPS, scalar2=None, op0=ALU.add), r=[sm], w=[sm])
            op(act, lambda e: e.activation(out=sm[0:M, 5:6], in_=sm[0:M, 4:5], func=AF.Sqrt), r=[sm], w=[sm])
            op(dve, lambda e: e.reciprocal(out=sm[0:M, 6:7], in_=sm[0:M, 5:6]), r=[sm], w=[sm])
            op(dve, lambda e: e.scalar_tensor_tensor(out=sm[0:M, 7:8], in0=sm[0:M, 2:3], scalar=-1.0, in1=sm[0:M, 6:7], op0=ALU.mult, op1=ALU.mult), r=[sm], w=[sm])
            op(act, lambda e: e.activation(out=scA[0:M, 0:1024], in_=sap, func=AF.Identity, scale=sm[0:M, 6:7], bias=sm[0:M, 7:8]), r=[src, sm], w=[junk])
            op(dve, lambda e: e.tensor_tensor(out=scA[0:M, 0:1024], in0=scA[0:M, 0:1024], in1=rows_g[0:M, idx, :], op=ALU.mult), r=[junk, rows_g], w=[junk])
            op(dve, lambda e: e.tensor_tensor(out=dst[0:M, :], in0=scA[0:M, 0:1024], in1=rows_b[0:M, idx, :], op=ALU.add), r=[junk, rows_b], w=[dst])

        def gelu_tanh(src_ap, dst_ap, tmp_ap, deps_r, deps_w, tmp_dep):
            op(dve, lambda e: e.tensor_tensor(out=tmp_ap, in0=src_ap, in1=src_ap, op=ALU.mult), r=deps_r, w=[tmp_dep])
            op(dve, lambda e: e.tensor_scalar(out=tmp_ap, in0=tmp_ap, scalar1=0.044715, scalar2=1.0, op0=ALU.mult, op1=ALU.add), r=[tmp_dep], w=[tmp_dep])
            op(dve, lambda e: e.tensor_tensor(out=tmp_ap, in0=tmp_ap, in1=src_ap, op=ALU.mult), r=[tmp_dep] + deps_r, w=[tmp_dep])
            op(act, lambda e: e.activation(out=tmp_ap, in_=tmp_ap, func=AF.Sigmoid, scale=1.5957691216), r=[tmp_dep], w=[tmp_dep])
            op(dve, lambda e: e.tensor_tensor(out=dst_ap, in0=tmp_ap, in1=src_ap, op=ALU.mult), r=[tmp_dep] + deps_r, w=deps_w)

        def layer_prep(li):
            kb.dma(pool, lambda e: e.dma_start(out=rows_g[:], in_=rows[li][:, 0:3072].rearrange("p (a d) -> p a d", a=3)), w=[rows_g])
            kb.dma(pool, lambda e: e.dma_start(out=rows_b[:], in_=rows[li][:, 3072:6144].rearrange("p (a d) -> p a d", a=3)), w=[rows_b])
            kb.dma(pool, lambda e: e.dma_start(out=rows_s[:], in_=rows[li][:, 6144:6912]), w=[rows_s])
            for k_ in range(2):
                for part_ in range(2):
                    c0_ = part_ * 1024 + k_ * 512
                    kb.dma("sp", lambda e, k_=k_, part_=part_, c0_=c0_: e.dma_start(out=bbd_b[:, k_, part_, :], in_=WB["bbd"][li, k_ * 128:(k_ + 1) * 128, c0_:c0_ + 512]), w=[bbd_b])
            kb.dma("sp", lambda e: e.dma_start(out=wglu_b[:], in_=WB["w_glu"][li].rearrange("(k p) n -> p k n", p=128)), w=[wglu_b])
            kb.dma("sp", lambda e: e.dma_start(out=cm_f[:], in_=cmat[li]), w=[cm_f])
            op(dve, lambda e: e.tensor_copy(out=cm_b[:].rearrange("p a b c -> p (a b c)"), in_=cm_f[:]), r=[cm_f], w=[cm_b])
            kb.dma("sp", lambda e: e.dma_start(out=scA[:, :], in_=keysT[li]), w=[scA])
            op(dve, lambda e: e.tensor_copy(out=keys_b[:].rearrange("p a b -> p (a b)"), in_=scA[:, :]), r=[scA], w=[keys_b])
            kb.dma("sp", lambda e: e.dma_start(out=scB[:, 0:640], in_=c_mask), w=[scB])
            for h in range(8):
                kb.dma("sp", lambda e, h=h: e.dma_start(out=scC[:, 0:640], in_=bias2[li][:, h * 640:(h + 1) * 640]), w=[scC])
                op(dve, lambda e, h=h: e.tensor_tensor(out=bias_b[:, h, :], in0=scC[:, 0:640], in1=scB[:, 0:640], op=ALU.add), r=[scC, scB], w=[bias_b])
            kb.dma("sp", lambda e: e.dma_start(out=par[:], in_=s5par[li]), w=[par])
            P = sp_
            lre, lim, ldt = par[:, 0:8], par[:, 8:16], par[:, 16:24]
            op(act, lambda e: e.activation(out=P["dt"][:], in_=ldt, func=AF.Exp), r=[par], w=[P["dt"]])
            op(dve, lambda e: e.tensor_tensor(out=P["ldr"][:], in0=lre, in1=P["dt"][:], op=ALU.mult), r=[par, P["dt"]], w=[P["ldr"]])
            op(dve, lambda e: e.tensor_tensor(out=P["th"][:], in0=lim, in1=P["dt"][:], op=ALU.mult), r=[par, P["dt"]], w=[P["th"]])
            op(dve, lambda e: e.tensor_scalar(out=P["nldr"][:], in0=P["ldr"][:], scalar1=-1.0, scalar2=None, op0=ALU.mult), r=[P["ldr"]], w=[P["nldr"]])

            def sin_reduced(dst, shift):
                TWO_PI = 2.0 * np.pi
                op(dve, lambda e: e.tensor_scalar(out=P["t0"][:], in0=P["th"][:], scalar1=float(shift), scalar2=1.0 / TWO_PI, op0=ALU.add, op1=ALU.mult), r=[P["th"]], w=[P["t0"]])
                tI = topi
                op(dve, lambda e: e.tensor_copy(out=tI[:, 0, 0:8], in_=P["t0"][:]), r=[P["t0"]], w=[tI])
                op(dve, lambda e: e.tensor_copy(out=P["t1"][:], in_=tI[:, 0, 0:8]), r=[tI], w=[P["t1"]])
                op(dve, lambda e: e.tensor_tensor(out=P["t2"][:], in0=P["t0"][:], in1=P["t1"][:], op=ALU.subtract), r=[P["t0"], P["t1"]], w=[P["t2"]])
                op(dve, lambda e: e.tensor_scalar(out=P["t1"][:], in0=P["t2"][:], scalar1=0.5, scalar2=None, op0=ALU.is_gt), r=[P["t2"]], w=[P["t1"]])
                op(dve, lambda e: e.tensor_tensor(out=P["t2"][:], in0=P["t2"][:], in1=P["t1"][:], op=ALU.subtract), r=[P["t2"], P["t1"]], w=[P["t2"]])
                op(dve, lambda e: e.tensor_scalar(out=P["t2"][:], in0=P["t2"][:], scalar1=TWO_PI, scalar2=3.1415925, op0=ALU.mult, op1=ALU.min), r=[P["t2"]], w=[P["t2"]])
                op(dve, lambda e: e.tensor_scalar(out=P["t2"][:], in0=P["t2"][:], scalar1=-3.1415925, scalar2=None, op0=ALU.max), r=[P["t2"]], w=[P["t2"]])
                op(act, lambda e: e.activation(out=dst[:], in_=P["t2"][:], func=AF.Sin), r=[P["t2"]], w=[dst])
            sin_reduced(P["s1"], 0.0)
            sin_reduced(P["c1"], np.pi / 2)
            Ere = scA[:, 0:1024].rearrange("p (c k) -> p c k", c=8)
            Eim = scA[:, 1024:2048].rearrange("p (c k) -> p c k", c=8)
            Tm = scB[:, 0:1024].rearrange("p (c k) -> p c k", c=8)
            Tm2 = scB[:, 1024:2048].rearrange("p (c k) -> p c k", c=8)
            op(dve, lambda e: e.tensor_copy(out=Ere[:, :, 0:1], in_=P["c1"][:].unsqueeze(2)), r=[P["c1"]], w=[scA])
            op(dve, lambda e: e.tensor_copy(out=Eim[:, :, 0:1], in_=P["s1"][:].unsqueeze(2)), r=[P["s1"]], w=[scA])
            n_have = 1
            while n_have < 128:
                m = n_have
                br = Ere[:, :, m - 1:m].to_broadcast([128, 8, m])
                bi = Eim[:, :, m - 1:m].to_broadcast([128, 8, m])
                op(dve, lambda e: e.tensor_tensor(out=Tm[:, :, 0:m], in0=Ere[:, :, 0:m], in1=br, op=ALU.mult), r=[scA], w=[scB])
                op(dve, lambda e: e.tensor_tensor(out=Tm2[:, :, 0:m], in0=Eim[:, :, 0:m], in1=bi, op=ALU.mult), r=[scA], w=[scB])
                op(dve, lambda e: e.tensor_tensor(out=Tm[:, :, 0:m], in0=Tm[:, :, 0:m], in1=Tm2[:, :, 0:m], op=ALU.subtract), r=[scB], w=[scB])
                op(dve, lambda e: e.tensor_tensor(out=Tm2[:, :, 0:m], in0=Ere[:, :, 0:m], in1=bi, op=ALU.mult), r=[scA, scB], w=[scB])
                op(dve, lambda e: e.tensor_tensor(out=Eim[:, :, m:2 * m], in0=Eim[:, :, 0:m], in1=br, op=ALU.mult), r=[scA], w=[scA])
                op(dve, lambda e: e.tensor_tensor(out=Eim[:, :, m:2 * m], in0=Eim[:, :, m:2 * m], in1=Tm2[:, :, 0:m], op=ALU.add), r=[scA, scB], w=[scA])
                op(dve, lambda e: e.tensor_copy(out=Ere[:, :, m:2 * m], in_=Tm[:, :, 0:m]), r=[scB], w=[scA])
                n_have *= 2
            MP = scB[:, 0:1024].rearrange("p (c k) -> p c k", c=8)
            MN = scB[:, 1024:2048].rearrange("p (c k) -> p c k", c=8)
            kb.dma("sp", lambda e: e.dma_start(out=scC[:, 0:128], in_=c_kk), w=[scC])
            kkb = scC[:, 0:128].unsqueeze(1).to_broadcast([128, 8, 128])
            op(dve, lambda e: e.tensor_tensor(out=MP, in0=kkb, in1=P["ldr"][:].unsqueeze(2).to_broadcast([128, 8, 128]), op=ALU.mult), r=[scC, P["ldr"]], w=[scB])
            op(act, lambda e: e.activation(out=MN, in_=MP, func=AF.Exp, scale=-1.0), r=[scB], w=[scB])
            op(act, lambda e: e.activation(out=MP, in_=MP, func=AF.Exp), r=[scB], w=[scB])
            op(dve, lambda e: e.tensor_tensor(out=a1t[:, 0, :, :], in0=MP, in1=Ere, op=ALU.mult), r=[scA, scB], w=[a1t])
            op(dve, lambda e: e.tensor_tensor(out=a1t[:, 1, :, :], in0=MP, in1=Eim, op=ALU.mult), r=[scA, scB], w=[a1t])
            op(dve, lambda e: e.tensor_scalar(out=P["are"][:], in0=a1t[:, 0, :, 0], scalar1=-1.0, scalar2=None, op0=ALU.add), r=[a1t], w=[P["are"]])
            op(dve, lambda e: e.tensor_copy(out=P["aim"][:], in_=a1t[:, 1, :, 0]), r=[a1t], w=[P["aim"]])
            op(dve, lambda e: e.tensor_tensor(out=P["l2"][:], in0=lre, in1=lre, op=ALU.mult), r=[par], w=[P["l2"]])
            op(dve, lambda e: e.tensor_tensor(out=P["t0"][:], in0=lim, in1=lim, op=ALU.mult), r=[par], w=[P["t0"]])
            op(dve, lambda e: e.tensor_tensor(out=P["l2"][:], in0=P["l2"][:], in1=P["t0"][:], op=ALU.add), r=[P["l2"], P["t0"]], w=[P["l2"]])
            op(dve, lambda e: e.reciprocal(out=P["l2"][:], in_=P["l2"][:]), r=[P["l2"]], w=[P["l2"]])
            op(dve, lambda e: e.tensor_tensor(out=P["t0"][:], in0=P["are"][:], in1=lre, op=ALU.mult), r=[P["are"], par], w=[P["t0"]])
            op(dve, lambda e: e.tensor_tensor(out=P["t1"][:], in0=P["aim"][:], in1=lim, op=ALU.mult), r=[P["aim"], par], w=[P["t1"]])
            op(dve, lambda e: e.tensor_tensor(out=P["t0"][:], in0=P["t0"][:], in1=P["t1"][:], op=ALU.add), r=[P["t0"], P["t1"]], w=[P["t0"]])
            op(dve, lambda e: e.tensor_tensor(out=P["kre"][:], in0=P["t0"][:], in1=P["l2"][:], op=ALU.mult), r=[P["t0"], P["l2"]], w=[P["kre"]])
            op(dve, lambda e: e.tensor_tensor(out=P["t0"][:], in0=P["aim"][:], in1=lre, op=ALU.mult), r=[P["aim"], par], w=[P["t0"]])
            op(dve, lambda e: e.tensor_tensor(out=P["t1"][:], in0=P["are"][:], in1=lim, op=ALU.mult), r=[P["are"], par], w=[P["t1"]])
            op(dve, lambda e: e.tensor_tensor(out=P["t0"][:], in0=P["t0"][:], in1=P["t1"][:], op=ALU.subtract), r=[P["t0"], P["t1"]], w=[P["t0"]])
            op(dve, lambda e: e.tensor_tensor(out=P["kim"][:], in0=P["t0"][:], in1=P["l2"][:], op=ALU.mult), r=[P["t0"], P["l2"]], w=[P["kim"]])
            GR = scC[:, :].rearrange("p (c k) -> p c k", c=8)
            kreb = P["kre"][:].unsqueeze(2).to_broadcast([128, 8, 128])
            kimb = P["kim"][:].unsqueeze(2).to_broadcast([128, 8, 128])
            for part in range(2):
                if part == 0:
                    op(dve, lambda e: e.tensor_tensor(out=GR, in0=Ere, in1=kreb, op=ALU.mult), r=[scA, P["kre"]], w=[scC])
                    op(dve, lambda e: e.tensor_tensor(out=MP, in0=Eim, in1=kimb, op=ALU.mult), r=[scA, P["kim"]], w=[scB])
                    op(dve, lambda e: e.tensor_tensor(out=GR, in0=GR, in1=MP, op=ALU.add), r=[scC, scB], w=[scC])
                else:
                    op(dve, lambda e: e.tensor_tensor(out=GR, in0=Ere, in1=kimb, op=ALU.mult), r=[scA, P["kim"]], w=[scC])
                    op(dve, lambda e: e.tensor_tensor(out=MP, in0=Eim, in1=kreb, op=ALU.mult), r=[scA, P["kre"]], w=[scB])
                    op(dve, lambda e: e.tensor_tensor(out=GR, in0=GR, in1=MP, op=ALU.subtract), r=[scC, scB], w=[scC])
                op(dve, lambda e: e.tensor_tensor(out=GR, in0=GR, in1=MN, op=ALU.mult), r=[scC, scB], w=[scC])
                for half in range(2):
                    pp = ps1()

                    def f(e, half=half, pp=pp):
                        ins = None
                        for cc in range(4):
                            c = half * 4 + cc
                            ins = e.transpose(out=pp[:, cc * 128:(cc + 1) * 128], in_=scC[:, c * 128:(c + 1) * 128], identity=identF[:])
                        return ins
                    op(pe, f, r=[scC, identF], w=[pp])
                    op(act, lambda e, half=half, pp=pp, part=part: e.activation(out=ainvk[:, part, half * 512:(half + 1) * 512], in_=pp[:, 0:512], func=AF.Copy), r=[pp], w=[ainvk])

        def tile_geom(n):
            sample = n >= NTP
            M = 64 if sample else 128
            row0 = NTP * 128 + (n - NTP) * 64 if sample else n * 128
            return sample, M, row0

        def p1a(li, n):
            sample = n >= NTP
            M = 64 if sample else 128
            row0 = NTP * 128 + (n - NTP) * 64 if sample else n * 128
            seq = (n - NTP + 1) if sample else 0
            last_of_seq = sample or (n == NTP - 1)
            src_x = xin if li == 0 else X1
            dst_x = X1 if li == 0 else y_out
            kb.dma("sp", lambda e: e.dma_start(out=x_f[0:M, :], in_=src_x[row0:row0 + M, :]), w=[x_f])
            if n == 0:
                op(pool, lambda e: e.memset(hst[:], 0.0), w=[hst])
                op(pool, lambda e: e.memset(sret_f[:], 0.0), w=[sret_f])
                op(pool, lambda e: e.memset(sret_b[:], 0.0), w=[sret_b])
            if sample:
                s = n - NTP
                kb.dma("sp", lambda e: e.dma_start(out=hst[:], in_=st_ssm[li, s]), w=[hst])
                kb.dma("sp", lambda e: e.dma_start(out=sret_f[:], in_=st_ret[li, s]), w=[sret_f])
                op(act, lambda e: e.activation(out=sret_b[:].rearrange("p h e -> p (h e)"), in_=sret_f[:], func=AF.Copy), r=[sret_f], w=[sret_b])
                kc_b = bigb[:, :].rearrange("p (t c) -> p t c", t=4)
                kb.dma(pool, lambda e: e.dma_start(out=kc_b, in_=cache_k[li, s].rearrange("(t p) c -> p t c", p=128)), w=[bigb])
                kb.dma(pool, lambda e: e.dma_start(out=vr[:, 0:4, :], in_=cache_v[li, s].rearrange("(t p) c -> p t c", p=128)), w=[vr])
                for t_ in range(4):
                    pb = psb()

                    def f(e, t_=t_, pb=pb):
                        ins = None
                        for hp in range(4):
                            ins = e.transpose(out=pb[:, hp * 128:(hp + 1) * 128], in_=kc_b[:, t_, hp * 128:(hp + 1) * 128], identity=ident[:])
                        return ins
                    op(pe, f, r=[bigb, ident], w=[pb])
                    op(dve, lambda e, t_=t_, pb=pb: e.tensor_copy(out=kt2[:, :, t_ * 128:(t_ + 1) * 128], in_=pb[:, 0:512].rearrange("p (h m) -> p h m", h=4)), r=[pb], w=[kt2])
            if sample:
                wslot, ws, boff = 4, 0, 0
                Wn = 576
            else:
                wslot = 4 + (n % 5)
                if n > 0 and n % 5 == 0:
                    op(pool, lambda e: e.tensor_copy(out=kt2[:, :, 0:512], in_=kt2[:, :, 640:1152]), r=[kt2], w=[kt2])
                    op(pool, lambda e: e.tensor_copy(out=vr[:, 0:4, :], in_=vr[:, 5:9, :]), r=[vr], w=[vr])
                nvalid = min(n, 4)
                ws = wslot - nvalid
                boff = (4 - nvalid) * 128
                Wn = (nvalid + 1) * 128

            op(act, lambda e: e.activation(out=xb[0:M, :], in_=x_f[0:M, :], func=AF.Copy), r=[x_f], w=[xb])
            transposes(xb, 8, M, xT, xT[:, :, 0:M])
            yield

            want_kv = sample or (n >= NTP - 4)
            kv_row = (512 + (n - NTP) * 64) if sample else (n - (NTP - 4)) * 128
            b4 = None
            for bi, (c0, N) in enumerate(IN_BLOCKS if "inproj" in STAGES else []):
                wt = wget(("w_in", li, 0, 8, c0, N))
                pp = ps1()

                def f(e, wt=wt, pp=pp, N=N):
                    ins = None
                    for k in range(8):
                        ins = e.matmul(pp[0:M, 0:N], lhsT=xT[:, k, 0:M], rhs=wt[:, k, 0:N], start=(k == 0), stop=(k == 7))
                    return ins
                op(pe, f, r=[xT, wt], w=[pp])
                if bi == 0:
                    op(act, lambda e, pp=pp: e.activation(out=u_f[0:M, :], in_=pp[0:M, 0:256], func=AF.Copy), r=[pp], w=[u_f])
                    op(act, lambda e, pp=pp: e.activation(out=u_b[0:M, :], in_=pp[0:M, 0:256], func=AF.Copy), r=[pp], w=[u_b])
                elif bi == 1:
                    op(act, lambda e, pp=pp: e.activation(out=qa_b[0:M, :], in_=pp[0:M, 0:512], func=AF.Copy), r=[pp], w=[qa_b])
                elif bi == 2:
                    op(act, lambda e, pp=pp: e.activation(out=ka_b[0:M, :], in_=pp[0:M, 0:512], func=AF.Copy), r=[pp], w=[ka_b])
                    if want_kv:
                        op(act, lambda e, pp=pp: e.activation(out=scC[0:M, 0:512], in_=pp[0:M, 0:512], func=AF.Copy), r=[pp], w=[scC])
                elif bi == 3:
                    op(act, lambda e, pp=pp: e.activation(out=vr[0:M, wslot, :], in_=pp[0:M, 0:512], func=AF.Copy), r=[pp], w=[vr])
                    if want_kv:
                        op(act, lambda e, pp=pp: e.activation(out=scC[0:M, 512:1024], in_=pp[0:M, 0:512], func=AF.Copy), r=[pp], w=[scC])
                        kb.dma("sp", lambda e: e.dma_start(out=k_out[li, kv_row:kv_row + M, :], in_=scC[0:M, 0:512]), r=[scC], is_out=True)
                        kb.dma("sp", lambda e: e.dma_start(out=v_out[li, kv_row:kv_row + M, :], in_=scC[0:M, 512:1024]), r=[scC], is_out=True)
                elif bi == 4:
                    b4 = pp
                    rope_and_ret_prep(b4, M, n)
                elif bi == 5:
                    op(act, lambda e, pp=pp: e.activation(out=vr_b[0:M, :], in_=pp[0:M, 0:256], func=AF.Copy), r=[pp], w=[vr_b])
                    op(act, lambda e, pp=pp: e.activation(out=sg_f[0:M, :], in_=pp[0:M, 256:512], func=AF.Silu), r=[pp], w=[sg_f])
                else:
                    g = (bi - 6) // 2
                    hh = (bi - 6) % 2
                    op(act, lambda e, pp=pp, g=g, hh=hh: e.activation(out=gates_b[0:M, g, hh * 512:(hh + 1) * 512], in_=pp[0:M, 0:512], func=AF.Sigmoid), r=[pp], w=[gates_b])
                yield

            tap = (li, n) == DBG_TILE
            if tap:
                dump("u_f", u_f[:, :], [u_f])
                dump("gates", gates_b[:, :, :].rearrange("p a b -> p (a b)"), [gates_b])
            if "s5" in STAGES:
                s5_stage(M)
                yield
            if "attn" in STAGES:
                yield from attn_stage(M, wslot, ws, boff, Wn)
            if "ret" in STAGES:
                ret_stage(M, sample)
                yield
            if tap:
                dump("yssm", yssm_b[:, :], [yssm_b])
                dump("yatt", yatt_b[:, :], [yatt_b])
                dump("yret", yret_b[:, :], [yret_b])
            if last_of_seq:
                kb.dma("sp", lambda e: e.dma_start(out=ssm_out[li, seq], in_=hst[:]), r=[hst], is_out=True)
                kb.dma("sp", lambda e: e.dma_start(out=ret_out[li, seq], in_=sret_f[:]), r=[sret_f], is_out=True)
            if "merge" in STAGES:
                yield from merge_stage(li, M, X1P[n % 2])

        def p2_tail(li, n):
            sample, M, row0 = tile_geom(n)
            dst_x = X1 if li == 0 else y_out
            XP = X1P[n % 2]
            peer_tail(li, M, XP)
            if "ple" in STAGES:
                ple_stage(li, M, row0, XP)
            kb.dma("sp", lambda e: e.dma_start(out=dst_x[row0:row0 + M, :], in_=XP[0:M, :]), r=[XP], is_out=True)

        def rope_and_ret_prep(b4, M, n):
            v4 = b4[0:M, 0:512].rearrange("p (g t f) -> p g t f", g=8, t=2)
            x1 = v4[:, :, 0, :]
            x2 = v4[:, :, 1, :]
            kb.dma("sp", lambda e: e.dma_start(out=cs_t[0:M, 0, :], in_=c_cos[0:M, n * 32:(n + 1) * 32]), w=[cs_t])
            kb.dma("sp", lambda e: e.dma_start(out=cs_t[0:M, 1, :], in_=c_sin[0:M, n * 32:(n + 1) * 32]), w=[cs_t])
            cosb = cs_t[0:M, 0, :].unsqueeze(1).to_broadcast([M, 8, 32])
            sinb = cs_t[0:M, 1, :].unsqueeze(1).to_broadcast([M, 8, 32])
            R = scC[0:M, 0:512].rearrange("p (g t f) -> p g t f", g=8, t=2)
            T = scC[0:M, 512:1024].rearrange("p (g t f) -> p g t f", g=8, t=2)
            op(dve, lambda e: e.tensor_tensor(out=R[:, :, 0, :], in0=x1, in1=cosb, op=ALU.mult), r=[b4, cs_t], w=[scC])
            op(dve, lambda e: e.tensor_tensor(out=T[:, :, 0, :], in0=x2, in1=sinb, op=ALU.mult), r=[b4, cs_t], w=[scC])
            op(dve, lambda e: e.tensor_tensor(out=R[:, :, 1, :], in0=x1, in1=sinb, op=ALU.mult), r=[b4, cs_t], w=[scC])
            op(dve, lambda e: e.tensor_tensor(out=T[:, :, 1, :], in0=x2, in1=cosb, op=ALU.mult), r=[b4, cs_t], w=[scC])
            op(dve, lambda e: e.tensor_tensor(out=R[:, :, 0, :], in0=R[:, :, 0, :], in1=T[:, :, 0, :], op=ALU.subtract), r=[scC], w=[scC])
            op(dve, lambda e: e.tensor_tensor(out=R[:, :, 1, :], in0=R[:, :, 1, :], in1=T[:, :, 1, :], op=ALU.add), r=[scC], w=[scC])
            op(act, lambda e: e.activation(out=qk_b[0:M, :], in_=scC[0:M, 0:512], func=AF.Copy), r=[scC], w=[qk_b])
            kwc = 8 if M == 64 else 4
            qv = scC[0:M, 0:256].rearrange("p (h d) -> p h d", h=4)
            kv = scC[0:M, 256:512].rearrange("p (h d) -> p h d", h=4)
            op(dve, lambda e: e.tensor_tensor(out=qs_b[0:M, :].rearrange("p (h d) -> p h d", h=4), in0=qv, in1=cret[0:M, 0:4].unsqueeze(2).to_broadcast([M, 4, 64]), op=ALU.mult), r=[scC, cret], w=[qs_b])
            op(dve, lambda e: e.tensor_tensor(out=kk_b[0:M, :].rearrange("p (h d) -> p h d", h=4), in0=kv, in1=cret[0:M, kwc:kwc + 4].unsqueeze(2).to_broadcast([M, 4, 64]), op=ALU.mult), r=[scC, cret], w=[kk_b])

        def s5_stage(M):
            transposes(u_b, 2, M, uT, uT[:, :, 0:M])
            bu = [ps2(), ps2()]
            for half in range(2):
                def f(e, half=half):
                    ins = None
                    for cb in range(2):
                        ins = e.matmul(bu[half][0:M, cb * 512:(cb + 1) * 512], lhsT=uT[:, cb, 0:M], rhs=bbd_b[:, cb, half, :], start=True, stop=True)
                    return ins
                op(pe, f, r=[uT, bbd_b], w=[bu[half]])
            t1 = scA[0:M, 0:1024]
            t2 = scA[0:M, 1024:2048]
            op(dve, lambda e: e.tensor_tensor(out=t1, in0=bu[0][0:M, :], in1=ainvk[0:M, 0, :], op=ALU.mult), r=[bu[0], ainvk], w=[scA])
            op(dve, lambda e: e.tensor_tensor(out=t2, in0=bu[1][0:M, :], in1=ainvk[0:M, 1, :], op=ALU.mult), r=[bu[1], ainvk], w=[scA])
            op(dve, lambda e: e.tensor_tensor(out=bigb[0:M, 0:1024], in0=t1, in1=t2, op=ALU.subtract), r=[scA], w=[bigb])
            op(dve, lambda e: e.tensor_tensor(out=t1, in0=bu[0][0:M, :], in1=ainvk[0:M, 1, :], op=ALU.mult), r=[bu[0], ainvk, bigb], w=[scA])
            op(dve, lambda e: e.tensor_tensor(out=t2, in0=bu[1][0:M, :], in1=ainvk[0:M, 0, :], op=ALU.mult), r=[bu[1], ainvk], w=[scA])
            op(dve, lambda e: e.tensor_tensor(out=bigb[0:M, 1024:2048], in0=t1, in1=t2, op=ALU.add), r=[scA], w=[bigb])
            cs = [ps2(), ps2()]
            for half in range(2):
                def f(e, half=half):
                    ins = None
                    for c in range(8):
                        ins = e.matmul(cs[half][:, c * 128:c * 128 + M], lhsT=bigb[0:M, half * 1024 + c * 128:half * 1024 + (c + 1) * 128], rhs=l2t[0:M, 0:M], start=True, stop=True)
                    return ins
                op(pe, f, r=[bigb, l2t], w=[cs[half]])
            tre = scA[:, 0:1024].rearrange("p (c i) -> p c i", c=8)[:, :, 0:M]
            tim = scA[:, 1024:2048].rearrange("p (c i) -> p c i", c=8)[:, :, 0:M]
            hre = scB[:, 0:1024].rearrange("p (c i) -> p c i", c=8)[:, :, 0:M]
            him = scB[:, 1024:2048].rearrange("p (c i) -> p c i", c=8)[:, :, 0:M]
            tmp = scC[:, 0:1024].rearrange("p (c i) -> p c i", c=8)[:, :, 0:M]
            csv = [cs[h_][:, :].rearrange("p (c i) -> p c i", c=8)[:, :, 0:M] for h_ in range(2)]
            op(dve, lambda e: e.tensor_tensor(out=tre, in0=csv[0], in1=hst[:, 0:8].unsqueeze(2).to_broadcast([128, 8, M]), op=ALU.add), r=[cs[0], hst], w=[scA])
            op(dve, lambda e: e.tensor_tensor(out=tim, in0=csv[1], in1=hst[:, 8:16].unsqueeze(2).to_broadcast([128, 8, M]), op=ALU.add), r=[cs[1], hst], w=[scA])
            a_re = a1t[:, 0, :, 0:M]
            a_im = a1t[:, 1, :, 0:M]
            op(dve, lambda e: e.tensor_tensor(out=hre, in0=tre, in1=a_re, op=ALU.mult), r=[scA, a1t], w=[scB])
            op(dve, lambda e: e.tensor_tensor(out=tmp, in0=tim, in1=a_im, op=ALU.mult), r=[scA, a1t], w=[scC])
            op(dve, lambda e: e.tensor_tensor(out=hre, in0=hre, in1=tmp, op=ALU.subtract), r=[scB, scC], w=[scB])
            op(dve, lambda e: e.tensor_tensor(out=him, in0=tre, in1=a_im, op=ALU.mult), r=[scA, a1t], w=[scB])
            op(dve, lambda e: e.tensor_tensor(out=tmp, in0=tim, in1=a_re, op=ALU.mult), r=[scA, a1t, scB], w=[scC])
            op(dve, lambda e: e.tensor_tensor(out=him, in0=him, in1=tmp, op=ALU.add), r=[scB, scC], w=[scB])
            hT_b = bigb[:, :].rearrange("p (c i) -> p c i", c=16)
            op(act, lambda e: e.activation(out=hT_b[:, 0:8, 0:M], in_=hre, func=AF.Copy), r=[scB], w=[bigb])
            op(act, lambda e: e.activation(out=hT_b[:, 8:16, 0:M], in_=him, func=AF.Copy, scale=-1.0), r=[scB], w=[bigb])
            op(dve, lambda e: e.tensor_copy(out=hst[:, 0:8], in_=hre[:, :, M - 1]), r=[scB], w=[hst])
            op(dve, lambda e: e.tensor_copy(out=hst[:, 8:16], in_=him[:, :, M - 1]), r=[scB], w=[hst])
            yp = ps1()

            def f(e):
                ins = None
                for c in range(8):
                    e.matmul(yp[0:M, 32 * c:32 * c + 32], lhsT=hT_b[:, c, 0:M], rhs=cm_b[:, c, 0, :], start=True, stop=False)
                    ins = e.matmul(yp[0:M, 32 * c:32 * c + 32], lhsT=hT_b[:, 8 + c, 0:M], rhs=cm_b[:, c, 1, :], start=False, stop=True)
                return ins
            op(pe, f, r=[bigb, cm_b], w=[yp])
            yf = scA[0:M, 0:256]
            zf = scA[0:M, 256:512]
            tz = scA[0:M, 512:768]
            op(dve, lambda e: e.tensor_tensor(out=yf, in0=u_f[0:M, :], in1=rows_s[0:M, R_DSK:R_DSK + 256], op=ALU.mult), r=[u_f, rows_s], w=[scA])
            op(dve, lambda e: e.tensor_tensor(out=yf, in0=yf, in1=yp[0:M, 0:256], op=ALU.add), r=[scA, yp], w=[scA])
            gelu_tanh(yf, zf, tz, [scA], [scA], scA)
            zb = u_b
            op(act, lambda e: e.activation(out=zb[0:M, :], in_=zf, func=AF.Copy), r=[scA, uT], w=[zb])
            transposes(zb, 2, M, uT, uT[:, :, 0:M])
            gp = ps1()

            def f2(e):
                ins = None
                for k in range(2):
                    ins = e.matmul(gp[0:M, 0:256], lhsT=uT[:, k, 0:M], rhs=wglu_b[:, k, :], start=(k == 0), stop=(k == 1))
                return ins
            op(pe, f2, r=[uT, wglu_b], w=[gp])
            op(dve, lambda e: e.tensor_tensor(out=tz, in0=gp[0:M, 0:256], in1=rows_s[0:M, R_BGLU:R_BGLU + 256], op=ALU.add), r=[gp, rows_s], w=[scA])
            op(act, lambda e: e.activation(out=tz, in_=tz, func=AF.Sigmoid), r=[scA], w=[scA])
            op(dve, lambda e: e.tensor_tensor(out=yssm_b[0:M, :], in0=zf, in1=tz, op=ALU.mult), r=[scA], w=[yssm_b])

        def attn_stage(M, wslot, ws, boff, Wn):
            transposes(qa_b, 4, M, qT2, qT2[:, :, 0:M])
            transposes(ka_b, 4, M, kt2, kt2[:, :, wslot * 128:wslot * 128 + M])
            c0 = ws * 128
            ops_ = ps_acc()
            S_sb = scA[0:M, 0:Wn]
            nblk = (Wn + 127) // 128
            for h in range(8):
                sp2 = ps2()

                def f(e, h=h, sp2=sp2):
                    n1 = min(512, Wn)
                    po = (h % 2) * 64
                    hp = h // 2
                    ins = e.matmul(sp2[0:M, 0:n1], lhsT=qT2[po:po + 64, hp, 0:M], rhs=kt2[po:po + 64, hp, c0:c0 + n1], start=True, stop=True)
                    if Wn > 512:
                        ins = e.matmul(sp2[0:M, 512:Wn], lhsT=qT2[po:po + 64, hp, 0:M], rhs=kt2[po:po + 64, hp, c0 + 512:c0 + Wn], start=True, stop=True)
                    return ins
                op(pe, f, r=[qT2, kt2], w=[sp2])
                op(dve, lambda e, h=h, sp2=sp2: e.scalar_tensor_tensor(out=S_sb, in0=sp2[0:M, 0:Wn], scalar=0.125, in1=bias_b[0:M, h, boff:boff + Wn], op0=ALU.mult, op1=ALU.add), r=[sp2, bias_b], w=[scA])
                op(dve, lambda e: e.reduce_max(out=sm[0:M, 16:17], in_=S_sb, axis=AX.X), r=[scA], w=[sm])
                op(dve, lambda e: e.tensor_scalar(out=sm[0:M, 17:18], in0=sm[0:M, 16:17], scalar1=-1.0, scalar2=None, op0=ALU.mult), r=[sm], w=[sm])
                op(act, lambda e, h=h: e.activation(out=pb_att[0:M, 0:Wn], in_=S_sb, func=AF.Exp, bias=sm[0:M, 17:18], accum_out=sm[0:M, 24 + h:25 + h]), r=[scA, sm], w=[pb_att, sm])
                pbt = psb()

                def f(e, pbt=pbt):
                    ins = None
                    for b in range(nblk):
                        bw = min(128, Wn - b * 128)
                        ins = e.transpose(out=pbt[0:bw, b * 128:b * 128 + M], in_=pb_att[0:M, b * 128:b * 128 + bw], identity=ident[0:M, 0:M])
                    return ins
                op(pe, f, r=[pb_att, ident], w=[pbt])
                lastw = Wn - (nblk - 1) * 128
                if lastw == 128:
                    op(dve, lambda e, pbt=pbt: e.tensor_copy(out=pT[:, 0:nblk, 0:M], in_=pbt[:, 0:nblk * 128].rearrange("p (b m) -> p b m", m=128)[:, :, 0:M]), r=[pbt], w=[pT])
                else:
                    op(dve, lambda e, pbt=pbt: e.tensor_copy(out=pT[:, 0:nblk - 1, 0:M], in_=pbt[:, 0:(nblk - 1) * 128].rearrange("p (b m) -> p b m", m=128)[:, :, 0:M]), r=[pbt], w=[pT])
                    op(dve, lambda e, pbt=pbt: e.tensor_copy(out=pT[0:lastw, nblk - 1, 0:M], in_=pbt[0:lastw, (nblk - 1) * 128:(nblk - 1) * 128 + M]), r=[pbt], w=[pT])

                def f(e, h=h):
                    ins = None
                    for b in range(nblk):
                        bw = min(128, Wn - b * 128)
                        ins = e.matmul(ops_[0:M, h * 64:(h + 1) * 64], lhsT=pT[0:bw, b, 0:M], rhs=vr[0:bw, ws + b, h * 64:(h + 1) * 64], start=(b == 0), stop=(b == nblk - 1))
                    return ins
                op(pe, f, r=[pT, vr], w=[ops_])
                yield
            op(dve, lambda e: e.reciprocal(out=sm[0:M, 32:40], in_=sm[0:M, 24:32]), r=[sm], w=[sm])
            op(dve, lambda e: e.tensor_tensor(out=yatt_b[0:M, :].rearrange("p (h d) -> p h d", h=8), in0=ops_[0:M, 0:512].rearrange("p (h d) -> p h d", h=8), in1=sm[0:M, 32:40].unsqueeze(2).to_broadcast([M, 8, 64]), op=ALU.mult), r=[ops_, sm], w=[yatt_b])

        def ret_stage(M, sample):
            pb = psb()

            def f(e):
                ins = None
                for g in range(8):
                    ins = e.transpose(out=pb[0:64, g * 128:g * 128 + M], in_=qk_b[0:M, g * 64:(g + 1) * 64], identity=ident[0:M, 0:M])
                return ins
            op(pe, f, r=[qk_b, ident], w=[pb])
            op(dve, lambda e: e.tensor_copy(out=qkT[:, :, 0:M], in_=pb[0:64, :].rearrange("p (h m) -> p h m", h=8)[:, :, 0:M]), r=[pb], w=[qkT])
            pb2 = psb()

            def f(e):
                ins = None
                for g in range(4):
                    ins = e.transpose(out=pb2[0:64, g * 128:g * 128 + M], in_=qs_b[0:M, g * 64:(g + 1) * 64], identity=ident[0:M, 0:M])
                return ins
            op(pe, f, r=[qs_b, ident], w=[pb2])
            op(dve, lambda e: e.tensor_copy(out=qsT[:, :, 0:M], in_=pb2[0:64, 0:512].rearrange("p (h m) -> p h m", h=4)[:, :, 0:M]), r=[pb2], w=[qsT])
            scp = ps1()

            def f(e):
                ins = None
                for h in range(4):
                    ins = e.matmul(scp[0:M, h * 128:h * 128 + M], lhsT=qkT[:, 4 + h, 0:M], rhs=qkT[:, h, 0:M], start=True, stop=True)
                return ins
            op(pe, f, r=[qkT], w=[scp])
            op(dve, lambda e: e.tensor_tensor(out=scT_b[0:M, :, 0:M], in0=scp[0:M, 0:512].rearrange("p (h i) -> p h i", h=4)[:, :, 0:M], in1=dtt[0:M, :, 0:M], op=ALU.mult), r=[scp, dtt], w=[scT_b])
            opp = ps1()

            def f(e):
                ins = None
                for h in range(4):
                    e.matmul(opp[0:M, h * 64:(h + 1) * 64], lhsT=scT_b[0:M, h, 0:M], rhs=vr_b[0:M, h * 64:(h + 1) * 64], start=True, stop=False)
                    ins = e.matmul(opp[0:M, h * 64:(h + 1) * 64], lhsT=qsT[:, h, 0:M], rhs=sret_b[:, h, :], start=False, stop=True)
                return ins
            op(pe, f, r=[scT_b, vr_b, qsT, sret_b], w=[opp])
            cp = ps1()

            def f(e):
                ins = None
                for h in range(4):
                    ins = e.matmul(cp[0:64, h * 64:(h + 1) * 64], lhsT=kk_b[0:M, h * 64:(h + 1) * 64], rhs=vr_b[0:M, h * 64:(h + 1) * 64], start=True, stop=True)
                return ins
            op(pe, f, r=[kk_b, vr_b], w=[cp])
            di = 1 if sample else 0
            op(dve, lambda e: e.tensor_tensor(out=sret_f[:], in0=sret_f[:], in1=dect[:, di, :], op=ALU.mult), r=[sret_f, dect], w=[sret_f])
            op(dve, lambda e: e.tensor_tensor(out=sret_f[:], in0=sret_f[:], in1=cp[0:64, 0:256], op=ALU.add), r=[sret_f, cp], w=[sret_f])
            op(act, lambda e: e.activation(out=sret_b[:].rearrange("p h e -> p (h e)"), in_=sret_f[:], func=AF.Copy), r=[sret_f], w=[sret_b])
            o3 = opp[0:M, 0:256].rearrange("p (h e) -> p h e", h=4)
            oc = scA[0:M, 0:256].rearrange("p (h e) -> p h e", h=4)
            sq = scA[0:M, 256:512].rearrange("p (h e) -> p h e", h=4)
            op(dve, lambda e: e.tensor_reduce(out=sm[0:M, 40:44], in_=o3, axis=AX.X, op=ALU.add), r=[opp], w=[sm])
            op(dve, lambda e: e.tensor_scalar(out=sm[0:M, 44:48], in0=sm[0:M, 40:44], scalar1=-1.0 / 64, scalar2=None, op0=ALU.mult), r=[sm], w=[sm])
            op(dve, lambda e: e.tensor_tensor(out=oc, in0=o3, in1=sm[0:M, 44:48].unsqueeze(2).to_broadcast([M, 4, 64]), op=ALU.add), r=[opp, sm], w=[scA])
            op(dve, lambda e: e.tensor_tensor(out=sq, in0=oc, in1=oc, op=ALU.mult), r=[scA], w=[scA])
            op(dve, lambda e: e.tensor_reduce(out=sm[0:M, 48:52], in_=sq, axis=AX.X, op=ALU.add), r=[scA], w=[sm])
            op(dve, lambda e: e.tensor_scalar(out=sm[0:M, 48:52], in0=sm[0:M, 48:52], scalar1=1.0 / 64, scalar2=LN_EPS, op0=ALU.mult, op1=ALU.add), r=[sm], w=[sm])
            op(act, lambda e: e.activation(out=sm[0:M, 52:56], in_=sm[0:M, 48:52], func=AF.Sqrt), r=[sm], w=[sm])
            op(dve, lambda e: e.reciprocal(out=sm[0:M, 56:60], in_=sm[0:M, 52:56]), r=[sm], w=[sm])
            op(dve, lambda e: e.tensor_tensor(out=oc, in0=oc, in1=sm[0:M, 56:60].unsqueeze(2).to_broadcast([M, 4, 64]), op=ALU.mult), r=[scA, sm], w=[scA])
            op(dve, lambda e: e.tensor_tensor(out=scA[0:M, 0:256], in0=scA[0:M, 0:256], in1=rows_s[0:M, R_GNG:R_GNG + 256], op=ALU.mult), r=[scA, rows_s], w=[scA])
            op(dve, lambda e: e.tensor_tensor(out=yret_b[0:M, :], in0=scA[0:M, 0:256], in1=sg_f[0:M, :], op=ALU.mult), r=[scA, sg_f], w=[yret_b])

        def merge_stage(li, M, XP):
            transposes(yssm_b, 2, M, yT, yT[:, 0:2, 0:M])
            transposes(yatt_b, 4, M, yT, yT[:, 2:6, 0:M])
            transposes(yret_b, 2, M, yT, yT[:, 6:8, 0:M])
            merged = scB[0:M, 0:1024]
            tmpm = scB[0:M, 1024:2048]
            for bi, (name, K, k0) in enumerate((("w_br_ssm", 2, 0), ("w_br_att", 4, 2), ("w_br_ret", 2, 6))):
                for hh in range(2):
                    wt = wget((name, li, 0, K, hh * 512, 512))
                    pp = ps1()

                    def f(e, wt=wt, pp=pp, K=K, k0=k0):
                        ins = None
                        for k in range(K):
                            ins = e.matmul(pp[0:M, 0:512], lhsT=yT[:, k0 + k, 0:M], rhs=wt[:, k, 0:512], start=(k == 0), stop=(k == K - 1))
                        return ins
                    op(pe, f, r=[yT, wt], w=[pp])
                    if bi == 0:
                        op(dve, lambda e, pp=pp, hh=hh: e.tensor_tensor(out=merged[:, hh * 512:(hh + 1) * 512], in0=pp[0:M, 0:512], in1=gates_b[0:M, 0, hh * 512:(hh + 1) * 512], op=ALU.mult), r=[pp, gates_b], w=[scB])
                    else:
                        op(dve, lambda e, pp=pp, hh=hh, bi=bi: e.tensor_tensor(out=tmpm[:, hh * 512:(hh + 1) * 512], in0=pp[0:M, 0:512], in1=gates_b[0:M, bi, hh * 512:(hh + 1) * 512], op=ALU.mult), r=[pp, gates_b], w=[scB])
                        op(dve, lambda e, hh=hh: e.tensor_tensor(out=merged[:, hh * 512:(hh + 1) * 512], in0=merged[:, hh * 512:(hh + 1) * 512], in1=tmpm[:, hh * 512:(hh + 1) * 512], op=ALU.add), r=[scB], w=[scB])
                    yield
            op(act, lambda e: e.activation(out=xb[0:M, :], in_=merged, func=AF.Copy), r=[scB], w=[xb])
            transposes(xb, 8, M, xT, xT[:, :, 0:M])
            for hh in range(2):
                wt = wget(("w_o", li, 0, 8, hh * 512, 512))
                pp = ps1()

                def f(e, wt=wt, pp=pp):
                    ins = None
                    for k in range(8):
                        ins = e.matmul(pp[0:M, 0:512], lhsT=xT[:, k, 0:M], rhs=wt[:, k, 0:512], start=(k == 0), stop=(k == 7))
                    return ins
                op(pe, f, r=[xT, wt], w=[pp])
                op(dve, lambda e, pp=pp, hh=hh: e.scalar_tensor_tensor(out=XP[0:M, hh * 512:(hh + 1) * 512], in0=x_f[0:M, hh * 512:(hh + 1) * 512], scalar=ALPHA, in1=pp[0:M, 0:512], op0=ALU.mult, op1=ALU.add), r=[x_f, pp], w=[XP])
                yield

        def peer_front(li, M, XP, EI, GT):
            layer_norm(XP, 0, XP, M)
            op(act, lambda e: e.activation(out=xb[0:M, :], in_=XP[0:M, :], func=AF.Copy), r=[XP], w=[xb])
            transposes(xb, 8, M, xT, xT[:, :, 0:M])
            qTp = bigb[:, :].rearrange("p (g m) -> p g m", g=16)
            for cb in range(4):
                wt = wget(("w_q", li, 0, 8, cb * 512, 512))
                pp = ps1()

                def f(e, wt=wt, pp=pp):
                    ins = None
                    for g in range(4):
                        for k in range(8):
                            ins = e.matmul(pp[:, g * 128:g * 128 + M], lhsT=wt[:, k, g * 128:(g + 1) * 128], rhs=xT[:, k, 0:M], start=(k == 0), stop=(k == 7))
                    return ins
                op(pe, f, r=[xT, wt], w=[pp])
                op(act, lambda e, pp=pp, cb=cb: e.activation(out=qTp[:, cb * 4:(cb + 1) * 4, 0:M], in_=pp[:, 0:512].rearrange("p (g m) -> p g m", g=4)[:, :, 0:M], func=AF.Copy), r=[pp], w=[bigb])
            s_f = scA
            for half in range(2):
                sp2 = ps2()

                def f(e, half=half, sp2=sp2):
                    ins = None
                    for gg in range(8):
                        g = half * 8 + gg
                        ins = e.matmul(sp2[0:M, gg * 128:(gg + 1) * 128], lhsT=qTp[:, g, 0:M], rhs=keys_b[:, g, :], start=True, stop=True)
                    return ins
                op(pe, f, r=[bigb, keys_b], w=[sp2])
                op(act, lambda e, half=half, sp2=sp2: e.activation(out=s_f[0:M, half * 1024:(half + 1) * 1024], in_=sp2[0:M, :], func=AF.Copy), r=[sp2], w=[scA])
            svs = [s_f[0:M, g * 128:(g + 1) * 128] for g in range(16)]
            for g in range(16):
                op(dve, lambda e, g=g: e.max(out=topa[0:M, g, 0:8], in_=svs[g]), r=[scA.lane(g)], w=[topa.lane(g)])
            for g in range(16):
                op(dve, lambda e, g=g: e.max_index(out=topi[0:M, g, 0:8], in_max=topa[0:M, g, 0:8], in_values=svs[g]), r=[scA.lane(g), topa.lane(g)], w=[topi.lane(g)])
            for g in range(16):
                op(dve, lambda e, g=g: e.match_replace(out=svs[g], in_to_replace=topa[0:M, g, 0:8], in_values=svs[g], imm_value=-1e30), r=[scA.lane(g), topa.lane(g)], w=[scA.lane(g)])
            for g in range(16):
                op(dve, lambda e, g=g: e.max(out=topa[0:M, g, 8:16], in_=svs[g]), r=[scA.lane(g)], w=[topa.lane(g)])
            for g in range(16):
                op(dve, lambda e, g=g: e.max_index(out=topi[0:M, g, 8:16], in_max=topa[0:M, g, 8:16], in_values=svs[g]), r=[scA.lane(g), topa.lane(g)], w=[topi.lane(g)])
            cand = scB[0:M, :].rearrange("p (h a b) -> p h a b", h=8, a=16)
            ta = topa[0:M, :, :].rearrange("p (h s) k -> p h s k", s=2)
            op(dve, lambda e: e.tensor_tensor(out=cand, in0=ta[:, :, 0, :].unsqueeze(3).to_broadcast([M, 8, 16, 16]), in1=ta[:, :, 1, :].unsqueeze(2).to_broadcast([M, 8, 16, 16]), op=ALU.add), r=[topa], w=[scB])
            cvs = [scB[0:M, h * 256:(h + 1) * 256] for h in range(8)]
            for h in range(8):
                op(dve, lambda e, h=h: e.max(out=top2[0:M, h, 0:8], in_=cvs[h]), r=[scB.lane(h)], w=[top2.lane(h)])
            for h in range(8):
                op(dve, lambda e, h=h: e.max_index(out=pos2[0:M, h, 0:8], in_max=top2[0:M, h, 0:8], in_values=cvs[h]), r=[scB.lane(h), top2.lane(h)], w=[pos2.lane(h)])
            for h in range(8):
                op(dve, lambda e, h=h: e.match_replace(out=cvs[h], in_to_replace=top2[0:M, h, 0:8], in_values=cvs[h], imm_value=-1e30), r=[scB.lane(h), top2.lane(h)], w=[scB.lane(h)])
            for h in range(8):
                op(dve, lambda e, h=h: e.max(out=top2[0:M, h, 8:16], in_=cvs[h]), r=[scB.lane(h)], w=[top2.lane(h)])
            for h in range(8):
                op(dve, lambda e, h=h: e.max_index(out=pos2[0:M, h, 8:16], in_max=top2[0:M, h, 8:16], in_values=cvs[h]), r=[scB.lane(h), top2.lane(h)], w=[pos2.lane(h)])
            posf = scC[0:M, 0:128].rearrange("p (h j) -> p h j", h=8)
            k1f = scC[0:M, 128:256].rearrange("p (h j) -> p h j", h=8)
            k2f = scC[0:M, 256:384].rearrange("p (h j) -> p h j", h=8)
            iaf = scC[0:M, 384:640].rearrange("p (g k) -> p g k", g=16)
            e1 = scC[0:M, 640:768].rearrange("p (h j) -> p h j", h=8)
            e2 = scC[0:M, 768:896].rearrange("p (h j) -> p h j", h=8)
            op(dve, lambda e: e.tensor_copy(out=posf, in_=pos2[0:M, :, :]), r=[pos2], w=[scC])
            op(dve, lambda e: e.tensor_copy(out=iaf, in_=topi[0:M, :, :]), r=[topi], w=[scC])
            op(dve, lambda e: e.tensor_scalar(out=k1f, in0=posf, scalar1=1.0 / 16, scalar2=None, op0=ALU.mult), r=[scC], w=[scC])
            op(dve, lambda e: e.tensor_copy(out=pos2[0:M, :, :], in_=k1f), r=[scC], w=[pos2])
            op(dve, lambda e: e.tensor_copy(out=k1f, in_=pos2[0:M, :, :]), r=[pos2], w=[scC])
            op(dve, lambda e: e.scalar_tensor_tensor(out=k2f, in0=k1f, scalar=16.0, in1=posf, op0=ALU.mult, op1=ALU.is_gt), r=[scC], w=[scC])
            op(dve, lambda e: e.tensor_tensor(out=k1f, in0=k1f, in1=k2f, op=ALU.subtract), r=[scC], w=[scC])
            op(dve, lambda e: e.scalar_tensor_tensor(out=k2f, in0=k1f, scalar=-16.0, in1=posf, op0=ALU.mult, op1=ALU.add), r=[scC], w=[scC])
            oh = scB[0:M, :].rearrange("p (h j k) -> p h j k", h=8, j=16)
            iab = iaf.rearrange("p (h s) k -> p h s k", s=2)
            iotab = iota16[0:M, :].unsqueeze(1).unsqueeze(1).to_broadcast([M, 8, 16, 16])
            for side, (kf, eo) in enumerate(((k1f, e1), (k2f, e2))):
                op(dve, lambda e, kf=kf: e.tensor_tensor(out=oh, in0=kf.unsqueeze(3).to_broadcast([M, 8, 16, 16]), in1=iotab, op=ALU.is_equal), r=[scC, iota16], w=[scB])
                op(dve, lambda e, side=side: e.tensor_tensor(out=oh, in0=oh, in1=iab[:, :, side, :].unsqueeze(2).to_broadcast([M, 8, 16, 16]), op=ALU.mult), r=[scB, scC], w=[scB])
                op(dve, lambda e, eo=eo: e.tensor_reduce(out=eo, in_=oh, axis=AX.X, op=ALU.add), r=[scB], w=[scC])
            op(dve, lambda e: e.scalar_tensor_tensor(out=e1, in0=e1, scalar=128.0, in1=e2, op0=ALU.mult, op1=ALU.add), r=[scC], w=[scC])
            op(dve, lambda e: e.tensor_copy(out=EI[0:M, :].rearrange("p (h j) -> p h j", h=8), in_=e1), r=[scC], w=[EI])
            gx = scC[0:M, 896:1024].rearrange("p (h j) -> p h j", h=8)
            op(dve, lambda e: e.tensor_tensor(out=gx, in0=top2[0:M, :, :], in1=top2[0:M, :, 0:1].to_broadcast([M, 8, 16]), op=ALU.subtract), r=[top2], w=[scC])
            op(act, lambda e: e.activation(out=gx, in_=gx, func=AF.Exp), r=[scC], w=[scC])
            op(dve, lambda e: e.tensor_reduce(out=sm[0:M, 8:16], in_=gx, axis=AX.X, op=ALU.add), r=[scC], w=[sm])
            op(dve, lambda e: e.reciprocal(out=sm[0:M, 8:16], in_=sm[0:M, 8:16]), r=[sm], w=[sm])
            op(dve, lambda e: e.tensor_tensor(out=GT[0:M, :].rearrange("p (h j) -> p h j", h=8), in0=gx, in1=sm[0:M, 8:16].unsqueeze(2).to_broadcast([M, 8, 16]), op=ALU.mult), r=[scC, sm], w=[GT])
        def peer_loop(li, M, XP, EI, GT):
            vacc = ps_vacc()
            gate = GT[0:M, :]
            NGRP = 2
            NG_ = 128 // NGRP

            def cols(g):
                return g * NGRP, (g + 1) * NGRP

            def gath(j):
                ub = ubuf[j % NGB]
                kb.dma(pool, lambda e: e.indirect_dma_start(out=ub[:, :], out_offset=None, in_=peer_uv[li], in_offset=bass.IndirectOffsetOnAxis(ap=EI[:, j:j + 1], axis=0)), r=[EI], w=[ub])

            def dot(j, g):
                ub = ubuf[j % NGB]
                op(dve, lambda e: e.scalar_tensor_tensor(out=ub[0:M, 0:1024], in0=ub[0:M, 0:1024], scalar=1.0, in1=XP[0:M, :], op0=ALU.mult, op1=ALU.mult, accum_out=pact[0:M, j:j + 1]), r=[ub, XP], w=[ub, pact.lane(g % 3)])

            def vcopy(j, g):
                ub = ubuf[j % NGB]
                vb = vring[j % NVR]
                op(act, lambda e: e.activation(out=vb[0:M, :], in_=ub[0:M, 1024:2048], func=AF.Copy), r=[ub], w=[vb])

            def c1(g):
                j0, j1 = cols(g)
                ln = g % 3
                op(dve, lambda e: e.scalar_tensor_tensor(out=ptmp[0:M, j0:j1], in0=pact[0:M, j0:j1], scalar=0.044715, in1=pact[0:M, j0:j1], op0=ALU.mult, op1=ALU.mult), r=[pact.lane(ln)], w=[ptmp.lane(ln)])

            def c2(g):
                j0, j1 = cols(g)
                ln = g % 3
                op(dve, lambda e: e.scalar_tensor_tensor(out=ptmp[0:M, j0:j1], in0=ptmp[0:M, j0:j1], scalar=1.0, in1=pact[0:M, j0:j1], op0=ALU.add, op1=ALU.mult), r=[ptmp.lane(ln), pact.lane(ln)], w=[ptmp.lane(ln)])

            def cg(g):
                j0, j1 = cols(g)
                ln = g % 3
                op(dve, lambda e: e.tensor_tensor(out=pag[0:M, j0:j1], in0=pact[0:M, j0:j1], in1=gate[:, j0:j1], op=ALU.mult), r=[pact.lane(ln), GT], w=[pag.lane(ln)])

            def sig(g):
                j0, j1 = cols(g)
                ln = g % 3
                op(act, lambda e: e.activation(out=ptmp[0:M, j0:j1], in_=ptmp[0:M, j0:j1], func=AF.Sigmoid, scale=1.5957691216), r=[ptmp.lane(ln)], w=[ptmp.lane(ln)])

            def mm_(g):
                j0, j1 = cols(g)
                ln = g % 3
                op(dve, lambda e: e.tensor_tensor(out=pwv[0:M, j0:j1], in0=ptmp[0:M, j0:j1], in1=pag[0:M, j0:j1], op=ALU.mult), r=[ptmp.lane(ln), pag.lane(ln)], w=[pwv.lane(ln)])

            def diag(j, g):
                dg = dgb[j % 4]
                op(dve, lambda e: e.tensor_scalar(out=dg[0:M, 0:M], in0=ident[0:M, 0:M], scalar1=pwv[0:M, j:j + 1], scalar2=None, op0=ALU.mult), r=[ident, pwv.lane(g % 3)], w=[dg])

            def vmm(j):
                dg = dgb[j % 4]
                vb = vring[j % NVR]

                def f(e):
                    e.matmul(vacc[0:M, 0:512], lhsT=dg[0:M, 0:M], rhs=vb[0:M, 0:512], start=(j == 0), stop=(j == 127))
                    return e.matmul(vacc[0:M, 512:1024], lhsT=dg[0:M, 0:M], rhs=vb[0:M, 512:1024], start=(j == 0), stop=(j == 127))
                op(pe, f, r=[dg, vb], w=[vacc])

            for it in range(NG_ + 2):
                g0_, g1_, g2_ = it, it - 1, it - 2
                A0 = g0_ < NG_
                A1 = 0 <= g1_ < NG_
                A2 = 0 <= g2_ < NG_
                if A0:
                    gath(2 * g0_)
                    gath(2 * g0_ + 1)
                    dot(2 * g0_, g0_)
                    vcopy(2 * g0_, g0_)
                if A1:
                    c1(g1_)
                if A2:
                    mm_(g2_)
                if PIPE_MID:
                    yield
                if A0:
                    dot(2 * g0_ + 1, g0_)
                    vcopy(2 * g0_ + 1, g0_)
                if A1:
                    c2(g1_)
                if A2:
                    diag(2 * g2_, g2_)
                if A1:
                    cg(g1_)
                if A2:
                    diag(2 * g2_ + 1, g2_)
                if A1:
                    sig(g1_)
                if A2:
                    vmm(2 * g2_)
                    vmm(2 * g2_ + 1)
                yield

        def peer_tail(li, M, XP):
            vacc = ps_vacc()
            xp2 = scB[0:M, 0:1024]
            op(dve, lambda e: e.scalar_tensor_tensor(out=xp2, in0=XP[0:M, :], scalar=ALPHA, in1=vacc[0:M, :], op0=ALU.mult, op1=ALU.add), r=[XP, vacc], w=[scB])
            layer_norm(scB, 1, XP, M, src_ap=xp2)

        def ple_stage(li, M, row0, XP):
            kb.dma("sp", lambda e: e.dma_start(out=pe_f[0:M, :], in_=pin[li, row0:row0 + M, :]), w=[pe_f])
            op(act, lambda e: e.activation(out=xb[0:M, :], in_=XP[0:M, :], func=AF.Copy), r=[XP], w=[xb])
            transposes(xb, 8, M, xT, xT[:, :, 0:M])
            sgm = scA[0:M, 0:1024]
            for hh in range(2):
                wt = wget(("w_g", li, 0, 8, hh * 512, 512))
                pp = ps1()

                def f(e, wt=wt, pp=pp):
                    ins = None
                    for k in range(8):
                        ins = e.matmul(pp[0:M, 0:512], lhsT=xT[:, k, 0:M], rhs=wt[:, k, 0:512], start=(k == 0), stop=(k == 7))
                    return ins
                op(pe, f, r=[xT, wt], w=[pp])
                op(act, lambda e, pp=pp, hh=hh: e.activation(out=sgm[:, hh * 512:(hh + 1) * 512], in_=pp[0:M, 0:512], func=AF.Sigmoid), r=[pp], w=[scA])
            op(act, lambda e: e.activation(out=u_b[0:M, :], in_=pe_f[0:M, :], func=AF.Copy), r=[pe_f], w=[u_b])
            transposes(u_b, 2, M, uT, uT[:, :, 0:M])
            for hh in range(2):
                wt = wget(("w_p", li, 0, 2, hh * 512, 512))
                pp = ps1()

                def f(e, wt=wt, pp=pp):
                    ins = None
                    for k in range(2):
                        ins = e.matmul(pp[0:M, 0:512], lhsT=uT[:, k, 0:M], rhs=wt[:, k, 0:512], start=(k == 0), stop=(k == 1))
                    return ins
                op(pe, f, r=[uT, wt], w=[pp])
                op(dve, lambda e, pp=pp, hh=hh: e.tensor_tensor(out=sgm[:, hh * 512:(hh + 1) * 512], in0=sgm[:, hh * 512:(hh + 1) * 512], in1=pp[0:M, 0:512], op=ALU.mult), r=[pp, scA], w=[scA])
            xp2 = scB[0:M, 0:1024]
            op(dve, lambda e: e.scalar_tensor_tensor(out=xp2, in0=XP[0:M, :], scalar=ALPHA, in1=sgm, op0=ALU.mult, op1=ALU.add), r=[XP, scA], w=[scB])
            layer_norm(scB, 2, XP, M, src_ap=xp2)

        def run_all(gen):
            for _ in gen:
                pass

        for li in range(DEPTH):
            if "prep" in STAGES:
                layer_prep(li)
            def replay(item):
                if item[0] == "wissue":
                    w_issue_upto(item[1])
                elif item[0] == "op":
                    kb.op(item[1], item[2], item[3], item[4])
                else:
                    kb.dma(item[1], item[2], item[3], item[4], is_out=item[5])

            def front(n_):
                sample_, M_, row0_ = tile_geom(n_)
                peer_front(li, M_, X1P[n_ % 2], EIP[n_ % 2], GTP[n_ % 2])

            assert "peer" in STAGES
            run_all(p1a(li, 0))
            front(0)
            for n in range(NT):
                sample, M, row0 = tile_geom(n)
                L = []
                if n + 1 < NT and PIPELINE:
                    kb.rec = L
                    run_all(p1a(li, n + 1))
                    front(n + 1)
                    kb.rec = None
                pend = list(L)
                nrep = 0

                def regions(lst):
                    out = set()
                    for b in lst:
                        out.add(id(b))
                    return out

                def expand(lst):
                    out = set()
                    for b in lst:
                        out.add(id(b))
                        if b.parent is not None:
                            out.add(id(b.parent))
                        for k_ in b.kids.values():
                            out.add(id(k_))
                    return out

                for _ in peer_loop(li, M, X1P[n % 2], EIP[n % 2], GTP[n % 2]):
                    kb.iter += 1
                    budget = PIPE_EVERY
                    blk_r = set()
                    blk_w = set()
                    keep = []
                    scanned = 0
                    stop = False
                    for item in pend:
                        if stop or budget <= 0 or scanned >= PIPE_WINDOW:
                            keep.append(item)
                            continue
                        scanned += 1
                        if item[0] == "wissue":
                            if keep:
                                stop = True
                                keep.append(item)
                            else:
                                replay(item)
                                nrep += 1
                            continue
                        r_, w_ = item[3], item[4]
                        er, ew = expand(r_), expand(w_)
                        conflict = bool(ew & (blk_r | blk_w)) or bool(er & blk_w)
                        if (not conflict) and kb.ready(item[1], r_, w_, PIPE_AGE):
                            replay(item)
                            nrep += 1
                            budget -= 1
                        else:
                            keep.append(item)
                            blk_r |= regions(r_)
                            blk_w |= regions(w_)
                            if not PIPE_OOO:
                                stop = True
                    pend = keep
                if PIPE_VERBOSE and len(L):
                    print("pipeline li=%d n=%d: replayed %d of %d inside the loop" % (li, n, nrep, len(L)))
                for item in pend:
                    replay(item)
                if n + 1 < NT and not PIPELINE:
                    run_all(p1a(li, n + 1))
                    front(n + 1)
                p2_tail(li, n)
        kb.finish()
    return nc


def _consts(NTP):
    NT = NTP + 2
    c = {}
    c["c_ident"] = np.eye(128, dtype=np.float32)
    j = np.arange(128)
    c["c_l2t"] = (j[:, None] <= j[None, :]).astype(np.float32)
    c["c_kk"] = np.broadcast_to(np.arange(1, 129, dtype=np.float32)[None, :], (128, 128)).copy()
    c["c_iota"] = np.broadcast_to(np.arange(16, dtype=np.float32)[None, :], (128, 16)).copy()
    i = np.arange(128)[:, None]
    jj = np.arange(640)[None, :]
    valid = np.where(i < 64, jj < 576, jj >= 64)
    c["c_mask"] = np.where(valid, 0.0, NEG).astype(np.float32)
    half = 32
    freqs = (10000.0 ** (-np.arange(half, dtype=np.float32) / half)).astype(np.float32)
    pos = np.zeros((128, NT), dtype=np.float32)
    for n in range(NT):
        pos[:, n] = (n * 128 + np.arange(128)) if n < NTP else (2048 + np.arange(128))
    ang = (pos[:, :, None].astype(np.float32) * freqs[None, None, :]).astype(np.float32)
    c["c_cos"] = np.cos(ang).astype(np.float32).reshape(128, NT * 32)
    c["c_sin"] = np.sin(ang).astype(np.float32).reshape(128, NT * 32)
    lg = np.log1p(-(2.0 ** (-5.0 - np.arange(4, dtype=np.float64))))
    ii = np.arange(128, dtype=np.float64)
    ret = np.zeros((128, 12), dtype=np.float64)
    ret[:, 0:4] = 0.125 * np.exp(lg[None, :] * (ii[:, None] + 1.0))
    ret[:, 4:8] = np.exp(lg[None, :] * (127.0 - ii[:, None]))
    ret[:, 8:12] = np.exp(lg[None, :] * np.maximum(63.0 - ii[:, None], 0.0))
    c["c_ret"] = ret.astype(np.float32)
    I = ii[None, :]
    J = ii[:, None]
    same = (np.floor(I / 64) == np.floor(J / 64))
    cross = (J < 64) & (I >= 64)
    dt = np.zeros((128, 4, 128), dtype=np.float64)
    for h in range(4):
        dt[:, h, :] = 0.125 * np.where(same, np.exp(lg[h] * np.abs(I - J)), np.where(cross, np.exp(lg[h] * (I - J)), 0.0))
    c["c_dt"] = dt.reshape(128, 512).astype(np.float32)
    dec = np.zeros((64, 2, 4, 64), dtype=np.float64)
    for h in range(4):
        dec[:, 0, h, :] = np.exp(lg[h] * 128.0)
        dec[:, 1, h, :] = np.exp(lg[h] * 64.0)
    c["c_dec"] = dec.reshape(64, 512).astype(np.float32)
    return c


def _prep_shared(inp):
    f = lambda a: np.ascontiguousarray(np.asarray(a, dtype=np.float32))
    sh = {}
    sh["w_in"] = f(inp["w_in"])
    sh["w_glu"] = f(inp["ssm_w_glu"])
    sh["w_br_ssm"] = f(inp["w_br_ssm"])
    sh["w_br_att"] = f(inp["w_br_att"])
    sh["w_br_ret"] = f(inp["w_br_ret"])
    sh["w_o"] = f(inp["w_o"])
    sh["w_q"] = f(inp["peer_w_q"])
    sh["w_g"] = f(inp["ple_w_g"])
    sh["w_p"] = f(inp["ple_w_p"])
    pu, pv = f(inp["peer_u"]), f(inp["peer_v"])
    for i in range(2):
        sh["peer_u%d" % i] = pu[i]
        sh["peer_v%d" % i] = pv[i]
    b_re, b_im = f(inp["ssm_b_re"]), f(inp["ssm_b_im"])
    c_re, c_im = f(inp["ssm_c_re"]), f(inp["ssm_c_im"])
    bbd = np.zeros((2, 256, 2048), dtype=np.float32)
    cm = np.zeros((2, 128, 8, 2, 32), dtype=np.float32)
    for li in range(2):
        for g in range(16):
            bbd[li, g * 16:(g + 1) * 16, g * 64:(g + 1) * 64] = b_re[li, g].T
            bbd[li, g * 16:(g + 1) * 16, 1024 + g * 64:1024 + (g + 1) * 64] = b_im[li, g].T
            r0 = (g % 2) * 64
            c0 = (g % 2) * 16
            cm[li, r0:r0 + 64, g // 2, 0, c0:c0 + 16] = c_re[li, g].T
            cm[li, r0:r0 + 64, g // 2, 1, c0:c0 + 16] = c_im[li, g].T
    sh["bbd"] = bbd
    sh["cmat"] = cm.reshape(2, 128, 512)
    sk = f(inp["peer_sub_keys"]).reshape(2, 16, 128, 128)
    sh["keysT"] = np.ascontiguousarray(sk.transpose(0, 3, 1, 2)).reshape(2, 128, 2048)
    par = np.zeros((2, 128, 24), dtype=np.float32)
    for li in range(2):
        par[li, :, 0:8] = f(inp["ssm_lam_re"])[li].reshape(8, 128).T
        par[li, :, 8:16] = f(inp["ssm_lam_im"])[li].reshape(8, 128).T
        par[li, :, 16:24] = np.repeat(f(inp["ssm_log_dt"])[li], 64).reshape(8, 128).T
    sh["s5par"] = par
    rows = np.zeros((2, NROW), dtype=np.float32)
    for li in range(2):
        for i_, key in enumerate(("ln1_g", "ln2_g", "ln3_g", "ln1_b", "ln2_b", "ln3_b")):
            rows[li, i_ * 1024:(i_ + 1) * 1024] = f(inp[key])[li]
        rows[li, 6144 + R_DSK:6144 + R_DSK + 256] = f(inp["ssm_d"])[li]
        rows[li, 6144 + R_BGLU:6144 + R_BGLU + 256] = f(inp["ssm_b_glu"])[li]
        rows[li, 6144 + R_GNG:6144 + R_GNG + 256] = f(inp["ret_gn_g"])[li]
    sh["rows"] = np.ascontiguousarray(np.broadcast_to(rows[:, None, :], (2, 128, NROW)))
    i = np.arange(128)[:, None]
    jj = np.arange(640)[None, :]
    idx = np.clip(i - jj + 512, -256, 256) + 256
    tab = f(inp["att_rel_bias"])
    b2 = tab[:, idx, :]
    sh["bias2"] = np.ascontiguousarray(b2.transpose(0, 1, 3, 2)).reshape(2, 128, 8 * 640)
    return sh


def _prep_core(inp, c, NTP, seq_len):
    f = lambda a: np.asarray(a, dtype=np.float32)
    b = c % 4
    s0 = 2 * c
    d = {}
    d["xin"] = np.ascontiguousarray(np.concatenate([f(inp["x_prompt"])[b, :seq_len], f(inp["x_sample"])[s0], f(inp["x_sample"])[s0 + 1]], axis=0))
    d["pin"] = np.ascontiguousarray(np.concatenate([f(inp["p_prompt"])[:, b, :seq_len], f(inp["p_sample"])[:, s0], f(inp["p_sample"])[:, s0 + 1]], axis=1))
    st = np.zeros((2, 2, 128, 16), dtype=np.float32)
    for li in range(2):
        for s in range(2):
            st[li, s, :, 0:8] = f(inp["state_ssm_re"])[li, s0 + s].reshape(8, 128).T
            st[li, s, :, 8:16] = f(inp["state_ssm_im"])[li, s0 + s].reshape(8, 128).T
    d["st_ssm"] = st
    d["cache_k"] = np.ascontiguousarray(f(inp["cache_attn_k"])[:, s0:s0 + 2].reshape(2, 2, 512, 512))
    d["cache_v"] = np.ascontiguousarray(f(inp["cache_attn_v"])[:, s0:s0 + 2].reshape(2, 2, 512, 512))
    sr = f(inp["state_ret"])[:, s0:s0 + 2]
    d["st_ret"] = np.ascontiguousarray(sr.transpose(0, 1, 3, 2, 4)).reshape(2, 2, 64, 256)
    return d


_CACHE = {}


def run_cores(inp, NTP=NTP_FULL, n_cores=8, dbg=None):
    key = (NTP, tuple(sorted(dbg.keys())) if dbg else None, tuple(sorted(STAGES)))
    if key not in _CACHE:
        _CACHE[key] = build_program(NTP, dbg)
    nc = _CACHE[key]
    sh = _prep_shared(inp)
    sh.update(_consts(NTP))
    in_maps = []
    for c in range(n_cores):
        m = dict(sh)
        m.update(_prep_core(inp, c, NTP, NTP * 128))
        in_maps.append(m)
    res = run_bass_kernel_spmd(nc, in_maps, core_ids=list(range(n_cores)))
    return res.results


def kernel(**inputs):
    NTP = NTP_FULL
    res = run_cores(inputs, NTP, 8)
    y_p = np.zeros((4, 4096, D), np.float32)
    y_s = np.zeros((16, 64, D), np.float32)
    ssm_re_p = np.zeros((2, 4, 16, 64), np.float32)
    ssm_im_p = np.zeros((2, 4, 16, 64), np.float32)
    k_p = np.zeros((2, 4, 512, 8, 64), np.float32)
    v_p = np.zeros((2, 4, 512, 8, 64), np.float32)
    ret_p = np.zeros((2, 4, 4, 64, 64), np.float32)
    ssm_re_s = np.zeros((2, 16, 16, 64), np.float32)
    ssm_im_s = np.zeros((2, 16, 16, 64), np.float32)
    k_s = np.zeros((2, 16, 64, 8, 64), np.float32)
    v_s = np.zeros((2, 16, 64, 8, 64), np.float32)
    ret_s = np.zeros((2, 16, 4, 64, 64), np.float32)
    for c in range(8):
        r = res[c]
        yo = np.asarray(r["y_out"])
        so = np.asarray(r["ssm_out"])
        ko = np.asarray(r["k_out"])
        vo = np.asarray(r["v_out"])
        ro = np.asarray(r["ret_out"])
        if c < 4:
            y_p[c] = yo[0:4096]
            for li in range(2):
                ssm_re_p[li, c] = so[li, 0][:, 0:8].T.reshape(16, 64)
                ssm_im_p[li, c] = so[li, 0][:, 8:16].T.reshape(16, 64)
                k_p[li, c] = ko[li, 0:512].reshape(512, 8, 64)
                v_p[li, c] = vo[li, 0:512].reshape(512, 8, 64)
                ret_p[li, c] = ro[li, 0].reshape(64, 4, 64).transpose(1, 0, 2)
        for s in range(2):
            q = 2 * c + s
            y_s[q] = yo[4096 + 64 * s:4096 + 64 * (s + 1)]
            for li in range(2):
                ssm_re_s[li, q] = so[li, 1 + s][:, 0:8].T.reshape(16, 64)
                ssm_im_s[li, q] = so[li, 1 + s][:, 8:16].T.reshape(16, 64)
                k_s[li, q] = ko[li, 512 + 64 * s:512 + 64 * (s + 1)].reshape(64, 8, 64)
                v_s[li, q] = vo[li, 512 + 64 * s:512 + 64 * (s + 1)].reshape(64, 8, 64)
                ret_s[li, q] = ro[li, 1 + s].reshape(64, 4, 64).transpose(1, 0, 2)
    return (y_p, y_s, ssm_re_p, ssm_im_p, k_p, v_p, ret_p, ssm_re_s, ssm_im_s, k_s, v_s, ret_s)
```
